# Optimizing a Trainium2 kernel written in Bass

```python
import math
import jax, jax.numpy as jnp
from jax import lax
import numpy as np

D_MODEL = 1024
BATCH = 8
SEQ = 2048
DEPTH = 1

ROPE_THETA = 500000.0
Q_BLOCK = 128
NORM_EPS = 1e-6
SUBLN_EPS = 1e-5

MLA_HEADS = 4
MLA_NOPE = 128
MLA_ROPE = 64
MLA_V = 128
KV_RANK = 128
MLA_WIDTH = MLA_HEADS * MLA_V

DIFF_HEADS = 4
DIFF_QK = 64
DIFF_V = 2 * DIFF_QK
DIFF_ROPE = DIFF_QK // 4
DIFF_WIDTH = DIFF_HEADS * DIFF_V

MIX_WIDTH = MLA_WIDTH + DIFF_WIDTH

PROJ_SIZES = (
    MLA_HEADS * MLA_NOPE,
    MLA_HEADS * MLA_ROPE,
    KV_RANK,
    MLA_ROPE,
    MLA_WIDTH,
    DIFF_HEADS * 2 * DIFF_QK,
    DIFF_HEADS * 2 * DIFF_QK,
    DIFF_WIDTH,
    DIFF_WIDTH,
)
PROJ_OUT = sum(PROJ_SIZES)
PROJ_SPLITS = tuple(int(s) for s in np.cumsum(PROJ_SIZES)[:-1])

kernel_name = "hybrid_mla_diffattn_parallel_heads"


def rmsnorm(x, g, eps=NORM_EPS):
    xf = x.astype(jnp.float32)
    y = xf * lax.rsqrt(jnp.mean(xf * xf, axis=-1, keepdims=True) + eps)
    return (y * g.astype(jnp.float32)).astype(x.dtype)


def rope_tables(seq, dim):
    inv_freq = ROPE_THETA ** (-jnp.arange(0, dim, 2, dtype=jnp.float32) / dim)
    ang = jnp.arange(seq, dtype=jnp.float32)[:, None] * inv_freq[None, :]
    return jnp.cos(ang), jnp.sin(ang)


def apply_rope(x, cos, sin):
    half = cos.shape[-1]
    shp = (1, x.shape[1]) + (1,) * (x.ndim - 3) + (half,)
    c = cos.reshape(shp).astype(x.dtype)
    s = sin.reshape(shp).astype(x.dtype)
    x1 = x[..., :half]
    x2 = x[..., half:2 * half]
    rot = jnp.concatenate([x1 * c - x2 * s, x2 * c + x1 * s], axis=-1)
    return jnp.concatenate([rot, x[..., 2 * half:]], axis=-1)


def causal_attention(q, k, v, scale):
    B, S, H, Dk = q.shape
    Dv = v.shape[-1]
    nb = S // Q_BLOCK
    qb = q.reshape(B, nb, Q_BLOCK, H, Dk).transpose(1, 0, 2, 3, 4)
    kpos = jnp.arange(S)

    def one_block(args):
        qi, i = args
        s = jnp.einsum('bqhd,bkhd->bhqk', qi, k).astype(jnp.float32) * scale
        qpos = i * Q_BLOCK + jnp.arange(Q_BLOCK)
        mask = kpos[None, :] <= qpos[:, None]
        s = jnp.where(mask[None, None], s, -jnp.inf)
        p = jax.nn.softmax(s, axis=-1).astype(v.dtype)
        return jnp.einsum('bhqk,bkhd->bqhd', p, v)

    out = lax.map(one_block, (qb, jnp.arange(nb)))
    return out.transpose(1, 0, 2, 3, 4).reshape(B, S, H, Dv)


def setup_inputs(seed: int = 0) -> dict:
    key = jax.random.key(seed)
    ks = jax.random.split(key, 14)
    nrm = jax.random.normal
    f32 = jnp.float32
    return {
        "x": nrm(ks[0], (BATCH, SEQ, D_MODEL), f32),
        "ln_pre_g": 1.0 + 0.02 * nrm(ks[1], (DEPTH, D_MODEL), f32),
        "w_in": nrm(ks[2], (DEPTH, D_MODEL, PROJ_OUT), f32) * D_MODEL ** -0.5,
        "kv_norm_g": 1.0 + 0.02 * nrm(ks[3], (DEPTH, KV_RANK), f32),
        "w_uk": nrm(ks[4], (DEPTH, KV_RANK, MLA_HEADS * MLA_NOPE), f32) * KV_RANK ** -0.5,
        "w_uv": nrm(ks[5], (DEPTH, KV_RANK, MLA_HEADS * MLA_V), f32) * KV_RANK ** -0.5,
        "lambda_q1": 0.1 * nrm(ks[6], (DEPTH, DIFF_QK), f32),
        "lambda_k1": 0.1 * nrm(ks[7], (DEPTH, DIFF_QK), f32),
        "lambda_q2": 0.1 * nrm(ks[8], (DEPTH, DIFF_QK), f32),
        "lambda_k2": 0.1 * nrm(ks[9], (DEPTH, DIFF_QK), f32),
        "subln_g": 1.0 + 0.02 * nrm(ks[10], (DEPTH, DIFF_V), f32),
        "w_out": nrm(ks[11], (DEPTH, MIX_WIDTH, D_MODEL), f32) * MIX_WIDTH ** -0.5,
        "ln_post_g": 1.0 + 0.02 * nrm(ks[12], (DEPTH, D_MODEL), f32),
    }


def reference(x, ln_pre_g, w_in, kv_norm_g, w_uk, w_uv, lambda_q1, lambda_k1,
              lambda_q2, lambda_k2, subln_g, w_out, ln_post_g):
    B, S, _ = x.shape
    cos_a, sin_a = rope_tables(S, MLA_ROPE)
    cos_b, sin_b = rope_tables(S, DIFF_ROPE)
    mla_scale = 1.0 / math.sqrt(MLA_NOPE + MLA_ROPE)
    diff_scale = 1.0 / math.sqrt(DIFF_QK)

    for l in range(DEPTH):
        lambda_init = 0.8 - 0.6 * math.exp(-0.3 * l)
        h = rmsnorm(x, ln_pre_g[l])
        proj = h @ w_in[l]
        (q_nope, q_rope, c_kv, k_rope, g_a,
         dq, dk, dv, g_b) = jnp.split(proj, PROJ_SPLITS, axis=-1)

        q_nope = q_nope.reshape(B, S, MLA_HEADS, MLA_NOPE)
        q_rope = apply_rope(q_rope.reshape(B, S, MLA_HEADS, MLA_ROPE), cos_a, sin_a)
        c_kv = rmsnorm(c_kv, kv_norm_g[l])
        k_nope = (c_kv @ w_uk[l]).reshape(B, S, MLA_HEADS, MLA_NOPE)
        v_a = (c_kv @ w_uv[l]).reshape(B, S, MLA_HEADS, MLA_V)
        k_rope = apply_rope(k_rope[:, :, None, :], cos_a, sin_a)
        k_rope = jnp.broadcast_to(k_rope, (B, S, MLA_HEADS, MLA_ROPE))
        q_a = jnp.concatenate([q_nope, q_rope], axis=-1)
        k_a = jnp.concatenate([k_nope, k_rope], axis=-1)
        o_a = causal_attention(q_a, k_a, v_a, mla_scale).reshape(B, S, MLA_WIDTH)
        o_a = o_a * jax.nn.silu(g_a)

        dq = apply_rope(dq.reshape(B, S, DIFF_HEADS, 2, DIFF_QK), cos_b, sin_b)
        dk = apply_rope(dk.reshape(B, S, DIFF_HEADS, 2, DIFF_QK), cos_b, sin_b)
        dv = dv.reshape(B, S, DIFF_HEADS, DIFF_V)
        lam = (jnp.exp(jnp.sum(lambda_q1[l].astype(jnp.float32) * lambda_k1[l].astype(jnp.float32)))
               - jnp.exp(jnp.sum(lambda_q2[l].astype(jnp.float32) * lambda_k2[l].astype(jnp.float32)))
               + lambda_init)
        a1 = causal_attention(dq[..., 0, :], dk[..., 0, :], dv, diff_scale)
        a2 = causal_attention(dq[..., 1, :], dk[..., 1, :], dv, diff_scale)
        o_b = a1 - lam.astype(a1.dtype) * a2
        o_b = rmsnorm(o_b, subln_g[l], SUBLN_EPS) * (1.0 - lambda_init)
        o_b = o_b.reshape(B, S, DIFF_WIDTH) * jax.nn.silu(g_b)

        mixed = jnp.concatenate([o_a, o_b], axis=-1) @ w_out[l]
        x = x + rmsnorm(mixed, ln_post_g[l])
    return x
```

```python
import math
from contextlib import ExitStack

import numpy as np

import concourse.bass as bass
import concourse.mybir as mybir
from concourse.bass_utils import run_bass_kernel_spmd

F32 = mybir.dt.float32
BF16 = mybir.dt.bfloat16
AF = mybir.ActivationFunctionType
ALU = mybir.AluOpType

S_LEN = 2048
D = 1024
NT = S_LEN // 128
NB = S_LEN // 512
ROPE_THETA = 500000.0
NORM_EPS = 1e-6
SUBLN_EPS = 1e-5
MLA_SCALE = 1.0 / math.sqrt(128 + 64)
DIFF_SCALE = 1.0 / math.sqrt(64)
LAMBDA_INIT = 0.8 - 0.6 * math.exp(-0.3 * 0)
MASK_NEG = -30000.0

ENGS = ["pe", "act", "dve", "pool", "sp"]


class Sched:
    def __init__(self, nc, ctx):
        self.nc = nc
        self.ctx = ctx
        self.streams = {e: [] for e in ENGS}
        self.sem = {e: ctx.enter_context(nc.semaphore("s_" + e)) for e in ENGS if e != "sp"}
        self.cnt = {e: 0 for e in ENGS}
        self.waited = {e: {} for e in ENGS}
        self.nd = 0

    def dma_sem(self):
        self.nd += 1
        return [self.ctx.enter_context(self.nc.semaphore(f"d{self.nd}")), 0]

    def _waits(self, eng, deps):
        ws = []
        for d in deps:
            if d is None:
                continue
            if isinstance(d, list):
                ws += self._waits(eng, d)
                continue
            sem, val = d
            k = id(sem)
            if self.waited[eng].get(k, 0) >= val:
                continue
            self.waited[eng][k] = val
            ws.append((sem, val))
        return ws

    def op(self, eng, fn, deps=(), signal=True):
        ws = self._waits(eng, deps)
        ent = [ws, fn, None]
        self.streams[eng].append(ent)
        if signal:
            self.cnt[eng] += 1
            ent[2] = (self.sem[eng], 1)
            return (self.sem[eng], self.cnt[eng])
        return None

    def last_token(self, eng):
        st = self.streams[eng]
        for ent in reversed(st):
            if ent[1] is None:
                continue
            if ent[2] is None:
                self.cnt[eng] += 1
                ent[2] = (self.sem[eng], 1)
            elif ent[2][0] is not self.sem[eng]:
                continue
            break
        if self.cnt[eng] == 0:
            return None
        return (self.sem[eng], self.cnt[eng])

    def dma(self, queue, dsem, out, in_, deps=()):
        ws = self._waits(queue, deps)
        dsem[1] += 16
        self.streams[queue].append([ws, lambda e: e.dma_start(out=out, in_=in_), (dsem[0], 16)])
        return (dsem[0], dsem[1])

    def wait(self, eng, deps):
        ws = self._waits(eng, deps)
        if ws:
            self.streams[eng].append([ws, None, None])

    def barrier(self, extra=()):
        toks = [self.last_token(e) for e in ("pe", "act", "dve", "pool")]
        toks = [t for t in toks if t is not None] + list(extra)
        for e in ENGS:
            self.wait(e, toks)
        return toks

    def emit(self):
        with self.nc.Block() as block:
            def run(name):
                def f(e):
                    embed_ok = name in ("act", "dve", "pool")
                    for ws, fn, inc in self.streams[name]:
                        ws = list(ws)
                        emb = None
                        if fn is not None and ws and embed_ok and not getattr(fn, "no_embed", False) \
                                and (inc is None or inc[0] is self.sem.get(name)):
                            emb = ws.pop()
                        for sem, val in ws:
                            e.wait_ge(sem, val)
                        if fn is not None:
                            ins = fn(e)
                            if emb is not None:
                                ins._wait_ge(emb[0], emb[1])
                            if inc is not None:
                                ins.then_inc(inc[0], inc[1])
                return f
            block.sync(run("sp"))
            block.tensor(run("pe"))
            block.scalar(run("act"))
            block.vector(run("dve"))
            block.gpsimd(run("pool"))


def f_mm(out, lhsT, rhs, start=True, stop=True):
    return lambda e: e.matmul(out, lhsT=lhsT, rhs=rhs, start=start, stop=stop)


def f_tr(out, in_, ident):
    return lambda e: e.transpose(out=out, in_=in_, identity=ident)


def f_act(out, in_, func, **kw):
    fn = lambda e: e.activation(out=out, in_=in_, func=func, **kw)
    if "accum_out" in kw:
        fn.no_embed = True
    return fn


def f_acopy(out, in_):
    return lambda e: e.copy(out=out, in_=in_)


def f_copy(out, in_):
    return lambda e: e.tensor_copy(out=out, in_=in_)


def f_tt(out, in0, in1, op):
    return lambda e: e.tensor_tensor(out=out, in0=in0, in1=in1, op=op)


def f_ts(out, in0, s1, s2, op0, op1=None):
    if op1 is None:
        return lambda e: e.tensor_scalar(out=out, in0=in0, scalar1=s1, scalar2=None, op0=op0)
    return lambda e: e.tensor_scalar(out=out, in0=in0, scalar1=s1, scalar2=s2, op0=op0, op1=op1)


def f_stt(out, in0, scalar, in1, op0, op1):
    return lambda e: e.scalar_tensor_tensor(out=out, in0=in0, scalar=scalar, in1=in1, op0=op0, op1=op1)


def f_recip(out, in_):
    return lambda e: e.reciprocal(out=out, in_=in_)


def f_memset(ap, v):
    return lambda e: e.memset(ap, v)


class Ring:
    def __init__(self, views):
        self.views = views
        self.free = [[] for _ in views]
        self.i = 0

    def acquire(self):
        s = self.i % len(self.views)
        self.i += 1
        deps = self.free[s]
        self.free[s] = []
        return s, self.views[s], deps

    def release(self, s, toks):
        self.free[s] = [t for t in toks if t is not None]


def _rope_tables():
    t = np.arange(S_LEN, dtype=np.float32)

    def tab(dim):
        inv = (np.float32(ROPE_THETA) ** (-(np.arange(0, dim, 2, dtype=np.float32)) / np.float32(dim))).astype(np.float32)
        ang = (t[:, None] * inv[None, :]).astype(np.float32)
        return np.cos(ang.astype(np.float64)).astype(np.float32), np.sin(ang.astype(np.float64)).astype(np.float32)

    ca, sa = tab(64)
    cb, sb = tab(16)
    CA = np.zeros((128, S_LEN), np.float32)
    SA = np.zeros((128, S_LEN), np.float32)
    CB = np.ones((128, S_LEN), np.float32)
    SB = np.zeros((128, S_LEN), np.float32)
    PA = np.zeros((128, 128), np.float32)
    PB = np.zeros((128, 128), np.float32)
    for r in range(128):
        d = r % 64
        i = d % 32
        CA[r] = ca[:, i]
        if d < 32:
            SA[r] = -sa[:, i]
            PA[r + 32, r] = 1.0
        else:
            SA[r] = sa[:, i]
            PA[r - 32, r] = 1.0
        if d < 8:
            CB[r] = cb[:, d]
            SB[r] = -sb[:, d]
            PB[r + 8, r] = 1.0
        elif d < 16:
            CB[r] = cb[:, d - 8]
            SB[r] = sb[:, d - 8]
            PB[r - 8, r] = 1.0
    return CA, SA, CB, SB, PA, PB


def _host_consts():
    CA, SA, CB, SB, PA, PB = _rope_tables()
    ident = np.eye(128, dtype=np.float32)
    kk = np.arange(128)[:, None]
    qq = np.arange(128)[None, :]
    mask = np.where(qq >= kk, 0.0, MASK_NEG).astype(np.float32)
    zero = np.zeros((128, 128), np.float32)
    neg = np.full((128, 128), MASK_NEG, np.float32)
    cmat = np.stack([ident, PA, np.ones((128, 128), np.float32), PB, mask,
                     mask, zero, mask, zero,
                     neg, mask, neg, mask],
                    axis=1)
    tabs = np.stack([CA, SA, CB, SB], axis=0)
    return np.ascontiguousarray(cmat), np.ascontiguousarray(tabs)


def _layout_weights(w_in, w_uk, w_uv, w_out):
    w = w_in[0]
    q_nope = w[:, 0:512]
    q_rope = w[:, 512:768]
    c_kv = w[:, 768:896]
    k_rope = w[:, 896:960]
    g_a = w[:, 960:1472]
    dq = w[:, 1472:1984]
    dk = w[:, 1984:2496]
    dv = w[:, 2496:3008]
    g_b = w[:, 3008:3520]
    groups = [
        np.concatenate([q_rope, c_kv, k_rope, k_rope], axis=1),
        q_nope,
        g_a,
        dq,
        dk,
        g_b,
        dv,
    ]
    wl = np.stack([g.reshape(8, 128, 512).transpose(1, 0, 2) for g in groups], axis=0)
    wo = w_out[0].reshape(8, 128, 1024).transpose(1, 0, 2)
    wkv = np.stack([w_uk[0], w_uv[0]], axis=1)
    return (np.ascontiguousarray(wl, dtype=np.float32), np.ascontiguousarray(wo, dtype=np.float32),
            np.ascontiguousarray(wkv, dtype=np.float32))


def build_program(debug=False, stop=None):
    nc = bass.Bass("TRN2", target_bir_lowering=False)
    x_d = nc.dram_tensor("x", [S_LEN, D], F32, kind="ExternalInput").ap()
    wl_d = nc.dram_tensor("wl", [7, 128, 8, 512], F32, kind="ExternalInput").ap()
    wo_d = nc.dram_tensor("wo", [128, 8, 1024], F32, kind="ExternalInput").ap()
    wkv_d = nc.dram_tensor("wkv", [128, 2, 512], F32, kind="ExternalInput").ap()
    cmat_d = nc.dram_tensor("cmat", [128, 13, 128], F32, kind="ExternalInput").ap()
    tabs_d = nc.dram_tensor("tabs", [4, 128, S_LEN], F32, kind="ExternalInput").ap()
    gpre_d = nc.dram_tensor("gpre", [1, D], F32, kind="ExternalInput").ap()
    gpost_d = nc.dram_tensor("gpost", [1, D], F32, kind="ExternalInput").ap()
    cols_d = nc.dram_tensor("cols", [128, 2], F32, kind="ExternalInput").ap()
    lamv_d = nc.dram_tensor("lamv", [1, 256], F32, kind="ExternalInput").ap()
    out_d = nc.dram_tensor("out", [S_LEN, D], F32, kind="ExternalOutput").ap()
    dbg_outs = {}

    ctx = ExitStack()
    with ctx:
        S = Sched(nc, ctx)

        def sb(name, shape, dt):
            return nc.alloc_sbuf_tensor(name, shape, dt)

        class _Stop(Exception):
            pass

        def stop_at(k):
            if stop is not None and stop == k:
                S.barrier()
                S.emit()
                raise _Stop()

        hT_t = sb("hT", [128, 8, S_LEN], BF16)
        AB_t = sb("AB", [128, 40960], BF16)
        mixA_t = sb("mixA", [128, 4, S_LEN], BF16)
        W_t = sb("W", [128, 2, 8, 512], BF16)
        tab_t = sb("tab", [128, 2, S_LEN], F32)
        T_t = sb("T", [128, 17408], BF16)
        cm_t = sb("cm", [128, 13, 128], BF16)
        small = sb("small", [128, 128], F32)
        lam_t = sb("lamt", [128, 320], F32)
        xs2_t = sb("xs2", [128, 2, 1024], F32)

        identb = cm_t[:, 0, :]
        permA = cm_t[:, 1, :]
        onesb = cm_t[:, 2, :]
        permB = cm_t[:, 3, :]
        maskb = cm_t[:, 4, :]
        tri3 = cm_t[:, 10:13, :].rearrange("p a b -> p (a b)")
        mask512 = [cm_t[:, 5:9, :].rearrange("p a b -> p (a b)"), cm_t[:, 9:13, :].rearrange("p a b -> p (a b)")]

        hT = hT_t
        mixB = hT_t[:, 0:4, :]

        def ab(off_k, n_k):
            return AB_t[:, off_k * 512:(off_k + n_k) * 512]

        QnT = ab(0, 16).rearrange("p (h t) -> p h t", h=4)
        QrT = ab(16, 8).rearrange("p (c t) -> p c t", c=2)
        KnT = ab(24, 16).rearrange("p (h t) -> p h t", h=4)
        KrLo = ab(40, 4)
        KrHi = ab(44, 4)
        Va = ab(48, 16).rearrange("p (k f) -> p k f", k=16)
        gaT = ab(64, 16).rearrange("p (h t) -> p h t", h=4)
        dqT = ab(0, 16).rearrange("p (h t) -> p h t", h=4)
        dkLo = ab(16, 16).rearrange("p (h t) -> p h t", h=4)
        dkHi = ab(32, 16).rearrange("p (h t) -> p h t", h=4)
        Vb = ab(48, 16).rearrange("p (k f) -> p k f", k=16)
        gbT = ab(64, 16).rearrange("p (h t) -> p h t", h=4)

        def tv(off_b, n_b, dt=BF16):
            v = T_t[:, off_b // 2:(off_b + n_b) // 2]
            if dt == F32:
                v = v.bitcast(F32)
            return v

        xs = [tv(0, 4096, F32), tv(4096, 4096, F32), xs2_t[:, 0, :], xs2_t[:, 1, :]]
        xn = [tv(8192, 2048), tv(10240, 2048)]
        junk = tv(12288, 2048)
        xbf = [tv(14336, 1024), tv(15360, 1024)]
        t1 = [tv(16384, 2048, F32), tv(18432, 2048, F32)]
        t2 = [tv(20480, 2048, F32), tv(22528, 2048, F32)]
        ckvnT = tv(24576, 4096)
        wkv = tv(28672, 2048).rearrange("p (a f) -> p a f", a=2)
        gpre = tv(30720, 4096, F32)
        PT = [tv(i * 1024, 1024) for i in range(6)]
        ep = [tv(6144 + i * 2048, 2048, F32) for i in range(12)]
        dsq = [tv(30720, 1024), tv(31744, 1024)]
        xs5 = [tv(i * 4096, 4096, F32) for i in range(3)]
        ybuf = [tv(12288 + i * 4096, 4096, F32) for i in range(3)]
        gpost = tv(24576, 4096, F32)
        junk5 = tv(28672, 2048)

        ssq = small[:, 0:16]
        ms = small[:, 16:32]
        rstd = small[:, 32:48]
        ssq5 = small[:, 48:80]
        ms5 = small[:, 80:96]
        rstd5 = small[:, 96:112]
        cols = small[:, 112:114]
        sgcol = small[:, 114:115]
        lsum = small[:, 115:117]
        lexp = small[:, 117:119]
        neglam = small[:, 119:120]
        mhcol = small[:, 120:121]
        epsA = small[:, 121:122]
        epsB = small[:, 122:123]

        psp = [nc.alloc_psum_tensor(f"psp{i}", [128, 2, 512], F32) for i in range(4)]

        def bank(b):
            return psp[b // 2][:, b % 2, :]

        d_xs = [S.dma_sem() for _ in range(4)]
        d_w = [S.dma_sem(), S.dma_sem()]
        d_tab = S.dma_sem()
        d_c = S.dma_sem()
        d_c2 = S.dma_sem()
        d_out = [S.dma_sem(), S.dma_sem(), S.dma_sem()]
        d_xs5 = [S.dma_sem(), S.dma_sem(), S.dma_sem()]
        d_dbg = S.dma_sem()

        try:
            x_tok = {}
            x_tok[0] = S.dma("sp", d_xs[0], xs[0], x_d[0:128, :])
            t_c = S.dma("pool", d_c, cm_t[:, 0:3, :], cmat_d[:, 0:3, :])
            S.dma("sp", d_c2, gpre, gpre_d.partition_broadcast(128))
            S.dma("sp", d_c2, cols, cols_d)
            t_c2 = S.dma("sp", d_c2, lam_t[:, 0:256], lamv_d.partition_broadcast(128))
            t_mh2 = S.op("pool", f_memset(mhcol, -0.5))
            S.op("pool", f_memset(epsA, NORM_EPS))
            t_eps = S.op("pool", f_memset(epsB, SUBLN_EPS))
            stop_at(0)

            pT_ring = Ring([bank(0), bank(1)])
            RG = {'acc': Ring([bank(2), bank(3), bank(4), bank(5)]), 'aux': Ring([bank(6), bank(7)])}
            xs_free = [None] * 4
            xn_free = [None, None]
            hT_ready = [None] * NT
            w_free = [[], []]
            w_ready = [None, None]
            flip = [0]
            junk_free = [None]

            def load_wgroup(g, slot, extra=()):
                w_ready[slot] = S.dma("pool", d_w[slot], W_t[:, slot], wl_d[g], deps=list(w_free[slot]) + list(extra))
                w_free[slot] = []

            xn_tok = {}

            def p0_A(t):
                s = t % 2
                xq = t % 4
                tx = x_tok.pop(t) if t in x_tok else S.dma("sp", d_xs[xq], xs[xq], x_d[t * 128:(t + 1) * 128, :],
                                                           deps=[xs_free[xq]])
                t_ssq = S.op("act", f_act(junk, xs[xq], AF.Square, accum_out=ssq[:, t:t + 1]), [tx, junk_free[0]])
                junk_free[0] = t_ssq
                t_ms = S.op("dve", f_ts(ms[:, t:t + 1], ssq[:, t:t + 1], 1.0 / D, NORM_EPS, ALU.mult, ALU.add), [t_ssq])
                t_rs = S.op("pool", f_tt(rstd[:, t:t + 1], ms[:, t:t + 1], mhcol, ALU.pow), [t_ms, t_mh2])
                t_xn = S.op("dve", f_stt(xn[s], xs[xq], rstd[:, t:t + 1], gpre, ALU.mult, ALU.mult),
                            [t_rs, t_c2, tx, xn_free[s]])
                xs_free[xq] = t_xn
                xn_tok[t] = t_xn

            def p0_B(t):
                s = t % 2
                t_xn = xn_tok.pop(t)
                ps_, pv, pdeps = pT_ring.acquire()
                pTv = pv.bitcast(BF16).rearrange("p (c t) -> p c t", c=8)
                t_tr = None
                for c in range(8):
                    t_tr = S.op("pe", f_tr(pTv[:, c, :], xn[s][:, c * 128:(c + 1) * 128], identb),
                                [t_xn, t_c] + pdeps, signal=(c == 7))
                xn_free[s] = t_tr
                dst = hT[:, :, t * 128:(t + 1) * 128]
                if t % 4 != 3:
                    t_h = S.op("act", f_acopy(dst, pTv), [t_tr])
                else:
                    t_h = S.op("dve", f_copy(dst, pTv), [t_tr])
                pT_ring.release(ps_, [t_h])
                hT_ready[t] = t_h

            p0_steps = [("A", 0)]
            for t in range(NT):
                if t + 1 < NT:
                    p0_steps.append(("A", t + 1))
                p0_steps.append(("B", t))

            def p0_run(until_tile=None, nsteps=None):
                n = 0
                while p0_steps:
                    if nsteps is not None and n >= nsteps:
                        break
                    if until_tile is not None and hT_ready[until_tile] is not None:
                        break
                    kind, t = p0_steps.pop(0)
                    (p0_A if kind == "A" else p0_B)(t)
                    n += 1

            deferred_pe = []

            def run_deferred():
                todo = list(deferred_pe)
                del deferred_pe[:]
                for fn_ in todo:
                    fn_()

            def flush_deferred():
                while deferred_pe:
                    run_deferred()

            def proj_fm(slot, ci, tb):
                a_s, acc, adeps = RG['acc'].acquire()
                tk = None
                for k in range(8):
                    tk = S.op("pe", f_mm(acc, W_t[:, slot, k, ci * 128:(ci + 1) * 128],
                                         hT[:, k, tb * 512:(tb + 1) * 512], k == 0, k == 7),
                              [w_ready[slot]] + hT_ready[4 * tb:4 * tb + 4] + adeps, signal=(k == 7))
                w_free[slot].append(tk)
                run_deferred()
                return a_s, acc, tk

            def evac_copy(dst, src, deps):
                flip[0] ^= 1
                if flip[0]:
                    return S.op("act", f_acopy(dst, src), deps)
                return S.op("dve", f_copy(dst, src), deps)

            rope_i = [0]
            rope_free = [(None, None), (None, None)]
            NR = [2]
            tab_ready = [None]
            tab_users = []

            def evac_rope(acc_s, acc, t_mm, perm, tb, dsts):
                r = rope_i[0] % NR[0]
                rope_i[0] += 1
                tsl = slice(tb * 512, (tb + 1) * 512)
                t_xb = S.op("act", f_acopy(xbf[r], acc), [t_mm, rope_free[r][0]])
                t_a = S.op("dve", f_tt(t1[r], acc, tab_t[:, 0, tsl], ALU.mult), [t_mm, t_xb, tab_ready[0], rope_free[r][1]])
                RG['acc'].release(acc_s, [t_xb, t_a])
                old_free = rope_free[r][1]
                tab_r = tab_ready[0]

                def part2():
                    p_s, pr, pdeps = RG['aux'].acquire()
                    t_pm = S.op("pe", f_mm(pr, perm, xbf[r]), [t_xb, t_c] + pdeps)
                    t_b = S.op("dve", f_tt(t2[r], pr, tab_t[:, 1, tsl], ALU.mult), [t_pm, tab_r, old_free])
                    RG['aux'].release(p_s, [t_b])
                    t_c_ = []
                    for (p0, p1), dst in dsts:
                        eng_ = "dve" if p0 == 64 else "pool"
                        t_c_.append(S.op(eng_, f_tt(dst, t1[r][p0:p1, :], t2[r][p0:p1, :], ALU.add), [t_a, t_b]))
                    rope_free[r] = (t_pm, t_c_)
                    tab_users.append(t_b)

                deferred_pe.append(part2)

            def load_tabs(which, extra=()):
                deps = list(tab_users) + list(extra)
                del tab_users[:]
                S.dma("sp", d_tab, tab_t[:, 0, :], tabs_d[2 * which], deps=deps)
                tab_ready[0] = S.dma("sp", d_tab, tab_t[:, 1, :], tabs_d[2 * which + 1], deps=deps)

            load_wgroup(0, 0)
            d_cB = S.dma_sem()
            t_cB = S.dma("pool", d_cB, cm_t[:, 3:13, :], cmat_d[:, 3:13, :])
            for t in range(1, 4):
                x_tok[t] = S.dma("sp", d_xs[t], xs[t], x_d[t * 128:(t + 1) * 128, :])
            load_tabs(0, extra=[x_tok[3], w_ready[0]])
            S.op("pool", f_memset(KrLo[64:128, :], 0.0))
            S.op("pool", f_memset(KrHi[0:64, :], 0.0))

            def ckv_chunk(tb):
                tsl = slice(tb * 512, (tb + 1) * 512)
                a_s, acc, tk = proj_fm(0, 2, tb)
                r = rope_i[0] % NR[0]
                rope_i[0] += 1
                sqb = xbf[r]
                msb = t1[r]
                rsb = t2[r]
                t_sq = S.op("act", f_act(sqb, acc, AF.Square), [tk, rope_free[r][0]])
                old_free = rope_free[r][1]

                def part2():
                    p_s, pr, pdeps = RG['aux'].acquire()
                    t_ss = S.op("pe", f_mm(pr, onesb, sqb), [t_sq, t_c] + pdeps)
                    t_m = S.op("act", f_act(msb, pr, AF.Ln, scale=1.0 / 128, bias=epsA), [t_ss, old_free, t_eps])
                    RG['aux'].release(p_s, [t_m])
                    t_r = S.op("act", f_act(rsb, msb, AF.Exp, scale=-0.5), [t_m, old_free])
                    t_ck = S.op("dve", f_stt(ckvnT[:, tsl], acc, cols[:, 0:1], rsb, ALU.mult, ALU.mult), [t_r, tk, t_c2])
                    RG['acc'].release(a_s, [t_sq, t_ck])
                    rope_free[r] = (t_ss, t_ck)
                    kv_up(tb, t_ck)

                deferred_pe.append(part2)

            kv_items = []

            def kv_up(tb, t_ck):
                tsl = slice(tb * 512, (tb + 1) * 512)

                def one(lhsT, rhs, dst):
                    def run():
                        p_s, pr, pdeps = RG['aux'].acquire()
                        t_u = S.op("pe", f_mm(pr, lhsT, rhs), [t_ck, t_wkv] + pdeps)
                        t_e = S.op("dve", f_copy(dst, pr), [t_u])
                        RG['aux'].release(p_s, [t_e])
                    return run

                for h in range(4):
                    kv_items.append(one(wkv[:, 0, h * 128:(h + 1) * 128], ckvnT[:, tsl], KnT[:, h, tsl]))
                for kl in range(4):
                    kb = 4 * tb + kl
                    kv_items.append(one(ckvnT[:, kb * 128:(kb + 1) * 128], wkv[:, 1, :], Va[:, kb, :]))

            def g1_block(tb):
                tsl = slice(tb * 512, (tb + 1) * 512)
                ckv_chunk(tb)
                p0_run(nsteps=2)
                for ci in range(2):
                    a_s, acc, tk = proj_fm(0, ci, tb)
                    evac_rope(a_s, acc, tk, permA, tb, [((0, 128), QrT[:, ci, tsl])])
                    p0_run(nsteps=2)
                a_s, acc, tk = proj_fm(0, 3, tb)
                evac_rope(a_s, acc, tk, permA, tb, [((0, 64), KrLo[0:64, tsl]), ((64, 128), KrHi[64:128, tsl])])
                p0_run(nsteps=2)

            p0_run(until_tile=3)
            load_wgroup(1, 1)
            d_wkv = S.dma_sem()
            t_wkv = S.dma("pool", d_wkv, wkv, wkv_d)
            t_l0 = S.op("dve", f_tt(lam_t[:, 256:320], lam_t[:, 0:64], lam_t[:, 64:128], ALU.mult), [t_c2])
            t_l1 = S.op("dve", lambda e: e.reduce_sum(out=lsum[:, 0:1], in_=lam_t[:, 256:320],
                                                      axis=mybir.AxisListType.X), [t_l0])
            t_l0b = S.op("dve", f_tt(lam_t[:, 256:320], lam_t[:, 128:192], lam_t[:, 192:256], ALU.mult), [t_l1])
            t_l2 = S.op("dve", lambda e: e.reduce_sum(out=lsum[:, 1:2], in_=lam_t[:, 256:320],
                                                      axis=mybir.AxisListType.X), [t_l0b])
            t_le = S.op("act", f_act(lexp, lsum, AF.Exp), [t_l2])
            t_nl = S.op("dve", f_stt(neglam, lexp[:, 1:2], -LAMBDA_INIT, lexp[:, 0:1], ALU.add, ALU.subtract), [t_le])
            t_sg = S.op("dve", f_ts(sgcol, cols[:, 1:2], 1.0 - LAMBDA_INIT, None, ALU.mult), [t_c2, t_nl])

            def g0_block(tb):
                for ci in range(4):
                    a_s, acc, tk = proj_fm(1, ci, tb)
                    t_e = S.op("act", f_acopy(QnT[:, ci, tb * 512:(tb + 1) * 512], acc), [tk])
                    RG['acc'].release(a_s, [t_e])
                    p0_run(nsteps=1)

            for tb in range(2):
                g1_block(tb)
                g0_block(tb)
                p0_run(until_tile=4 * tb + 7)
            g1_block(2)
            p0_run()
            g1_block(3)
            flush_deferred()
            load_wgroup(2, 0)
            load_tabs(1)
            g0_block(2)
            g0_block(3)
            flush_deferred()
            load_wgroup(3, 1)
            for ci in range(4):
                for tb in range(NB):
                    a_s, acc, tk = proj_fm(0, ci, tb)
                    t_e = S.op("act", f_act(gaT[:, ci, tb * 512:(tb + 1) * 512], acc, AF.Silu), [tk])
                    RG['acc'].release(a_s, [t_e])
                    for _ in range(2):
                        if kv_items:
                            kv_items.pop(0)()
            while kv_items:
                kv_items.pop(0)()
            load_wgroup(4, 0)

            def dump(name, view, shape, dt):
                if not debug:
                    return
                o = nc.dram_tensor("dbg_" + name, shape, dt, kind="ExternalOutput").ap()
                dbg_outs[name] = o
                tok = S.dma("sp", d_dbg, o, view, deps=S.barrier())
                S.barrier([tok])

            stop_at(3)
            S.barrier([t_cB])
            dump("hT", hT[:], [128, 8, S_LEN], BF16)
            dump("AB1", AB_t[:], [128, 40960], BF16)

            s_ring_box = [None]

            def attention(nmaps, qT_of, v_of, scale, s_banks, o_banks, sum_banks, epilogue, look):
                npar = len(o_banks)
                s_ring = Ring([bank(b) for b in s_banks])
                s_ring_box[0] = s_ring
                pt_ring = Ring(PT)
                o_free = [[[] for _ in range(nmaps)] for _ in range(npar)]
                sum_free = [[[] for _ in range(nmaps)] for _ in range(npar)]
                tiles = []
                blk = 0
                for h in range(4):
                    for j in (0, 3, 1, 2):
                        for kb in range(4 * j + 4):
                            tiles.append((h, j, kb, blk % npar))
                        blk += 1
                pend = {}
                deferred = []

                def issue_S(idx):
                    h, j, kb, par = tiles[idx]
                    il = kb - 4 * j
                    c0 = 128 * max(0, il)
                    ent = []
                    for m in range(nmaps):
                        s_s, sv, sdeps = s_ring.acquire()
                        parts = qT_of(h, m)
                        tk = None
                        for pi, (kview, qview) in enumerate(parts):
                            last = (pi == len(parts) - 1) and il < 0
                            tk = S.op("pe", f_mm(sv[:, c0:512], kview[:, kb * 128:(kb + 1) * 128],
                                                 qview[:, j * 512 + c0:(j + 1) * 512], pi == 0, last),
                                      sdeps, signal=last)
                        if il >= 0:
                            tk = S.op("pe", f_mm(sv[:, c0:c0 + 128], identb, maskb, False, True), [])
                        p_s, pv, pdeps = pt_ring.acquire()
                        t_e = S.op("act", f_act(pv[:, c0:512], sv[:, c0:512], AF.Exp, scale=scale), [tk] + pdeps)
                        s_ring.release(s_s, [t_e])
                        ent.append((p_s, pv, t_e))
                    pend[idx] = (c0, ent)

                def issue_PV(idx):
                    h, j, kb, par = tiles[idx]
                    c0, ent = pend.pop(idx)
                    first = (kb == 0)
                    last = (kb == 4 * j + 3)
                    toks = []
                    for m, (p_s, pv, t_e) in enumerate(ent):
                        ob = bank(o_banks[par][m])
                        S.op("pe", f_mm(ob[:, c0:512], v_of(h, kb), pv[:, c0:512], first, last),
                             [t_e] + (o_free[par][m] if first else []), signal=False)
                    for m, (p_s, pv, t_e) in enumerate(ent):
                        sb_ = bank(sum_banks[par][m])
                        tk = S.op("pe", f_mm(sb_[:, c0:512], onesb, pv[:, c0:512], first, last),
                                  (sum_free[par][m] if first else []))
                        pt_ring.release(p_s, [tk])
                        toks.append(tk)
                    if last:
                        fo, fs, dfn = epilogue(h, j, toks[-1], o_banks[par], sum_banks[par])
                        for m in range(nmaps):
                            o_free[par][m] = fo[m]
                            sum_free[par][m] = fs[m]
                        if dfn is not None:
                            deferred.append([4, dfn])

                n = len(tiles)
                for i in range(min(look, n)):
                    issue_S(i)
                for i in range(n):
                    issue_PV(i)
                    for d in deferred:
                        d[0] -= 1
                    while deferred and deferred[0][0] <= 0:
                        deferred.pop(0)[1]()
                    if i + look < n:
                        issue_S(i + look)
                while deferred:
                    deferred.pop(0)[1]()

            ep_i = [0]
            ep_prev = [[], []]

            def epilogue_A(h, j, t_last, obanks, sbanks):
                r = ep_i[0] % 2
                ep_i[0] += 1
                rec = ep[r * 6 + 0]
                on = ep[r * 6 + 1]
                tsl = slice(j * 512, (j + 1) * 512)
                t_r = S.op("dve", f_recip(rec, bank(sbanks[0])), [t_last] + ep_prev[r])
                t_o = S.op("dve", f_tt(on, bank(obanks[0]), rec, ALU.mult), [t_r])
                t_g = S.op("pool", f_tt(mixA_t[:, h, tsl], on, gaT[:, h, tsl], ALU.mult), [t_o])
                ep_prev[r] = [t_o, t_g]
                return [[t_o]], [[t_r]], None

            def qparts_A(h, m):
                kr = KrLo if h % 2 == 0 else KrHi
                return [(KnT[:, h, :], QnT[:, h, :]), (kr, QrT[:, h // 2, :])]

            attention(1, qparts_A, lambda h, kb: Va[:, kb, h * 128:(h + 1) * 128], MLA_SCALE,
                      s_banks=[0, 1, 2, 7], o_banks=[[5], [3]], sum_banks=[[6], [4]], epilogue=epilogue_A, look=3)

            stop_at(4)
            S.barrier()
            dump("mixA", mixA_t[:], [128, 4, S_LEN], BF16)

            NR[0] = 6
            del xbf[:], t1[:], t2[:], rope_free[:]
            for i_ in range(6):
                xbf.append(tv(i_ * 5120, 1024))
                t1.append(tv(i_ * 5120 + 1024, 2048, F32))
                t2.append(tv(i_ * 5120 + 3072, 2048, F32))
                rope_free.append((None, None))
            rope_i[0] = 0
            RG['acc'] = Ring([bank(2), bank(3), bank(4), bank(0)])
            RG['aux'] = Ring([bank(5), bank(6), bank(7), bank(1)])
            for ci in range(4):
                for tb in range(NB):
                    a_s, acc, tk = proj_fm(1, ci, tb)
                    evac_rope(a_s, acc, tk, permB, tb, [((0, 128), dqT[:, ci, tb * 512:(tb + 1) * 512])])
            flush_deferred()
            load_wgroup(5, 1)
            for ci in range(4):
                for tb in range(NB):
                    a_s, acc, tk = proj_fm(0, ci, tb)
                    tsl = slice(tb * 512, (tb + 1) * 512)
                    evac_rope(a_s, acc, tk, permB, tb, [((0, 64), dkLo[0:64, ci, tsl]), ((64, 128), dkHi[64:128, ci, tsl])])
                    S.op("act", f_act(dkLo[64:128, ci, tsl], mask512[0][64:128, :], AF.Copy, scale=0.0), [t_c])
                    S.op("act", f_act(dkHi[0:64, ci, tsl], mask512[0][0:64, :], AF.Copy, scale=0.0), [t_c])
            flush_deferred()
            load_wgroup(6, 0)
            for ci in range(4):
                for tb in range(NB):
                    a_s, acc, tk = proj_fm(1, ci, tb)
                    t_e = S.op("act", f_act(gbT[:, ci, tb * 512:(tb + 1) * 512], acc, AF.Silu), [tk])
                    RG['acc'].release(a_s, [t_e])
            for t in range(NT):
                a_s, acc, adeps = RG['acc'].acquire()
                tk = None
                for k in range(8):
                    tk = S.op("pe", f_mm(acc, hT[:, k, t * 128:(t + 1) * 128], W_t[:, 0, k, :], k == 0, k == 7),
                              [w_ready[0]] + adeps, signal=(k == 7))
                w_free[0].append(tk)
                t_e = evac_copy(Vb[:, t, :], acc, [tk])
                RG['acc'].release(a_s, [t_e])
            wout = W_t[:].rearrange("p s k f -> p (s k f)").rearrange("p (k f) -> p k f", k=8)
            t_wout = S.dma("pool", d_w[0], wout, wo_d, deps=w_free[0] + w_free[1])

            stop_at(5)
            S.barrier()
            dump("AB2", AB_t[:], [128, 40960], BF16)

            ep_prev[0] = []
            ep_prev[1] = []

            QB = 256
            NQ = S_LEN // QB
            s_ring = Ring([bank(0), bank(1), bank(2), bank(3)])
            pt_ring = Ring(PT)
            o_bk = [4, 5]
            sum_bk = [6, 7]
            o_free = [[], []]
            sum_free = [[], []]
            ssq_free = [None]
            tilesB = []
            blk = 0
            for h in range(4):
                for jq in (0, 7, 1, 6, 2, 5, 3, 4):
                    for kb in range(2 * jq + 2):
                        tilesB.append((h, jq, kb, blk % 2))
                    blk += 1
            pendB = {}
            deferredB = []

            EB0 = 6144
            p1_prev = [[], []]
            dq_prev = [[], [], [], []]
            p2_prev = [[], []]
            blkB = [0]
            p2_i = [0]

            def epilogue_B(h, jq, par, t_last):
                n = blkB[0]
                blkB[0] += 1
                Erec = tv(EB0 + par * 4096, 2048, F32)
                Eo = tv(EB0 + par * 4096 + 2048, 2048, F32)
                k4 = n % 4
                Ed = tv(EB0 + 8192 + k4 * 1536, 1024, F32)
                Eq = tv(EB0 + 8192 + k4 * 1536 + 1024, 512)
                qsl = slice(jq * QB, (jq + 1) * QB)
                if jq in (4,):
                    t_ln = S.op("act", f_act(Erec, bank(sum_bk[par]), AF.Ln), [t_last] + p1_prev[par])
                    t_rc = S.op("act", f_act(Erec, Erec, AF.Exp, scale=-1.0), [t_ln])
                else:
                    t_ln = S.op("dve", f_recip(Erec, bank(sum_bk[par])), [t_last] + p1_prev[par])
                    t_rc = t_ln
                t_o = S.op("dve", f_tt(Eo, bank(o_bk[par]), Erec, ALU.mult), [t_rc, t_last] + p1_prev[par])
                t_d = S.op("dve", f_stt(Ed, Eo[:, QB:2 * QB], neglam, Eo[:, 0:QB], ALU.mult, ALU.add),
                           [t_o, t_nl] + dq_prev[k4])
                p1_prev[par] = [t_d]
                t_q = S.op("pool", f_tt(Eq, Ed, Ed, ALU.mult), [t_d] + dq_prev[k4])

                def part2():
                    r2 = p2_i[0] % 2
                    p2_i[0] += 1
                    Er = tv(EB0 + 14336 + r2 * 2048, 1024, F32)
                    Eob = tv(EB0 + 14336 + r2 * 2048 + 1024, 1024, F32)
                    q_s, qbk, qdeps = s_ring.acquire()
                    sq = qbk[:, 0:QB]
                    t_s = S.op("pe", f_mm(sq, onesb, Eq), [t_q] + qdeps)
                    t_m = S.op("act", f_act(Er, sq, AF.Ln, scale=1.0 / 128, bias=epsB), [t_s, t_eps] + p2_prev[r2])
                    s_ring.release(q_s, [t_m])
                    t_p = S.op("act", f_act(Er, Er, AF.Exp, scale=-0.5), [t_m])
                    t_b = S.op("dve", f_stt(Eob, Ed, sgcol, Er, ALU.mult, ALU.mult), [t_p, t_d, t_sg] + p2_prev[r2])
                    t_g = S.op("pool", f_tt(mixB[:, h, qsl], Eob, gbT[:, h, qsl], ALU.mult), [t_b])
                    p2_prev[r2] = [t_b, t_g]
                    dq_prev[k4] = [t_b, t_s]

                return [t_o], [t_ln], part2

            def issue_SB(idx):
                h, jq, kb, par = tilesB[idx]
                il = kb - 2 * jq
                s_s, sv, sdeps = s_ring.acquire()
                qv = dqT[:, h, jq * QB:(jq + 1) * QB]
                ksl = slice(kb * 128, (kb + 1) * 128)
                live = None
                if il == 1:
                    live = lambda ap: ap.rearrange("p (m c) -> p m c", m=2)[:, :, 128:256]
                    qv1 = dqT[:, h, jq * QB + 128:(jq + 1) * QB]
                    S.op("pe", f_mm(sv[:, 128:2 * QB], identb, tri3, True, False), sdeps, signal=False)
                    S.op("pe", f_mm(sv[:, 128:QB], dkLo[:, h, ksl], qv1, False, False), [], signal=False)
                    tk = S.op("pe", f_mm(sv[:, QB + 128:2 * QB], dkHi[:, h, ksl], qv1, False, True), [])
                elif il == 0:
                    S.op("pe", f_mm(sv, identb, mask512[il], True, False), sdeps, signal=False)
                    S.op("pe", f_mm(sv[:, 0:QB], dkLo[:, h, ksl], qv, False, False), [], signal=False)
                    tk = S.op("pe", f_mm(sv[:, QB:2 * QB], dkHi[:, h, ksl], qv, False, True), [])
                else:
                    S.op("pe", f_mm(sv[:, 0:QB], dkLo[:, h, ksl], qv, True, True), sdeps, signal=False)
                    tk = S.op("pe", f_mm(sv[:, QB:2 * QB], dkHi[:, h, ksl], qv, True, True), [])
                p_s, pv, pdeps = pt_ring.acquire()
                if live is not None:
                    t_e = S.op("act", f_act(live(pv), live(sv), AF.Exp, scale=DIFF_SCALE), [tk] + pdeps)
                else:
                    t_e = S.op("act", f_act(pv, sv, AF.Exp, scale=DIFF_SCALE), [tk] + pdeps)
                s_ring.release(s_s, [t_e])
                pendB[idx] = (p_s, pv, t_e, live)

            def issue_PVB(idx):
                h, jq, kb, par = tilesB[idx]
                p_s, pv, t_e, live = pendB.pop(idx)
                first = (kb == 0)
                last = (kb == 2 * jq + 1)
                ob_, sb_ = bank(o_bk[par]), bank(sum_bk[par])
                vh = Vb[:, kb, h * 128:(h + 1) * 128]
                if live is not None:
                    assert last and not first
                    l0, l1 = slice(128, QB), slice(QB + 128, 2 * QB)
                    S.op("pe", f_mm(ob_[:, l0], vh, pv[:, l0], False, False), [t_e], signal=False)
                    S.op("pe", f_mm(ob_[:, l1], vh, pv[:, l1], False, True), [], signal=False)
                    S.op("pe", f_mm(sb_[:, l0], onesb, pv[:, l0], False, False), [], signal=False)
                    tk = S.op("pe", f_mm(sb_[:, l1], onesb, pv[:, l1], False, True), [])
                else:
                    S.op("pe", f_mm(ob_, vh, pv, first, last),
                         [t_e] + (o_free[par] if first else []), signal=False)
                    tk = S.op("pe", f_mm(sb_, onesb, pv, first, last), (sum_free[par] if first else []))
                pt_ring.release(p_s, [tk])
                if last:
                    fo, fs, dfn = epilogue_B(h, jq, par, tk)
                    o_free[par] = fo
                    sum_free[par] = fs
                    deferredB.append([12, dfn])

            nB = len(tilesB)
            LOOK = 3
            for i in range(min(LOOK, nB)):
                issue_SB(i)
            for i in range(nB):
                issue_PVB(i)
                for d_ in deferredB:
                    d_[0] -= 1
                while deferredB and deferredB[0][0] <= 0:
                    deferredB.pop(0)[1]()
                if i + LOOK < nB:
                    issue_SB(i + LOOK)
            while deferredB:
                deferredB.pop(0)[1]()

            stop_at(6)
            S.barrier()
            dump("mixB", mixB, [128, 4, S_LEN], BF16)

            t_gpost = S.dma("sp", d_c2, gpost, gpost_d.partition_broadcast(128))
            pair_ring = Ring([psp[0], psp[1], psp[2], psp[3]])
            xs_free5 = [None, None, None]
            y_free = [None, None, None]
            out_toks = []
            gv = gpost.rearrange("p (a f) -> p a f", a=2)
            st5 = {}

            def p5_A(t):
                s = t % 3
                tx = S.dma("sp", d_xs5[s], xs5[s], x_d[t * 128:(t + 1) * 128, :], deps=[xs_free5[s]])
                p_s, pp, pdeps = pair_ring.acquire()
                tk = None
                for half in range(2):
                    for c in range(8):
                        src_ = mixA_t[:, c, t * 128:(t + 1) * 128] if c < 4 else mixB[:, c - 4, t * 128:(t + 1) * 128]
                        tk = S.op("pe", f_mm(pp[:, half, :], src_, wout[:, c, half * 512:(half + 1) * 512], c == 0, c == 7),
                                  [t_wout] + pdeps, signal=(half == 1 and c == 7))
                t_q0 = S.op("act", f_act(junk5[:, 0:512], pp[:, 0, :], AF.Square, accum_out=ssq5[:, 2 * t:2 * t + 1]), [tk])
                t_q1 = S.op("act", f_act(junk5[:, 512:1024], pp[:, 1, :], AF.Square,
                                         accum_out=ssq5[:, 2 * t + 1:2 * t + 2]), [tk, t_q0])
                t_m = S.op("dve", f_stt(ms5[:, t:t + 1], ssq5[:, 2 * t:2 * t + 1], 1.0, ssq5[:, 2 * t + 1:2 * t + 2],
                                        ALU.mult, ALU.add), [t_q0, t_q1])
                t_m2 = S.op("dve", f_ts(rstd5[:, t:t + 1], ms5[:, t:t + 1], 1.0 / D, NORM_EPS, ALU.mult, ALU.add), [t_m])
                t_r = S.op("pool", f_tt(ms5[:, t:t + 1], rstd5[:, t:t + 1], mhcol, ALU.pow), [t_m2, t_mh2])
                st5[t] = (tx, p_s, pp, t_r)

            def p5_B(t):
                s = t % 3
                tx, p_s, pp, t_r = st5.pop(t)
                yv = ybuf[s].rearrange("p (a f) -> p a f", a=2)
                t_y = None
                for half in range(2):
                    t_y = S.op("act", f_act(yv[:, half, :], pp[:, half, :], AF.Identity, scale=ms5[:, t:t + 1]),
                               [t_r, y_free[s]])
                pair_ring.release(p_s, [t_y])
                t_w = S.op("dve", f_tt(ybuf[s], ybuf[s], gpost, ALU.mult), [t_y, t_gpost])
                t_a = S.op("dve", f_tt(ybuf[s], ybuf[s], xs5[s], ALU.add), [t_w, tx])
                xs_free5[s] = t_a
                t_o = S.dma("sp", d_out[s], out_d[t * 128:(t + 1) * 128, :], ybuf[s], deps=[t_a])
                y_free[s] = t_o
                out_toks.append(t_o)

            p5_A(0)
            for t in range(NT):
                if t + 1 < NT:
                    p5_A(t + 1)
                p5_B(t)
            S.barrier(out_toks[-3:])
            S.emit()
        except _Stop:
            pass
    if debug:
        return nc, dbg_outs
    return nc


_CACHE = {}


def _prep_inputs(inputs):
    x = np.asarray(inputs["x"], dtype=np.float32)
    wl, wo, wkv = _layout_weights(np.asarray(inputs["w_in"]), np.asarray(inputs["w_uk"]),
                                  np.asarray(inputs["w_uv"]), np.asarray(inputs["w_out"]))
    cmat, tabs = _host_consts()
    gpre = np.ascontiguousarray(np.asarray(inputs["ln_pre_g"], dtype=np.float32).reshape(1, D))
    gpost = np.ascontiguousarray(np.asarray(inputs["ln_post_g"], dtype=np.float32).reshape(1, D))
    cols = np.ascontiguousarray(np.stack([np.asarray(inputs["kv_norm_g"], dtype=np.float32).reshape(128),
                                          np.asarray(inputs["subln_g"], dtype=np.float32).reshape(128)], axis=1))
    lamv = np.ascontiguousarray(np.concatenate([
        np.asarray(inputs["lambda_q1"], dtype=np.float32).reshape(64),
        np.asarray(inputs["lambda_k1"], dtype=np.float32).reshape(64),
        np.asarray(inputs["lambda_q2"], dtype=np.float32).reshape(64),
        np.asarray(inputs["lambda_k2"], dtype=np.float32).reshape(64)]).reshape(1, 256))
    shared = {"wl": wl, "wo": wo, "wkv": wkv, "cmat": cmat, "tabs": tabs, "gpre": gpre, "gpost": gpost,
              "cols": cols, "lamv": lamv}
    return x, shared


def kernel(**inputs):
    x, shared = _prep_inputs(inputs)
    B = x.shape[0]
    if "nc" not in _CACHE:
        _CACHE["nc"] = build_program()
    nc = _CACHE["nc"]
    in_maps = [dict(shared, x=np.ascontiguousarray(x[b])) for b in range(B)]
    res = run_bass_kernel_spmd(nc, in_maps, core_ids=list(range(B)))
    out = np.stack([np.asarray(r["out"], dtype=np.float32) for r in res.results], axis=0)
    return out
```

```python
import math
from contextlib import ExitStack

import numpy as np

import concourse.bass as bass
import concourse.mybir as mybir
from concourse.bass_utils import run_bass_kernel_spmd

F32 = mybir.dt.float32
BF16 = mybir.dt.bfloat16
AF = mybir.ActivationFunctionType
ALU = mybir.AluOpType

S_LEN = 2048
D = 1024
NT = S_LEN // 128
NB = S_LEN // 512
ROPE_THETA = 500000.0
NORM_EPS = 1e-6
SUBLN_EPS = 1e-5
MLA_SCALE = 1.0 / math.sqrt(128 + 64)
DIFF_SCALE = 1.0 / math.sqrt(64)
LAMBDA_INIT = 0.8 - 0.6 * math.exp(-0.3 * 0)
MASK_NEG = -30000.0

ENGS = ["pe", "act", "dve", "pool", "sp"]


class Sched:
    def __init__(self, nc, ctx):
        self.nc = nc
        self.ctx = ctx
        self.streams = {e: [] for e in ENGS}
        self.sem = {e: ctx.enter_context(nc.semaphore("s_" + e)) for e in ENGS if e != "sp"}
        self.cnt = {e: 0 for e in ENGS}
        self.waited = {e: {} for e in ENGS}
        self.nd = 0

    def dma_sem(self):
        self.nd += 1
        return [self.ctx.enter_context(self.nc.semaphore(f"d{self.nd}")), 0]

    def _waits(self, eng, deps):
        ws = []
        for d in deps:
            if d is None:
                continue
            if isinstance(d, list):
                ws += self._waits(eng, d)
                continue
            sem, val = d
            k = id(sem)
            if self.waited[eng].get(k, 0) >= val:
                continue
            self.waited[eng][k] = val
            ws.append((sem, val))
        return ws

    def op(self, eng, fn, deps=(), signal=True):
        ws = self._waits(eng, deps)
        ent = [ws, fn, None]
        self.streams[eng].append(ent)
        if signal:
            self.cnt[eng] += 1
            ent[2] = (self.sem[eng], 1)
            return (self.sem[eng], self.cnt[eng])
        return None

    def last_token(self, eng):
        st = self.streams[eng]
        for ent in reversed(st):
            if ent[1] is None:
                continue
            if ent[2] is None:
                self.cnt[eng] += 1
                ent[2] = (self.sem[eng], 1)
            elif ent[2][0] is not self.sem[eng]:
                continue
            break
        if self.cnt[eng] == 0:
            return None
        return (self.sem[eng], self.cnt[eng])

    def dma(self, queue, dsem, out, in_, deps=()):
        ws = self._waits(queue, deps)
        dsem[1] += 16
        self.streams[queue].append([ws, lambda e: e.dma_start(out=out, in_=in_), (dsem[0], 16)])
        return (dsem[0], dsem[1])

    def wait(self, eng, deps):
        ws = self._waits(eng, deps)
        if ws:
            self.streams[eng].append([ws, None, None])

    def barrier(self, extra=()):
        toks = [self.last_token(e) for e in ("pe", "act", "dve", "pool")]
        toks = [t for t in toks if t is not None] + list(extra)
        for e in ENGS:
            self.wait(e, toks)
        return toks

    def emit(self):
        with self.nc.Block() as block:
            def run(name):
                def f(e):
                    embed_ok = name in ("act", "dve", "pool")
                    for ws, fn, inc in self.streams[name]:
                        ws = list(ws)
                        emb = None
                        if fn is not None and ws and embed_ok and not getattr(fn, "no_embed", False) \
                                and (inc is None or inc[0] is self.sem.get(name)):
                            emb = ws.pop()
                        for sem, val in ws:
                            e.wait_ge(sem, val)
                        if fn is not None:
                            ins = fn(e)
                            if emb is not None:
                                ins._wait_ge(emb[0], emb[1])
                            if inc is not None:
                                ins.then_inc(inc[0], inc[1])
                return f
            block.sync(run("sp"))
            block.tensor(run("pe"))
            block.scalar(run("act"))
            block.vector(run("dve"))
            block.gpsimd(run("pool"))


def f_mm(out, lhsT, rhs, start=True, stop=True):
    return lambda e: e.matmul(out, lhsT=lhsT, rhs=rhs, start=start, stop=stop)


def f_tr(out, in_, ident):
    return lambda e: e.transpose(out=out, in_=in_, identity=ident)


def f_act(out, in_, func, **kw):
    fn = lambda e: e.activation(out=out, in_=in_, func=func, **kw)
    if "accum_out" in kw:
        fn.no_embed = True
    return fn


def f_acopy(out, in_):
    return lambda e: e.copy(out=out, in_=in_)


def f_copy(out, in_):
    return lambda e: e.tensor_copy(out=out, in_=in_)


def f_tt(out, in0, in1, op):
    return lambda e: e.tensor_tensor(out=out, in0=in0, in1=in1, op=op)


def f_ts(out, in0, s1, s2, op0, op1=None):
    if op1 is None:
        return lambda e: e.tensor_scalar(out=out, in0=in0, scalar1=s1, scalar2=None, op0=op0)
    return lambda e: e.tensor_scalar(out=out, in0=in0, scalar1=s1, scalar2=s2, op0=op0, op1=op1)


def f_stt(out, in0, scalar, in1, op0, op1):
    return lambda e: e.scalar_tensor_tensor(out=out, in0=in0, scalar=scalar, in1=in1, op0=op0, op1=op1)


def f_recip(out, in_):
    return lambda e: e.reciprocal(out=out, in_=in_)


def f_memset(ap, v):
    return lambda e: e.memset(ap, v)


class Ring:
    def __init__(self, views):
        self.views = views
        self.free = [[] for _ in views]
        self.i = 0

    def acquire(self):
        s = self.i % len(self.views)
        self.i += 1
        deps = self.free[s]
        self.free[s] = []
        return s, self.views[s], deps

    def release(self, s, toks):
        self.free[s] = [t for t in toks if t is not None]


def _rope_tables():
    t = np.arange(S_LEN, dtype=np.float32)

    def tab(dim):
        inv = (np.float32(ROPE_THETA) ** (-(np.arange(0, dim, 2, dtype=np.float32)) / np.float32(dim))).astype(np.float32)
        ang = (t[:, None] * inv[None, :]).astype(np.float32)
        return np.cos(ang.astype(np.float64)).astype(np.float32), np.sin(ang.astype(np.float64)).astype(np.float32)

    ca, sa = tab(64)
    cb, sb = tab(16)
    CA = np.zeros((128, S_LEN), np.float32)
    SA = np.zeros((128, S_LEN), np.float32)
    CB = np.ones((128, S_LEN), np.float32)
    SB = np.zeros((128, S_LEN), np.float32)
    PA = np.zeros((128, 128), np.float32)
    PB = np.zeros((128, 128), np.float32)
    for r in range(128):
        d = r % 64
        i = d % 32
        CA[r] = ca[:, i]
        if d < 32:
            SA[r] = -sa[:, i]
            PA[r + 32, r] = 1.0
        else:
            SA[r] = sa[:, i]
            PA[r - 32, r] = 1.0
        if d < 8:
            CB[r] = cb[:, d]
            SB[r] = -sb[:, d]
            PB[r + 8, r] = 1.0
        elif d < 16:
            CB[r] = cb[:, d - 8]
            SB[r] = sb[:, d - 8]
            PB[r - 8, r] = 1.0
    return CA, SA, CB, SB, PA, PB


def _host_consts():
    CA, SA, CB, SB, PA, PB = _rope_tables()
    ident = np.eye(128, dtype=np.float32)
    kk = np.arange(128)[:, None]
    qq = np.arange(128)[None, :]
    mask = np.where(qq >= kk, 0.0, MASK_NEG).astype(np.float32)
    zero = np.zeros((128, 128), np.float32)
    neg = np.full((128, 128), MASK_NEG, np.float32)
    cmat = np.stack([ident, PA, np.ones((128, 128), np.float32), PB, mask,
                     mask, zero, mask, zero,
                     neg, mask, neg, mask],
                    axis=1)
    tabs = np.stack([CA, SA, CB, SB], axis=0)
    return np.ascontiguousarray(cmat), np.ascontiguousarray(tabs)


def _layout_weights(w_in, w_uk, w_uv, w_out):
    w = w_in[0]
    q_nope = w[:, 0:512]
    q_rope = w[:, 512:768]
    c_kv = w[:, 768:896]
    k_rope = w[:, 896:960]
    g_a = w[:, 960:1472]
    dq = w[:, 1472:1984]
    dk = w[:, 1984:2496]
    dv = w[:, 2496:3008]
    g_b = w[:, 3008:3520]
    groups = [
        np.concatenate([q_rope, c_kv, k_rope, k_rope], axis=1),
        q_nope,
        g_a,
        dq,
        dk,
        g_b,
        dv,
    ]
    wl = np.stack([g.reshape(8, 128, 512).transpose(1, 0, 2) for g in groups], axis=0)
    wo = w_out[0].reshape(8, 128, 1024).transpose(1, 0, 2)
    wkv = np.stack([w_uk[0], w_uv[0]], axis=1)
    return (np.ascontiguousarray(wl, dtype=np.float32), np.ascontiguousarray(wo, dtype=np.float32),
            np.ascontiguousarray(wkv, dtype=np.float32))


def build_program(debug=False, stop=None):
    nc = bass.Bass("TRN2", target_bir_lowering=False)
    x_d = nc.dram_tensor("x", [S_LEN, D], F32, kind="ExternalInput").ap()
    wl_d = nc.dram_tensor("wl", [7, 128, 8, 512], F32, kind="ExternalInput").ap()
    wo_d = nc.dram_tensor("wo", [128, 8, 1024], F32, kind="ExternalInput").ap()
    wkv_d = nc.dram_tensor("wkv", [128, 2, 512], F32, kind="ExternalInput").ap()
    cmat_d = nc.dram_tensor("cmat", [128, 13, 128], F32, kind="ExternalInput").ap()
    tabs_d = nc.dram_tensor("tabs", [4, 128, S_LEN], F32, kind="ExternalInput").ap()
    gpre_d = nc.dram_tensor("gpre", [1, D], F32, kind="ExternalInput").ap()
    gpost_d = nc.dram_tensor("gpost", [1, D], F32, kind="ExternalInput").ap()
    cols_d = nc.dram_tensor("cols", [128, 2], F32, kind="ExternalInput").ap()
    lamv_d = nc.dram_tensor("lamv", [1, 256], F32, kind="ExternalInput").ap()
    out_d = nc.dram_tensor("out", [S_LEN, D], F32, kind="ExternalOutput").ap()
    dbg_outs = {}

    ctx = ExitStack()
    with ctx:
        S = Sched(nc, ctx)

        def sb(name, shape, dt):
            return nc.alloc_sbuf_tensor(name, shape, dt)

        class _Stop(Exception):
            pass

        def stop_at(k):
            if stop is not None and stop == k:
                S.barrier()
                S.emit()
                raise _Stop()

        hT_t = sb("hT", [128, 8, S_LEN], BF16)
        AB_t = sb("AB", [128, 40960], BF16)
        mixA_t = sb("mixA", [128, 4, S_LEN], BF16)
        W_t = sb("W", [128, 2, 8, 512], BF16)
        tab_t = sb("tab", [128, 2, S_LEN], F32)
        T_t = sb("T", [128, 17408], BF16)
        cm_t = sb("cm", [128, 13, 128], BF16)
        small = sb("small", [128, 128], F32)
        lam_t = sb("lamt", [128, 320], F32)
        xs2_t = sb("xs2", [128, 2, 1024], F32)

        identb = cm_t[:, 0, :]
        permA = cm_t[:, 1, :]
        onesb = cm_t[:, 2, :]
        permB = cm_t[:, 3, :]
        maskb = cm_t[:, 4, :]
        tri3 = cm_t[:, 10:13, :].rearrange("p a b -> p (a b)")
        mask512 = [cm_t[:, 5:9, :].rearrange("p a b -> p (a b)"), cm_t[:, 9:13, :].rearrange("p a b -> p (a b)")]

        hT = hT_t
        mixB = hT_t[:, 0:4, :]

        def ab(off_k, n_k):
            return AB_t[:, off_k * 512:(off_k + n_k) * 512]

        QnT = ab(0, 16).rearrange("p (h t) -> p h t", h=4)
        QrT = ab(16, 8).rearrange("p (c t) -> p c t", c=2)
        KnT = ab(24, 16).rearrange("p (h t) -> p h t", h=4)
        KrLo = ab(40, 4)
        KrHi = ab(44, 4)
        Va = ab(48, 16).rearrange("p (k f) -> p k f", k=16)
        gaT = ab(64, 16).rearrange("p (h t) -> p h t", h=4)
        dqT = ab(0, 16).rearrange("p (h t) -> p h t", h=4)
        dkLo = ab(16, 16).rearrange("p (h t) -> p h t", h=4)
        dkHi = ab(32, 16).rearrange("p (h t) -> p h t", h=4)
        Vb = ab(48, 16).rearrange("p (k f) -> p k f", k=16)
        gbT = ab(64, 16).rearrange("p (h t) -> p h t", h=4)

        def tv(off_b, n_b, dt=BF16):
            v = T_t[:, off_b // 2:(off_b + n_b) // 2]
            if dt == F32:
                v = v.bitcast(F32)
            return v

        xs = [tv(0, 4096, F32), tv(4096, 4096, F32), xs2_t[:, 0, :], xs2_t[:, 1, :]]
        xn = [tv(8192, 2048), tv(10240, 2048)]
        junk = tv(12288, 2048)
        xbf = [tv(14336, 1024), tv(15360, 1024)]
        t1 = [tv(16384, 2048, F32), tv(18432, 2048, F32)]
        t2 = [tv(20480, 2048, F32), tv(22528, 2048, F32)]
        ckvnT = tv(24576, 4096)
        wkv = tv(28672, 2048).rearrange("p (a f) -> p a f", a=2)
        gpre = tv(30720, 4096, F32)
        PT = [tv(i * 1024, 1024) for i in range(6)]
        ep = [tv(6144 + i * 2048, 2048, F32) for i in range(12)]
        dsq = [tv(30720, 1024), tv(31744, 1024)]
        xs5 = [tv(i * 4096, 4096, F32) for i in range(3)]
        ybuf = [tv(12288 + i * 4096, 4096, F32) for i in range(3)]
        gpost = tv(24576, 4096, F32)
        junk5 = tv(28672, 2048)

        ssq = small[:, 0:16]
        ms = small[:, 16:32]
        rstd = small[:, 32:48]
        ssq5 = small[:, 48:80]
        ms5 = small[:, 80:96]
        rstd5 = small[:, 96:112]
        cols = small[:, 112:114]
        sgcol = small[:, 114:115]
        lsum = small[:, 115:117]
        lexp = small[:, 117:119]
        neglam = small[:, 119:120]
        mhcol = small[:, 120:121]
        epsA = small[:, 121:122]
        epsB = small[:, 122:123]

        psp = [nc.alloc_psum_tensor(f"psp{i}", [128, 2, 512], F32) for i in range(4)]

        def bank(b):
            return psp[b // 2][:, b % 2, :]

        d_xs = [S.dma_sem() for _ in range(4)]
        d_w = [S.dma_sem(), S.dma_sem()]
        d_tab = S.dma_sem()
        d_c = S.dma_sem()
        d_c2 = S.dma_sem()
        d_out = [S.dma_sem(), S.dma_sem(), S.dma_sem()]
        d_xs5 = [S.dma_sem(), S.dma_sem(), S.dma_sem()]
        d_dbg = S.dma_sem()

        try:
            x_tok = {}
            x_tok[0] = S.dma("sp", d_xs[0], xs[0], x_d[0:128, :])
            t_c = S.dma("pool", d_c, cm_t[:, 0:3, :], cmat_d[:, 0:3, :])
            S.dma("sp", d_c2, gpre, gpre_d.partition_broadcast(128))
            S.dma("sp", d_c2, cols, cols_d)
            t_c2 = S.dma("sp", d_c2, lam_t[:, 0:256], lamv_d.partition_broadcast(128))
            t_mh2 = S.op("pool", f_memset(mhcol, -0.5))
            S.op("pool", f_memset(epsA, NORM_EPS))
            t_eps = S.op("pool", f_memset(epsB, SUBLN_EPS))
            stop_at(0)

            pT_ring = Ring([bank(0), bank(1)])
            RG = {'acc': Ring([bank(2), bank(3), bank(4), bank(5)]), 'aux': Ring([bank(6), bank(7)])}
            xs_free = [None] * 4
            xn_free = [None, None]
            hT_ready = [None] * NT
            w_free = [[], []]
            w_ready = [None, None]
            flip = [0]
            junk_free = [None]

            def load_wgroup(g, slot, extra=()):
                w_ready[slot] = S.dma("pool", d_w[slot], W_t[:, slot], wl_d[g], deps=list(w_free[slot]) + list(extra))
                w_free[slot] = []

            xn_tok = {}

            def p0_A(t):
                s = t % 2
                xq = t % 4
                tx = x_tok.pop(t) if t in x_tok else S.dma("sp", d_xs[xq], xs[xq], x_d[t * 128:(t + 1) * 128, :],
                                                           deps=[xs_free[xq]])
                t_ssq = S.op("act", f_act(junk, xs[xq], AF.Square, accum_out=ssq[:, t:t + 1]), [tx, junk_free[0]])
                junk_free[0] = t_ssq
                t_ms = S.op("dve", f_ts(ms[:, t:t + 1], ssq[:, t:t + 1], 1.0 / D, NORM_EPS, ALU.mult, ALU.add), [t_ssq])
                t_rs = S.op("pool", f_tt(rstd[:, t:t + 1], ms[:, t:t + 1], mhcol, ALU.pow), [t_ms, t_mh2])
                t_xn = S.op("dve", f_stt(xn[s], xs[xq], rstd[:, t:t + 1], gpre, ALU.mult, ALU.mult),
                            [t_rs, t_c2, tx, xn_free[s]])
                xs_free[xq] = t_xn
                xn_tok[t] = t_xn

            def p0_B(t):
                s = t % 2
                t_xn = xn_tok.pop(t)
                ps_, pv, pdeps = pT_ring.acquire()
                pTv = pv.bitcast(BF16).rearrange("p (c t) -> p c t", c=8)
                t_tr = None
                for c in range(8):
                    t_tr = S.op("pe", f_tr(pTv[:, c, :], xn[s][:, c * 128:(c + 1) * 128], identb),
                                [t_xn, t_c] + pdeps, signal=(c == 7))
                xn_free[s] = t_tr
                dst = hT[:, :, t * 128:(t + 1) * 128]
                if t % 4 != 3:
                    t_h = S.op("act", f_acopy(dst, pTv), [t_tr])
                else:
                    t_h = S.op("dve", f_copy(dst, pTv), [t_tr])
                pT_ring.release(ps_, [t_h])
                hT_ready[t] = t_h

            p0_steps = [("A", 0)]
            for t in range(NT):
                if t + 1 < NT:
                    p0_steps.append(("A", t + 1))
                p0_steps.append(("B", t))

            def p0_run(until_tile=None, nsteps=None):
                n = 0
                while p0_steps:
                    if nsteps is not None and n >= nsteps:
                        break
                    if until_tile is not None and hT_ready[until_tile] is not None:
                        break
                    kind, t = p0_steps.pop(0)
                    (p0_A if kind == "A" else p0_B)(t)
                    n += 1

            deferred_pe = []

            def run_deferred():
                todo = list(deferred_pe)
                del deferred_pe[:]
                for fn_ in todo:
                    fn_()

            def flush_deferred():
                while deferred_pe:
                    run_deferred()

            def proj_fm(slot, ci, tb):
                a_s, acc, adeps = RG['acc'].acquire()
                tk = None
                for k in range(8):
                    tk = S.op("pe", f_mm(acc, W_t[:, slot, k, ci * 128:(ci + 1) * 128],
                                         hT[:, k, tb * 512:(tb + 1) * 512], k == 0, k == 7),
                              [w_ready[slot]] + hT_ready[4 * tb:4 * tb + 4] + adeps, signal=(k == 7))
                w_free[slot].append(tk)
                run_deferred()
                return a_s, acc, tk

            def evac_copy(dst, src, deps):
                flip[0] ^= 1
                if flip[0]:
                    return S.op("act", f_acopy(dst, src), deps)
                return S.op("dve", f_copy(dst, src), deps)

            rope_i = [0]
            rope_free = [(None, None), (None, None)]
            NR = [2]
            tab_ready = [None]
            tab_users = []

            def evac_rope(acc_s, acc, t_mm, perm, tb, dsts):
                r = rope_i[0] % NR[0]
                rope_i[0] += 1
                tsl = slice(tb * 512, (tb + 1) * 512)
                t_xb = S.op("act", f_acopy(xbf[r], acc), [t_mm, rope_free[r][0]])
                t_a = S.op("dve", f_tt(t1[r], acc, tab_t[:, 0, tsl], ALU.mult), [t_mm, t_xb, tab_ready[0], rope_free[r][1]])
                RG['acc'].release(acc_s, [t_xb, t_a])
                old_free = rope_free[r][1]
                tab_r = tab_ready[0]

                def part2():
                    p_s, pr, pdeps = RG['aux'].acquire()
                    t_pm = S.op("pe", f_mm(pr, perm, xbf[r]), [t_xb, t_c] + pdeps)
                    t_b = S.op("dve", f_tt(t2[r], pr, tab_t[:, 1, tsl], ALU.mult), [t_pm, tab_r, old_free])
                    RG['aux'].release(p_s, [t_b])
                    t_c_ = []
                    for (p0, p1), dst in dsts:
                        eng_ = "dve" if p0 == 64 else "pool"
                        t_c_.append(S.op(eng_, f_tt(dst, t1[r][p0:p1, :], t2[r][p0:p1, :], ALU.add), [t_a, t_b]))
                    rope_free[r] = (t_pm, t_c_)
                    tab_users.append(t_b)

                deferred_pe.append(part2)

            def load_tabs(which, extra=()):
                deps = list(tab_users) + list(extra)
                del tab_users[:]
                S.dma("sp", d_tab, tab_t[:, 0, :], tabs_d[2 * which], deps=deps)
                tab_ready[0] = S.dma("sp", d_tab, tab_t[:, 1, :], tabs_d[2 * which + 1], deps=deps)

            load_wgroup(0, 0)
            d_cB = S.dma_sem()
            t_cB = S.dma("pool", d_cB, cm_t[:, 3:13, :], cmat_d[:, 3:13, :])
            for t in range(1, 4):
                x_tok[t] = S.dma("sp", d_xs[t], xs[t], x_d[t * 128:(t + 1) * 128, :])
            load_tabs(0, extra=[x_tok[3], w_ready[0]])
            S.op("pool", f_memset(KrLo[64:128, :], 0.0))
            S.op("pool", f_memset(KrHi[0:64, :], 0.0))

            def ckv_chunk(tb):
                tsl = slice(tb * 512, (tb + 1) * 512)
                a_s, acc, tk = proj_fm(0, 2, tb)
                r = rope_i[0] % NR[0]
                rope_i[0] += 1
                sqb = xbf[r]
                msb = t1[r]
                rsb = t2[r]
                t_sq = S.op("act", f_act(sqb, acc, AF.Square), [tk, rope_free[r][0]])
                old_free = rope_free[r][1]

                def part2():
                    p_s, pr, pdeps = RG['aux'].acquire()
                    t_ss = S.op("pe", f_mm(pr, onesb, sqb), [t_sq, t_c] + pdeps)
                    t_m = S.op("act", f_act(msb, pr, AF.Ln, scale=1.0 / 128, bias=epsA), [t_ss, old_free, t_eps])
                    RG['aux'].release(p_s, [t_m])
                    t_r = S.op("act", f_act(rsb, msb, AF.Exp, scale=-0.5), [t_m, old_free])
                    t_ck = S.op("dve", f_stt(ckvnT[:, tsl], acc, cols[:, 0:1], rsb, ALU.mult, ALU.mult), [t_r, tk, t_c2])
                    RG['acc'].release(a_s, [t_sq, t_ck])
                    rope_free[r] = (t_ss, t_ck)
                    kv_up(tb, t_ck)

                deferred_pe.append(part2)

            kv_items = []

            def kv_up(tb, t_ck):
                tsl = slice(tb * 512, (tb + 1) * 512)

                def one(lhsT, rhs, dst):
                    def run():
                        p_s, pr, pdeps = RG['aux'].acquire()
                        t_u = S.op("pe", f_mm(pr, lhsT, rhs), [t_ck, t_wkv] + pdeps)
                        t_e = S.op("dve", f_copy(dst, pr), [t_u])
                        RG['aux'].release(p_s, [t_e])
                    return run

                for h in range(4):
                    kv_items.append(one(wkv[:, 0, h * 128:(h + 1) * 128], ckvnT[:, tsl], KnT[:, h, tsl]))
                for kl in range(4):
                    kb = 4 * tb + kl
                    kv_items.append(one(ckvnT[:, kb * 128:(kb + 1) * 128], wkv[:, 1, :], Va[:, kb, :]))

            def g1_block(tb):
                tsl = slice(tb * 512, (tb + 1) * 512)
                ckv_chunk(tb)
                p0_run(nsteps=2)
                for ci in range(2):
                    a_s, acc, tk = proj_fm(0, ci, tb)
                    evac_rope(a_s, acc, tk, permA, tb, [((0, 128), QrT[:, ci, tsl])])
                    p0_run(nsteps=2)
                a_s, acc, tk = proj_fm(0, 3, tb)
                evac_rope(a_s, acc, tk, permA, tb, [((0, 64), KrLo[0:64, tsl]), ((64, 128), KrHi[64:128, tsl])])
                p0_run(nsteps=2)

            p0_run(until_tile=3)
            load_wgroup(1, 1)
            d_wkv = S.dma_sem()
            t_wkv = S.dma("pool", d_wkv, wkv, wkv_d)
            t_l0 = S.op("dve", f_tt(lam_t[:, 256:320], lam_t[:, 0:64], lam_t[:, 64:128], ALU.mult), [t_c2])
            t_l1 = S.op("dve", lambda e: e.reduce_sum(out=lsum[:, 0:1], in_=lam_t[:, 256:320],
                                                      axis=mybir.AxisListType.X), [t_l0])
            t_l0b = S.op("dve", f_tt(lam_t[:, 256:320], lam_t[:, 128:192], lam_t[:, 192:256], ALU.mult), [t_l1])
            t_l2 = S.op("dve", lambda e: e.reduce_sum(out=lsum[:, 1:2], in_=lam_t[:, 256:320],
                                                      axis=mybir.AxisListType.X), [t_l0b])
            t_le = S.op("act", f_act(lexp, lsum, AF.Exp), [t_l2])
            t_nl = S.op("dve", f_stt(neglam, lexp[:, 1:2], -LAMBDA_INIT, lexp[:, 0:1], ALU.add, ALU.subtract), [t_le])
            t_sg = S.op("dve", f_ts(sgcol, cols[:, 1:2], 1.0 - LAMBDA_INIT, None, ALU.mult), [t_c2, t_nl])

            def g0_block(tb):
                for ci in range(4):
                    a_s, acc, tk = proj_fm(1, ci, tb)
                    t_e = S.op("act", f_acopy(QnT[:, ci, tb * 512:(tb + 1) * 512], acc), [tk])
                    RG['acc'].release(a_s, [t_e])
                    p0_run(nsteps=1)

            for tb in range(2):
                g1_block(tb)
                g0_block(tb)
                p0_run(until_tile=4 * tb + 7)
            g1_block(2)
            p0_run()
            g1_block(3)
            flush_deferred()
            load_wgroup(2, 0)
            load_tabs(1)
            g0_block(2)
            g0_block(3)
            flush_deferred()
            load_wgroup(3, 1)
            for ci in range(4):
                for tb in range(NB):
                    a_s, acc, tk = proj_fm(0, ci, tb)
                    t_e = S.op("act", f_act(gaT[:, ci, tb * 512:(tb + 1) * 512], acc, AF.Silu), [tk])
                    RG['acc'].release(a_s, [t_e])
                    for _ in range(2):
                        if kv_items:
                            kv_items.pop(0)()
            while kv_items:
                kv_items.pop(0)()
            load_wgroup(4, 0)

            def dump(name, view, shape, dt):
                if not debug:
                    return
                o = nc.dram_tensor("dbg_" + name, shape, dt, kind="ExternalOutput").ap()
                dbg_outs[name] = o
                tok = S.dma("sp", d_dbg, o, view, deps=S.barrier())
                S.barrier([tok])

            stop_at(3)
            S.barrier([t_cB])
            dump("hT", hT[:], [128, 8, S_LEN], BF16)
            dump("AB1", AB_t[:], [128, 40960], BF16)

            s_ring_box = [None]

            def attention(nmaps, qT_of, v_of, scale, s_banks, o_banks, sum_banks, epilogue, look):
                npar = len(o_banks)
                s_ring = Ring([bank(b) for b in s_banks])
                s_ring_box[0] = s_ring
                pt_ring = Ring(PT)
                o_free = [[[] for _ in range(nmaps)] for _ in range(npar)]
                sum_free = [[[] for _ in range(nmaps)] for _ in range(npar)]
                tiles = []
                blk = 0
                for h in range(4):
                    for j in (0, 3, 1, 2):
                        for kb in range(4 * j + 4):
                            tiles.append((h, j, kb, blk % npar))
                        blk += 1
                pend = {}
                deferred = []

                def issue_S(idx):
                    h, j, kb, par = tiles[idx]
                    il = kb - 4 * j
                    c0 = 128 * max(0, il)
                    ent = []
                    for m in range(nmaps):
                        s_s, sv, sdeps = s_ring.acquire()
                        parts = qT_of(h, m)
                        tk = None
                        for pi, (kview, qview) in enumerate(parts):
                            last = (pi == len(parts) - 1) and il < 0
                            tk = S.op("pe", f_mm(sv[:, c0:512], kview[:, kb * 128:(kb + 1) * 128],
                                                 qview[:, j * 512 + c0:(j + 1) * 512], pi == 0, last),
                                      sdeps, signal=last)
                        if il >= 0:
                            tk = S.op("pe", f_mm(sv[:, c0:c0 + 128], identb, maskb, False, True), [])
                        p_s, pv, pdeps = pt_ring.acquire()
                        t_e = S.op("act", f_act(pv[:, c0:512], sv[:, c0:512], AF.Exp, scale=scale), [tk] + pdeps)
                        s_ring.release(s_s, [t_e])
                        ent.append((p_s, pv, t_e))
                    pend[idx] = (c0, ent)

                def issue_PV(idx):
                    h, j, kb, par = tiles[idx]
                    c0, ent = pend.pop(idx)
                    first = (kb == 0)
                    last = (kb == 4 * j + 3)
                    toks = []
                    for m, (p_s, pv, t_e) in enumerate(ent):
                        ob = bank(o_banks[par][m])
                        S.op("pe", f_mm(ob[:, c0:512], v_of(h, kb), pv[:, c0:512], first, last),
                             [t_e] + (o_free[par][m] if first else []), signal=False)
                    for m, (p_s, pv, t_e) in enumerate(ent):
                        sb_ = bank(sum_banks[par][m])
                        tk = S.op("pe", f_mm(sb_[:, c0:512], onesb, pv[:, c0:512], first, last),
                                  (sum_free[par][m] if first else []))
                        pt_ring.release(p_s, [tk])
                        toks.append(tk)
                    if last:
                        fo, fs, dfn = epilogue(h, j, toks[-1], o_banks[par], sum_banks[par])
                        for m in range(nmaps):
                            o_free[par][m] = fo[m]
                            sum_free[par][m] = fs[m]
                        if dfn is not None:
                            deferred.append([4, dfn])

                n = len(tiles)
                for i in range(min(look, n)):
                    issue_S(i)
                for i in range(n):
                    issue_PV(i)
                    for d in deferred:
                        d[0] -= 1
                    while deferred and deferred[0][0] <= 0:
                        deferred.pop(0)[1]()
                    if i + look < n:
                        issue_S(i + look)
                while deferred:
                    deferred.pop(0)[1]()

            ep_i = [0]
            ep_prev = [[], []]

            def epilogue_A(h, j, t_last, obanks, sbanks):
                r = ep_i[0] % 2
                ep_i[0] += 1
                rec = ep[r * 6 + 0]
                on = ep[r * 6 + 1]
                tsl = slice(j * 512, (j + 1) * 512)
                t_r = S.op("dve", f_recip(rec, bank(sbanks[0])), [t_last] + ep_prev[r])
                t_o = S.op("dve", f_tt(on, bank(obanks[0]), rec, ALU.mult), [t_r])
                t_g = S.op("pool", f_tt(mixA_t[:, h, tsl], on, gaT[:, h, tsl], ALU.mult), [t_o])
                ep_prev[r] = [t_o, t_g]
                return [[t_o]], [[t_r]], None

            def qparts_A(h, m):
                kr = KrLo if h % 2 == 0 else KrHi
                return [(KnT[:, h, :], QnT[:, h, :]), (kr, QrT[:, h // 2, :])]

            attention(1, qparts_A, lambda h, kb: Va[:, kb, h * 128:(h + 1) * 128], MLA_SCALE,
                      s_banks=[0, 1, 2, 7], o_banks=[[5], [3]], sum_banks=[[6], [4]], epilogue=epilogue_A, look=3)

            stop_at(4)
            if debug:
                S.barrier()
                dve_tail = None
            else:
                toks_ = [S.last_token(e_) for e_ in ("pe", "act", "dve", "pool")]
                for e_ in ("act", "dve", "pool", "sp"):
                    S.wait(e_, toks_)
                S.wait("pe", [toks_[1]])
                dve_tail = toks_[2]
            dump("mixA", mixA_t[:], [128, 4, S_LEN], BF16)

            NR[0] = 6
            del xbf[:], t1[:], t2[:], rope_free[:]
            for i_ in range(6):
                xbf.append(tv(i_ * 5120, 1024))
                t1.append(tv(i_ * 5120 + 1024, 2048, F32))
                t2.append(tv(i_ * 5120 + 3072, 2048, F32))
                rope_free.append((None, None))
            rope_i[0] = 0
            RG['acc'] = Ring([bank(2), bank(0), bank(3), bank(4)])
            RG['aux'] = Ring([bank(7), bank(1), bank(5), bank(6)])
            for rg_ in (RG['acc'], RG['aux']):
                rg_.free[2] = [dve_tail]
                rg_.free[3] = [dve_tail]
            for ci in range(4):
                for tb in range(NB):
                    a_s, acc, tk = proj_fm(1, ci, tb)
                    evac_rope(a_s, acc, tk, permB, tb, [((0, 128), dqT[:, ci, tb * 512:(tb + 1) * 512])])
            flush_deferred()
            load_wgroup(5, 1)
            for ci in range(4):
                for tb in range(NB):
                    a_s, acc, tk = proj_fm(0, ci, tb)
                    tsl = slice(tb * 512, (tb + 1) * 512)
                    evac_rope(a_s, acc, tk, permB, tb, [((0, 64), dkLo[0:64, ci, tsl]), ((64, 128), dkHi[64:128, ci, tsl])])
                    S.op("act", f_act(dkLo[64:128, ci, tsl], mask512[0][64:128, :], AF.Copy, scale=0.0), [t_c])
                    S.op("act", f_act(dkHi[0:64, ci, tsl], mask512[0][0:64, :], AF.Copy, scale=0.0), [t_c])
            flush_deferred()
            load_wgroup(6, 0)
            for ci in range(4):
                for tb in range(NB):
                    a_s, acc, tk = proj_fm(1, ci, tb)
                    t_e = S.op("act", f_act(gbT[:, ci, tb * 512:(tb + 1) * 512], acc, AF.Silu), [tk])
                    RG['acc'].release(a_s, [t_e])
            for t in range(NT):
                a_s, acc, adeps = RG['acc'].acquire()
                tk = None
                for k in range(8):
                    tk = S.op("pe", f_mm(acc, hT[:, k, t * 128:(t + 1) * 128], W_t[:, 0, k, :], k == 0, k == 7),
                              [w_ready[0]] + adeps, signal=(k == 7))
                w_free[0].append(tk)
                t_e = evac_copy(Vb[:, t, :], acc, [tk])
                RG['acc'].release(a_s, [t_e])
            wout = W_t[:].rearrange("p s k f -> p (s k f)").rearrange("p (k f) -> p k f", k=8)
            t_wout = S.dma("pool", d_w[0], wout, wo_d, deps=w_free[0] + w_free[1])

            stop_at(5)
            S.barrier()
            dump("AB2", AB_t[:], [128, 40960], BF16)

            ep_prev[0] = []
            ep_prev[1] = []

            QB = 256
            NQ = S_LEN // QB
            s_ring = Ring([bank(0), bank(1), bank(2), bank(3)])
            pt_ring = Ring(PT)
            o_bk = [4, 5]
            sum_bk = [6, 7]
            o_free = [[], []]
            sum_free = [[], []]
            ssq_free = [None]
            tilesB = []
            blk = 0
            for h in range(4):
                for jq in (0, 7, 1, 6, 2, 5, 3, 4):
                    for kb in range(2 * jq + 2):
                        tilesB.append((h, jq, kb, blk % 2))
                    blk += 1
            pendB = {}
            deferredB = []

            EB0 = 6144
            p1_prev = [[], []]
            dq_prev = [[], [], [], []]
            p2_prev = [[], []]
            blkB = [0]
            p2_i = [0]

            def epilogue_B(h, jq, par, t_last):
                n = blkB[0]
                blkB[0] += 1
                Erec = tv(EB0 + par * 4096, 2048, F32)
                Eo = tv(EB0 + par * 4096 + 2048, 2048, F32)
                k4 = n % 4
                Ed = tv(EB0 + 8192 + k4 * 1536, 1024, F32)
                Eq = tv(EB0 + 8192 + k4 * 1536 + 1024, 512)
                qsl = slice(jq * QB, (jq + 1) * QB)
                if jq in (7, 6, 4):
                    t_ln = S.op("act", f_act(Erec, bank(sum_bk[par]), AF.Ln), [t_last] + p1_prev[par])
                    t_rc = S.op("act", f_act(Erec, Erec, AF.Exp, scale=-1.0), [t_ln])
                else:
                    t_ln = S.op("dve", f_recip(Erec, bank(sum_bk[par])), [t_last] + p1_prev[par])
                    t_rc = t_ln
                t_o = S.op("dve", f_tt(Eo, bank(o_bk[par]), Erec, ALU.mult), [t_rc, t_last] + p1_prev[par])
                t_d = S.op("dve", f_stt(Ed, Eo[:, QB:2 * QB], neglam, Eo[:, 0:QB], ALU.mult, ALU.add),
                           [t_o, t_nl] + dq_prev[k4])
                p1_prev[par] = [t_d]
                t_q = S.op("pool", f_tt(Eq, Ed, Ed, ALU.mult), [t_d] + dq_prev[k4])

                def part2():
                    r2 = p2_i[0] % 2
                    p2_i[0] += 1
                    Er = tv(EB0 + 14336 + r2 * 2048, 1024, F32)
                    Eob = tv(EB0 + 14336 + r2 * 2048 + 1024, 1024, F32)
                    q_s, qbk, qdeps = s_ring.acquire()
                    sq = qbk[:, 0:QB]
                    t_s = S.op("pe", f_mm(sq, onesb, Eq), [t_q] + qdeps)
                    t_m = S.op("act", f_act(Er, sq, AF.Ln, scale=1.0 / 128, bias=epsB), [t_s, t_eps] + p2_prev[r2])
                    s_ring.release(q_s, [t_m])
                    t_p = S.op("act", f_act(Er, Er, AF.Exp, scale=-0.5), [t_m])
                    t_b = S.op("dve", f_stt(Eob, Ed, sgcol, Er, ALU.mult, ALU.mult), [t_p, t_d, t_sg] + p2_prev[r2])
                    t_g = S.op("pool", f_tt(mixB[:, h, qsl], Eob, gbT[:, h, qsl], ALU.mult), [t_b])
                    p2_prev[r2] = [t_b, t_g]
                    dq_prev[k4] = [t_b, t_s]

                return [t_o], [t_ln], part2

            def issue_SB(idx):
                h, jq, kb, par = tilesB[idx]
                il = kb - 2 * jq
                s_s, sv, sdeps = s_ring.acquire()
                qv = dqT[:, h, jq * QB:(jq + 1) * QB]
                ksl = slice(kb * 128, (kb + 1) * 128)
                live = None
                if il == 1:
                    live = lambda ap: ap.rearrange("p (m c) -> p m c", m=2)[:, :, 128:256]
                    qv1 = dqT[:, h, jq * QB + 128:(jq + 1) * QB]
                    S.op("pe", f_mm(sv[:, 128:2 * QB], identb, tri3, True, False), sdeps, signal=False)
                    S.op("pe", f_mm(sv[:, 128:QB], dkLo[:, h, ksl], qv1, False, False), [], signal=False)
                    tk = S.op("pe", f_mm(sv[:, QB + 128:2 * QB], dkHi[:, h, ksl], qv1, False, True), [])
                elif il == 0:
                    S.op("pe", f_mm(sv, identb, mask512[il], True, False), sdeps, signal=False)
                    S.op("pe", f_mm(sv[:, 0:QB], dkLo[:, h, ksl], qv, False, False), [], signal=False)
                    tk = S.op("pe", f_mm(sv[:, QB:2 * QB], dkHi[:, h, ksl], qv, False, True), [])
                else:
                    S.op("pe", f_mm(sv[:, 0:QB], dkLo[:, h, ksl], qv, True, True), sdeps, signal=False)
                    tk = S.op("pe", f_mm(sv[:, QB:2 * QB], dkHi[:, h, ksl], qv, True, True), [])
                p_s, pv, pdeps = pt_ring.acquire()
                if live is not None:
                    t_e = S.op("act", f_act(live(pv), live(sv), AF.Exp, scale=DIFF_SCALE), [tk] + pdeps)
                else:
                    t_e = S.op("act", f_act(pv, sv, AF.Exp, scale=DIFF_SCALE), [tk] + pdeps)
                s_ring.release(s_s, [t_e])
                pendB[idx] = (p_s, pv, t_e, live)

            def issue_PVB(idx):
                h, jq, kb, par = tilesB[idx]
                p_s, pv, t_e, live = pendB.pop(idx)
                first = (kb == 0)
                last = (kb == 2 * jq + 1)
                ob_, sb_ = bank(o_bk[par]), bank(sum_bk[par])
                vh = Vb[:, kb, h * 128:(h + 1) * 128]
                if live is not None:
                    assert last and not first
                    l0, l1 = slice(128, QB), slice(QB + 128, 2 * QB)
                    S.op("pe", f_mm(ob_[:, l0], vh, pv[:, l0], False, False), [t_e], signal=False)
                    S.op("pe", f_mm(ob_[:, l1], vh, pv[:, l1], False, True), [], signal=False)
                    S.op("pe", f_mm(sb_[:, l0], onesb, pv[:, l0], False, False), [], signal=False)
                    tk = S.op("pe", f_mm(sb_[:, l1], onesb, pv[:, l1], False, True), [])
                else:
                    S.op("pe", f_mm(ob_, vh, pv, first, last),
                         [t_e] + (o_free[par] if first else []), signal=False)
                    tk = S.op("pe", f_mm(sb_, onesb, pv, first, last), (sum_free[par] if first else []))
                pt_ring.release(p_s, [tk])
                if last:
                    fo, fs, dfn = epilogue_B(h, jq, par, tk)
                    o_free[par] = fo
                    sum_free[par] = fs
                    deferredB.append([12, dfn])

            nB = len(tilesB)
            LOOK = 3
            for i in range(min(LOOK, nB)):
                issue_SB(i)
            for i in range(nB):
                issue_PVB(i)
                for d_ in deferredB:
                    d_[0] -= 1
                while deferredB and deferredB[0][0] <= 0:
                    deferredB.pop(0)[1]()
                if i + LOOK < nB:
                    issue_SB(i + LOOK)
            while deferredB:
                deferredB.pop(0)[1]()

            stop_at(6)
            S.barrier()
            dump("mixB", mixB, [128, 4, S_LEN], BF16)

            t_gpost = S.dma("sp", d_c2, gpost, gpost_d.partition_broadcast(128))
            pair_ring = Ring([psp[0], psp[1], psp[2], psp[3]])
            xs_free5 = [None, None, None]
            y_free = [None, None, None]
            out_toks = []
            gv = gpost.rearrange("p (a f) -> p a f", a=2)
            st5 = {}

            def p5_A(t):
                s = t % 3
                tx = S.dma("sp", d_xs5[s], xs5[s], x_d[t * 128:(t + 1) * 128, :], deps=[xs_free5[s]])
                p_s, pp, pdeps = pair_ring.acquire()
                tk = None
                for half in range(2):
                    for c in range(8):
                        src_ = mixA_t[:, c, t * 128:(t + 1) * 128] if c < 4 else mixB[:, c - 4, t * 128:(t + 1) * 128]
                        tk = S.op("pe", f_mm(pp[:, half, :], src_, wout[:, c, half * 512:(half + 1) * 512], c == 0, c == 7),
                                  [t_wout] + pdeps, signal=(half == 1 and c == 7))
                t_q0 = S.op("act", f_act(junk5[:, 0:512], pp[:, 0, :], AF.Square, accum_out=ssq5[:, 2 * t:2 * t + 1]), [tk])
                t_q1 = S.op("act", f_act(junk5[:, 512:1024], pp[:, 1, :], AF.Square,
                                         accum_out=ssq5[:, 2 * t + 1:2 * t + 2]), [tk, t_q0])
                t_m = S.op("dve", f_stt(ms5[:, t:t + 1], ssq5[:, 2 * t:2 * t + 1], 1.0, ssq5[:, 2 * t + 1:2 * t + 2],
                                        ALU.mult, ALU.add), [t_q0, t_q1])
                t_m2 = S.op("dve", f_ts(rstd5[:, t:t + 1], ms5[:, t:t + 1], 1.0 / D, NORM_EPS, ALU.mult, ALU.add), [t_m])
                t_r = S.op("pool", f_tt(ms5[:, t:t + 1], rstd5[:, t:t + 1], mhcol, ALU.pow), [t_m2, t_mh2])
                st5[t] = (tx, p_s, pp, t_r)

            def p5_B(t):
                s = t % 3
                tx, p_s, pp, t_r = st5.pop(t)
                yv = ybuf[s].rearrange("p (a f) -> p a f", a=2)
                t_y = None
                for half in range(2):
                    t_y = S.op("act", f_act(yv[:, half, :], pp[:, half, :], AF.Identity, scale=ms5[:, t:t + 1]),
                               [t_r, y_free[s]])
                pair_ring.release(p_s, [t_y])
                t_w = S.op("dve", f_tt(ybuf[s], ybuf[s], gpost, ALU.mult), [t_y, t_gpost])
                t_a = S.op("dve", f_tt(ybuf[s], ybuf[s], xs5[s], ALU.add), [t_w, tx])
                xs_free5[s] = t_a
                t_o = S.dma("sp", d_out[s], out_d[t * 128:(t + 1) * 128, :], ybuf[s], deps=[t_a])
                y_free[s] = t_o
                out_toks.append(t_o)

            p5_A(0)
            for t in range(NT):
                if t + 1 < NT:
                    p5_A(t + 1)
                p5_B(t)
            S.barrier(out_toks[-3:])
            S.emit()
        except _Stop:
            pass
    if debug:
        return nc, dbg_outs
    return nc


_CACHE = {}


def _prep_inputs(inputs):
    x = np.asarray(inputs["x"], dtype=np.float32)
    wl, wo, wkv = _layout_weights(np.asarray(inputs["w_in"]), np.asarray(inputs["w_uk"]),
                                  np.asarray(inputs["w_uv"]), np.asarray(inputs["w_out"]))
    cmat, tabs = _host_consts()
    gpre = np.ascontiguousarray(np.asarray(inputs["ln_pre_g"], dtype=np.float32).reshape(1, D))
    gpost = np.ascontiguousarray(np.asarray(inputs["ln_post_g"], dtype=np.float32).reshape(1, D))
    cols = np.ascontiguousarray(np.stack([np.asarray(inputs["kv_norm_g"], dtype=np.float32).reshape(128),
                                          np.asarray(inputs["subln_g"], dtype=np.float32).reshape(128)], axis=1))
    lamv = np.ascontiguousarray(np.concatenate([
        np.asarray(inputs["lambda_q1"], dtype=np.float32).reshape(64),
        np.asarray(inputs["lambda_k1"], dtype=np.float32).reshape(64),
        np.asarray(inputs["lambda_q2"], dtype=np.float32).reshape(64),
        np.asarray(inputs["lambda_k2"], dtype=np.float32).reshape(64)]).reshape(1, 256))
    shared = {"wl": wl, "wo": wo, "wkv": wkv, "cmat": cmat, "tabs": tabs, "gpre": gpre, "gpost": gpost,
              "cols": cols, "lamv": lamv}
    return x, shared


def kernel(**inputs):
    x, shared = _prep_inputs(inputs)
    B = x.shape[0]
    if "nc" not in _CACHE:
        _CACHE["nc"] = build_program()
    nc = _CACHE["nc"]
    in_maps = [dict(shared, x=np.ascontiguousarray(x[b])) for b in range(B)]
    res = run_bass_kernel_spmd(nc, in_maps, core_ids=list(range(B)))
    out = np.stack([np.asarray(r["out"], dtype=np.float32) for r in res.results], axis=0)
    return out
```

```python
import math
from contextlib import ExitStack

import numpy as np

import concourse.bass as bass
import concourse.mybir as mybir
from concourse.bass_utils import run_bass_kernel_spmd

F32 = mybir.dt.float32
BF16 = mybir.dt.bfloat16
AF = mybir.ActivationFunctionType
ALU = mybir.AluOpType

S_LEN = 2048
D = 1024
NT = S_LEN // 128
NB = S_LEN // 512
ROPE_THETA = 500000.0
NORM_EPS = 1e-6
SUBLN_EPS = 1e-5
MLA_SCALE = 1.0 / math.sqrt(128 + 64)
DIFF_SCALE = 1.0 / math.sqrt(64)
LAMBDA_INIT = 0.8 - 0.6 * math.exp(-0.3 * 0)
MASK_NEG = -30000.0

ENGS = ["pe", "act", "dve", "pool", "sp"]


class Sched:
    def __init__(self, nc, ctx):
        self.nc = nc
        self.ctx = ctx
        self.streams = {e: [] for e in ENGS}
        self.sem = {e: ctx.enter_context(nc.semaphore("s_" + e)) for e in ENGS if e != "sp"}
        self.cnt = {e: 0 for e in ENGS}
        self.waited = {e: {} for e in ENGS}
        self.nd = 0

    def dma_sem(self):
        self.nd += 1
        return [self.ctx.enter_context(self.nc.semaphore(f"d{self.nd}")), 0]

    def _waits(self, eng, deps):
        ws = []
        for d in deps:
            if d is None:
                continue
            if isinstance(d, list):
                ws += self._waits(eng, d)
                continue
            sem, val = d
            k = id(sem)
            if self.waited[eng].get(k, 0) >= val:
                continue
            self.waited[eng][k] = val
            ws.append((sem, val))
        return ws

    def op(self, eng, fn, deps=(), signal=True):
        ws = self._waits(eng, deps)
        ent = [ws, fn, None]
        self.streams[eng].append(ent)
        if signal:
            self.cnt[eng] += 1
            ent[2] = (self.sem[eng], 1)
            return (self.sem[eng], self.cnt[eng])
        return None

    def last_token(self, eng):
        st = self.streams[eng]
        for ent in reversed(st):
            if ent[1] is None:
                continue
            if ent[2] is None:
                self.cnt[eng] += 1
                ent[2] = (self.sem[eng], 1)
            elif ent[2][0] is not self.sem[eng]:
                continue
            break
        if self.cnt[eng] == 0:
            return None
        return (self.sem[eng], self.cnt[eng])

    def dma(self, queue, dsem, out, in_, deps=()):
        ws = self._waits(queue, deps)
        dsem[1] += 16
        self.streams[queue].append([ws, lambda e: e.dma_start(out=out, in_=in_), (dsem[0], 16)])
        return (dsem[0], dsem[1])

    def wait(self, eng, deps):
        ws = self._waits(eng, deps)
        if ws:
            self.streams[eng].append([ws, None, None])

    def barrier(self, extra=()):
        toks = [self.last_token(e) for e in ("pe", "act", "dve", "pool")]
        toks = [t for t in toks if t is not None] + list(extra)
        for e in ENGS:
            self.wait(e, toks)
        return toks

    def emit(self):
        with self.nc.Block() as block:
            def run(name):
                def f(e):
                    embed_ok = name in ("act", "dve", "pool")
                    for ws, fn, inc in self.streams[name]:
                        ws = list(ws)
                        emb = None
                        if fn is not None and ws and embed_ok and not getattr(fn, "no_embed", False) \
                                and (inc is None or inc[0] is self.sem.get(name)):
                            emb = ws.pop()
                        for sem, val in ws:
                            e.wait_ge(sem, val)
                        if fn is not None:
                            ins = fn(e)
                            if emb is not None:
                                ins._wait_ge(emb[0], emb[1])
                            if inc is not None:
                                ins.then_inc(inc[0], inc[1])
                return f
            block.sync(run("sp"))
            block.tensor(run("pe"))
            block.scalar(run("act"))
            block.vector(run("dve"))
            block.gpsimd(run("pool"))


def f_mm(out, lhsT, rhs, start=True, stop=True):
    return lambda e: e.matmul(out, lhsT=lhsT, rhs=rhs, start=start, stop=stop)


def f_tr(out, in_, ident):
    return lambda e: e.transpose(out=out, in_=in_, identity=ident)


def f_act(out, in_, func, **kw):
    fn = lambda e: e.activation(out=out, in_=in_, func=func, **kw)
    if "accum_out" in kw:
        fn.no_embed = True
    return fn


def f_acopy(out, in_):
    return lambda e: e.copy(out=out, in_=in_)


def f_copy(out, in_):
    return lambda e: e.tensor_copy(out=out, in_=in_)


def f_tt(out, in0, in1, op):
    return lambda e: e.tensor_tensor(out=out, in0=in0, in1=in1, op=op)


def f_ts(out, in0, s1, s2, op0, op1=None):
    if op1 is None:
        return lambda e: e.tensor_scalar(out=out, in0=in0, scalar1=s1, scalar2=None, op0=op0)
    return lambda e: e.tensor_scalar(out=out, in0=in0, scalar1=s1, scalar2=s2, op0=op0, op1=op1)


def f_stt(out, in0, scalar, in1, op0, op1):
    return lambda e: e.scalar_tensor_tensor(out=out, in0=in0, scalar=scalar, in1=in1, op0=op0, op1=op1)


def f_recip(out, in_):
    return lambda e: e.reciprocal(out=out, in_=in_)


def f_memset(ap, v):
    return lambda e: e.memset(ap, v)


class Ring:
    def __init__(self, views):
        self.views = views
        self.free = [[] for _ in views]
        self.i = 0

    def acquire(self):
        s = self.i % len(self.views)
        self.i += 1
        deps = self.free[s]
        self.free[s] = []
        return s, self.views[s], deps

    def release(self, s, toks):
        self.free[s] = [t for t in toks if t is not None]


def _rope_tables():
    t = np.arange(S_LEN, dtype=np.float32)

    def tab(dim):
        inv = (np.float32(ROPE_THETA) ** (-(np.arange(0, dim, 2, dtype=np.float32)) / np.float32(dim))).astype(np.float32)
        ang = (t[:, None] * inv[None, :]).astype(np.float32)
        return np.cos(ang.astype(np.float64)).astype(np.float32), np.sin(ang.astype(np.float64)).astype(np.float32)

    ca, sa = tab(64)
    cb, sb = tab(16)
    CA = np.zeros((128, S_LEN), np.float32)
    SA = np.zeros((128, S_LEN), np.float32)
    CB = np.ones((128, S_LEN), np.float32)
    SB = np.zeros((128, S_LEN), np.float32)
    PA = np.zeros((128, 128), np.float32)
    PB = np.zeros((128, 128), np.float32)
    for r in range(128):
        d = r % 64
        i = d % 32
        CA[r] = ca[:, i]
        if d < 32:
            SA[r] = -sa[:, i]
            PA[r + 32, r] = 1.0
        else:
            SA[r] = sa[:, i]
            PA[r - 32, r] = 1.0
        if d < 8:
            CB[r] = cb[:, d]
            SB[r] = -sb[:, d]
            PB[r + 8, r] = 1.0
        elif d < 16:
            CB[r] = cb[:, d - 8]
            SB[r] = sb[:, d - 8]
            PB[r - 8, r] = 1.0
    return CA, SA, CB, SB, PA, PB


def _host_consts():
    CA, SA, CB, SB, PA, PB = _rope_tables()
    ident = np.eye(128, dtype=np.float32)
    kk = np.arange(128)[:, None]
    qq = np.arange(128)[None, :]
    mask = np.where(qq >= kk, 0.0, MASK_NEG).astype(np.float32)
    zero = np.zeros((128, 128), np.float32)
    neg = np.full((128, 128), MASK_NEG, np.float32)
    cmat = np.stack([ident, PA, np.ones((128, 128), np.float32), PB, mask,
                     mask, zero, mask, zero,
                     neg, mask, neg, mask],
                    axis=1)
    tabs = np.stack([CA, SA, CB, SB], axis=0)
    return np.ascontiguousarray(cmat), np.ascontiguousarray(tabs)


def _layout_weights(w_in, w_uk, w_uv, w_out):
    w = w_in[0]
    q_nope = w[:, 0:512]
    q_rope = w[:, 512:768]
    c_kv = w[:, 768:896]
    k_rope = w[:, 896:960]
    g_a = w[:, 960:1472]
    dq = w[:, 1472:1984]
    dk = w[:, 1984:2496]
    dv = w[:, 2496:3008]
    g_b = w[:, 3008:3520]
    groups = [
        np.concatenate([q_rope, c_kv, k_rope, k_rope], axis=1),
        q_nope,
        g_a,
        dq,
        dk,
        g_b,
        dv,
    ]
    wl = np.stack([g.reshape(8, 128, 512).transpose(1, 0, 2) for g in groups], axis=0)
    wo = w_out[0].reshape(8, 128, 1024).transpose(1, 0, 2)
    wkv = np.stack([w_uk[0], w_uv[0]], axis=1)
    return (np.ascontiguousarray(wl, dtype=np.float32), np.ascontiguousarray(wo, dtype=np.float32),
            np.ascontiguousarray(wkv, dtype=np.float32))


def build_program(debug=False, stop=None):
    nc = bass.Bass("TRN2", target_bir_lowering=False)
    x_d = nc.dram_tensor("x", [S_LEN, D], F32, kind="ExternalInput").ap()
    wl_d = nc.dram_tensor("wl", [7, 128, 8, 512], F32, kind="ExternalInput").ap()
    wo_d = nc.dram_tensor("wo", [128, 8, 1024], F32, kind="ExternalInput").ap()
    wkv_d = nc.dram_tensor("wkv", [128, 2, 512], F32, kind="ExternalInput").ap()
    cmat_d = nc.dram_tensor("cmat", [128, 13, 128], F32, kind="ExternalInput").ap()
    tabs_d = nc.dram_tensor("tabs", [4, 128, S_LEN], F32, kind="ExternalInput").ap()
    gpre_d = nc.dram_tensor("gpre", [1, D], F32, kind="ExternalInput").ap()
    gpost_d = nc.dram_tensor("gpost", [1, D], F32, kind="ExternalInput").ap()
    cols_d = nc.dram_tensor("cols", [128, 2], F32, kind="ExternalInput").ap()
    lamv_d = nc.dram_tensor("lamv", [1, 256], F32, kind="ExternalInput").ap()
    out_d = nc.dram_tensor("out", [S_LEN, D], F32, kind="ExternalOutput").ap()
    dbg_outs = {}

    ctx = ExitStack()
    with ctx:
        S = Sched(nc, ctx)

        def sb(name, shape, dt):
            return nc.alloc_sbuf_tensor(name, shape, dt)

        class _Stop(Exception):
            pass

        def stop_at(k):
            if stop is not None and stop == k:
                S.barrier()
                S.emit()
                raise _Stop()

        hT_t = sb("hT", [128, 8, S_LEN], BF16)
        AB_t = sb("AB", [128, 40960], BF16)
        mixA_t = sb("mixA", [128, 4, S_LEN], BF16)
        W_t = sb("W", [128, 2, 8, 512], BF16)
        tab_t = sb("tab", [128, 2, S_LEN], F32)
        T_t = sb("T", [128, 17408], BF16)
        cm_t = sb("cm", [128, 13, 128], BF16)
        small = sb("small", [128, 128], F32)
        lam_t = sb("lamt", [128, 320], F32)
        xs2_t = sb("xs2", [128, 2, 1024], F32)

        identb = cm_t[:, 0, :]
        permA = cm_t[:, 1, :]
        onesb = cm_t[:, 2, :]
        permB = cm_t[:, 3, :]
        maskb = cm_t[:, 4, :]
        tri3 = cm_t[:, 10:13, :].rearrange("p a b -> p (a b)")
        mask512 = [cm_t[:, 5:9, :].rearrange("p a b -> p (a b)"), cm_t[:, 9:13, :].rearrange("p a b -> p (a b)")]

        hT = hT_t
        mixB = hT_t[:, 0:4, :]

        def ab(off_k, n_k):
            return AB_t[:, off_k * 512:(off_k + n_k) * 512]

        QnT = ab(0, 16).rearrange("p (h t) -> p h t", h=4)
        QrT = ab(16, 8).rearrange("p (c t) -> p c t", c=2)
        KnT = ab(24, 16).rearrange("p (h t) -> p h t", h=4)
        KrLo = ab(40, 4)
        KrHi = ab(44, 4)
        Va = ab(48, 16).rearrange("p (k f) -> p k f", k=16)
        gaT = ab(64, 16).rearrange("p (h t) -> p h t", h=4)
        dqT = ab(0, 16).rearrange("p (h t) -> p h t", h=4)
        dkLo = ab(16, 16).rearrange("p (h t) -> p h t", h=4)
        dkHi = ab(32, 16).rearrange("p (h t) -> p h t", h=4)
        Vb = ab(48, 16).rearrange("p (k f) -> p k f", k=16)
        gbT = ab(64, 16).rearrange("p (h t) -> p h t", h=4)

        def tv(off_b, n_b, dt=BF16):
            v = T_t[:, off_b // 2:(off_b + n_b) // 2]
            if dt == F32:
                v = v.bitcast(F32)
            return v

        xs = [tv(0, 4096, F32), tv(4096, 4096, F32), xs2_t[:, 0, :], xs2_t[:, 1, :]]
        xn = [tv(8192, 2048), tv(10240, 2048)]
        junk = tv(12288, 2048)
        xbf = [tv(14336, 1024), tv(15360, 1024)]
        t1 = [tv(16384, 2048, F32), tv(18432, 2048, F32)]
        t2 = [tv(20480, 2048, F32), tv(22528, 2048, F32)]
        ckvnT = tv(24576, 4096)
        wkv = tv(28672, 2048).rearrange("p (a f) -> p a f", a=2)
        gpre = tv(30720, 4096, F32)
        PT = [tv(i * 1024, 1024) for i in range(6)]
        ep = [tv(6144 + i * 2048, 2048, F32) for i in range(12)]
        dsq = [tv(30720, 1024), tv(31744, 1024)]
        xs5 = [tv(i * 4096, 4096, F32) for i in range(3)]
        ybuf = [tv(12288 + i * 4096, 4096, F32) for i in range(3)]
        gpost = tv(24576, 4096, F32)
        junk5 = tv(28672, 2048)

        ssq = small[:, 0:16]
        ms = small[:, 16:32]
        rstd = small[:, 32:48]
        ssq5 = small[:, 48:80]
        ms5 = small[:, 80:96]
        rstd5 = small[:, 96:112]
        cols = small[:, 112:114]
        sgcol = small[:, 114:115]
        lsum = small[:, 115:117]
        lexp = small[:, 117:119]
        neglam = small[:, 119:120]
        mhcol = small[:, 120:121]
        epsA = small[:, 121:122]
        epsB = small[:, 122:123]

        psp = [nc.alloc_psum_tensor(f"psp{i}", [128, 2, 512], F32) for i in range(4)]

        def bank(b):
            return psp[b // 2][:, b % 2, :]

        d_xs = [S.dma_sem() for _ in range(4)]
        d_w = [S.dma_sem(), S.dma_sem()]
        d_tab = S.dma_sem()
        d_c = S.dma_sem()
        d_c2 = S.dma_sem()
        d_out = [S.dma_sem(), S.dma_sem(), S.dma_sem()]
        d_xs5 = [S.dma_sem(), S.dma_sem(), S.dma_sem()]
        d_dbg = S.dma_sem()

        try:
            x_tok = {}
            x_tok[0] = S.dma("sp", d_xs[0], xs[0], x_d[0:128, :])
            t_c = S.dma("pool", d_c, cm_t[:, 0:3, :], cmat_d[:, 0:3, :])
            S.dma("sp", d_c2, gpre, gpre_d.partition_broadcast(128))
            S.dma("sp", d_c2, cols, cols_d)
            t_c2 = S.dma("sp", d_c2, lam_t[:, 0:256], lamv_d.partition_broadcast(128))
            t_mh2 = S.op("pool", f_memset(mhcol, -0.5))
            S.op("pool", f_memset(epsA, NORM_EPS))
            t_eps = S.op("pool", f_memset(epsB, SUBLN_EPS))
            stop_at(0)

            pT_ring = Ring([bank(0), bank(1)])
            RG = {'acc': Ring([bank(2), bank(3), bank(4), bank(5)]), 'aux': Ring([bank(6), bank(7)])}
            xs_free = [None] * 4
            xn_free = [None, None]
            hT_ready = [None] * NT
            w_free = [[], []]
            w_ready = [None, None]
            flip = [0]
            junk_free = [None]

            def load_wgroup(g, slot, extra=()):
                w_ready[slot] = S.dma("pool", d_w[slot], W_t[:, slot], wl_d[g], deps=list(w_free[slot]) + list(extra))
                w_free[slot] = []

            xn_tok = {}

            def p0_A(t):
                s = t % 2
                xq = t % 4
                tx = x_tok.pop(t) if t in x_tok else S.dma("sp", d_xs[xq], xs[xq], x_d[t * 128:(t + 1) * 128, :],
                                                           deps=[xs_free[xq]])
                t_ssq = S.op("act", f_act(junk, xs[xq], AF.Square, accum_out=ssq[:, t:t + 1]), [tx, junk_free[0]])
                junk_free[0] = t_ssq
                t_ms = S.op("dve", f_ts(ms[:, t:t + 1], ssq[:, t:t + 1], 1.0 / D, NORM_EPS, ALU.mult, ALU.add), [t_ssq])
                t_rs = S.op("pool", f_tt(rstd[:, t:t + 1], ms[:, t:t + 1], mhcol, ALU.pow), [t_ms, t_mh2])
                t_xn = S.op("dve", f_stt(xn[s], xs[xq], rstd[:, t:t + 1], gpre, ALU.mult, ALU.mult),
                            [t_rs, t_c2, tx, xn_free[s]])
                xs_free[xq] = t_xn
                xn_tok[t] = t_xn

            def p0_B(t):
                s = t % 2
                t_xn = xn_tok.pop(t)
                ps_, pv, pdeps = pT_ring.acquire()
                pTv = pv.bitcast(BF16).rearrange("p (c t) -> p c t", c=8)
                t_tr = None
                for c in range(8):
                    t_tr = S.op("pe", f_tr(pTv[:, c, :], xn[s][:, c * 128:(c + 1) * 128], identb),
                                [t_xn, t_c] + pdeps, signal=(c == 7))
                xn_free[s] = t_tr
                dst = hT[:, :, t * 128:(t + 1) * 128]
                if t % 4 != 3:
                    t_h = S.op("act", f_acopy(dst, pTv), [t_tr])
                else:
                    t_h = S.op("dve", f_copy(dst, pTv), [t_tr])
                pT_ring.release(ps_, [t_h])
                hT_ready[t] = t_h

            p0_steps = [("A", 0)]
            for t in range(NT):
                if t + 1 < NT:
                    p0_steps.append(("A", t + 1))
                p0_steps.append(("B", t))

            def p0_run(until_tile=None, nsteps=None):
                n = 0
                while p0_steps:
                    if nsteps is not None and n >= nsteps:
                        break
                    if until_tile is not None and hT_ready[until_tile] is not None:
                        break
                    kind, t = p0_steps.pop(0)
                    (p0_A if kind == "A" else p0_B)(t)
                    n += 1

            deferred_pe = []

            def run_deferred():
                todo = list(deferred_pe)
                del deferred_pe[:]
                for fn_ in todo:
                    fn_()

            def flush_deferred():
                while deferred_pe:
                    run_deferred()

            def proj_fm(slot, ci, tb):
                a_s, acc, adeps = RG['acc'].acquire()
                tk = None
                for k in range(8):
                    tk = S.op("pe", f_mm(acc, W_t[:, slot, k, ci * 128:(ci + 1) * 128],
                                         hT[:, k, tb * 512:(tb + 1) * 512], k == 0, k == 7),
                              [w_ready[slot]] + hT_ready[4 * tb:4 * tb + 4] + adeps, signal=(k == 7))
                w_free[slot].append(tk)
                run_deferred()
                return a_s, acc, tk

            def evac_copy(dst, src, deps):
                flip[0] ^= 1
                if flip[0]:
                    return S.op("act", f_acopy(dst, src), deps)
                return S.op("dve", f_copy(dst, src), deps)

            rope_i = [0]
            rope_free = [(None, None), (None, None)]
            NR = [2]
            tab_ready = [None]
            tab_users = []

            def evac_rope(acc_s, acc, t_mm, perm, tb, dsts):
                r = rope_i[0] % NR[0]
                rope_i[0] += 1
                tsl = slice(tb * 512, (tb + 1) * 512)
                t_xb = S.op("act", f_acopy(xbf[r], acc), [t_mm, rope_free[r][0]])
                t_a = S.op("dve", f_tt(t1[r], acc, tab_t[:, 0, tsl], ALU.mult), [t_mm, t_xb, tab_ready[0], rope_free[r][1]])
                RG['acc'].release(acc_s, [t_xb, t_a])
                old_free = rope_free[r][1]
                tab_r = tab_ready[0]

                def part2():
                    p_s, pr, pdeps = RG['aux'].acquire()
                    t_pm = S.op("pe", f_mm(pr, perm, xbf[r]), [t_xb, t_c] + pdeps)
                    t_b = S.op("dve", f_tt(t2[r], pr, tab_t[:, 1, tsl], ALU.mult), [t_pm, tab_r, old_free])
                    RG['aux'].release(p_s, [t_b])
                    t_c_ = []
                    for (p0, p1), dst in dsts:
                        eng_ = "dve" if p0 == 64 else "pool"
                        t_c_.append(S.op(eng_, f_tt(dst, t1[r][p0:p1, :], t2[r][p0:p1, :], ALU.add), [t_a, t_b]))
                    rope_free[r] = (t_pm, t_c_)
                    tab_users.append(t_b)

                deferred_pe.append(part2)

            def load_tabs(which, extra=()):
                deps = list(tab_users) + list(extra)
                del tab_users[:]
                S.dma("sp", d_tab, tab_t[:, 0, :], tabs_d[2 * which], deps=deps)
                tab_ready[0] = S.dma("sp", d_tab, tab_t[:, 1, :], tabs_d[2 * which + 1], deps=deps)

            load_wgroup(0, 0)
            d_cB = S.dma_sem()
            t_cB = S.dma("pool", d_cB, cm_t[:, 3:13, :], cmat_d[:, 3:13, :])
            for t in range(1, 4):
                x_tok[t] = S.dma("sp", d_xs[t], xs[t], x_d[t * 128:(t + 1) * 128, :])
            load_tabs(0, extra=[x_tok[3], w_ready[0]])
            S.op("pool", f_memset(KrLo[64:128, :], 0.0))
            S.op("pool", f_memset(KrHi[0:64, :], 0.0))

            def ckv_chunk(tb):
                tsl = slice(tb * 512, (tb + 1) * 512)
                a_s, acc, tk = proj_fm(0, 2, tb)
                r = rope_i[0] % NR[0]
                rope_i[0] += 1
                sqb = xbf[r]
                msb = t1[r]
                rsb = t2[r]
                t_sq = S.op("act", f_act(sqb, acc, AF.Square), [tk, rope_free[r][0]])
                old_free = rope_free[r][1]

                def part2():
                    p_s, pr, pdeps = RG['aux'].acquire()
                    t_ss = S.op("pe", f_mm(pr, onesb, sqb), [t_sq, t_c] + pdeps)
                    t_m = S.op("act", f_act(msb, pr, AF.Ln, scale=1.0 / 128, bias=epsA), [t_ss, old_free, t_eps])
                    RG['aux'].release(p_s, [t_m])
                    t_r = S.op("act", f_act(rsb, msb, AF.Exp, scale=-0.5), [t_m, old_free])
                    t_ck = S.op("dve", f_stt(ckvnT[:, tsl], acc, cols[:, 0:1], rsb, ALU.mult, ALU.mult), [t_r, tk, t_c2])
                    RG['acc'].release(a_s, [t_sq, t_ck])
                    rope_free[r] = (t_ss, t_ck)
                    kv_up(tb, t_ck)

                deferred_pe.append(part2)

            kv_items = []

            def kv_up(tb, t_ck):
                tsl = slice(tb * 512, (tb + 1) * 512)

                def one(lhsT, rhs, dst):
                    def run():
                        p_s, pr, pdeps = RG['aux'].acquire()
                        t_u = S.op("pe", f_mm(pr, lhsT, rhs), [t_ck, t_wkv] + pdeps)
                        t_e = S.op("dve", f_copy(dst, pr), [t_u])
                        RG['aux'].release(p_s, [t_e])
                    return run

                for h in range(4):
                    kv_items.append(one(wkv[:, 0, h * 128:(h + 1) * 128], ckvnT[:, tsl], KnT[:, h, tsl]))
                for kl in range(4):
                    kb = 4 * tb + kl
                    kv_items.append(one(ckvnT[:, kb * 128:(kb + 1) * 128], wkv[:, 1, :], Va[:, kb, :]))

            def g1_block(tb):
                tsl = slice(tb * 512, (tb + 1) * 512)
                ckv_chunk(tb)
                p0_run(nsteps=2)
                for ci in range(2):
                    a_s, acc, tk = proj_fm(0, ci, tb)
                    evac_rope(a_s, acc, tk, permA, tb, [((0, 128), QrT[:, ci, tsl])])
                    p0_run(nsteps=2)
                a_s, acc, tk = proj_fm(0, 3, tb)
                evac_rope(a_s, acc, tk, permA, tb, [((0, 64), KrLo[0:64, tsl]), ((64, 128), KrHi[64:128, tsl])])
                p0_run(nsteps=2)

            p0_run(until_tile=3)
            load_wgroup(1, 1)
            d_wkv = S.dma_sem()
            t_wkv = S.dma("pool", d_wkv, wkv, wkv_d)
            t_l0 = S.op("dve", f_tt(lam_t[:, 256:320], lam_t[:, 0:64], lam_t[:, 64:128], ALU.mult), [t_c2])
            t_l1 = S.op("dve", lambda e: e.reduce_sum(out=lsum[:, 0:1], in_=lam_t[:, 256:320],
                                                      axis=mybir.AxisListType.X), [t_l0])
            t_l0b = S.op("dve", f_tt(lam_t[:, 256:320], lam_t[:, 128:192], lam_t[:, 192:256], ALU.mult), [t_l1])
            t_l2 = S.op("dve", lambda e: e.reduce_sum(out=lsum[:, 1:2], in_=lam_t[:, 256:320],
                                                      axis=mybir.AxisListType.X), [t_l0b])
            t_le = S.op("act", f_act(lexp, lsum, AF.Exp), [t_l2])
            t_nl = S.op("dve", f_stt(neglam, lexp[:, 1:2], -LAMBDA_INIT, lexp[:, 0:1], ALU.add, ALU.subtract), [t_le])
            t_sg = S.op("dve", f_ts(sgcol, cols[:, 1:2], 1.0 - LAMBDA_INIT, None, ALU.mult), [t_c2, t_nl])

            def g0_block(tb):
                for ci in range(4):
                    a_s, acc, tk = proj_fm(1, ci, tb)
                    t_e = S.op("act", f_acopy(QnT[:, ci, tb * 512:(tb + 1) * 512], acc), [tk])
                    RG['acc'].release(a_s, [t_e])
                    p0_run(nsteps=1)

            for tb in range(2):
                g1_block(tb)
                g0_block(tb)
                p0_run(until_tile=4 * tb + 7)
            g1_block(2)
            p0_run()
            g1_block(3)
            flush_deferred()
            load_wgroup(2, 0)
            load_tabs(1)
            g0_block(2)
            g0_block(3)
            flush_deferred()
            load_wgroup(3, 1)
            for ci in range(4):
                for tb in range(NB):
                    a_s, acc, tk = proj_fm(0, ci, tb)
                    t_e = S.op("act", f_act(gaT[:, ci, tb * 512:(tb + 1) * 512], acc, AF.Silu), [tk])
                    RG['acc'].release(a_s, [t_e])
                    for _ in range(2):
                        if kv_items:
                            kv_items.pop(0)()
            while kv_items:
                kv_items.pop(0)()
            load_wgroup(4, 0)

            def dump(name, view, shape, dt):
                if not debug:
                    return
                o = nc.dram_tensor("dbg_" + name, shape, dt, kind="ExternalOutput").ap()
                dbg_outs[name] = o
                tok = S.dma("sp", d_dbg, o, view, deps=S.barrier())
                S.barrier([tok])

            stop_at(3)
            S.barrier([t_cB])
            dump("hT", hT[:], [128, 8, S_LEN], BF16)
            dump("AB1", AB_t[:], [128, 40960], BF16)

            s_ring_box = [None]

            def attention(nmaps, qT_of, v_of, scale, s_banks, o_banks, sum_banks, epilogue, look):
                npar = len(o_banks)
                s_ring = Ring([bank(b) for b in s_banks])
                s_ring_box[0] = s_ring
                pt_ring = Ring(PT)
                o_free = [[[] for _ in range(nmaps)] for _ in range(npar)]
                sum_free = [[[] for _ in range(nmaps)] for _ in range(npar)]
                tiles = []
                blk = 0
                for h in range(4):
                    for j in (0, 3, 1, 2):
                        for kb in range(4 * j + 4):
                            tiles.append((h, j, kb, blk % npar))
                        blk += 1
                pend = {}
                deferred = []

                def issue_S(idx):
                    h, j, kb, par = tiles[idx]
                    il = kb - 4 * j
                    c0 = 128 * max(0, il)
                    ent = []
                    for m in range(nmaps):
                        s_s, sv, sdeps = s_ring.acquire()
                        parts = qT_of(h, m)
                        tk = None
                        for pi, (kview, qview) in enumerate(parts):
                            last = (pi == len(parts) - 1) and il < 0
                            tk = S.op("pe", f_mm(sv[:, c0:512], kview[:, kb * 128:(kb + 1) * 128],
                                                 qview[:, j * 512 + c0:(j + 1) * 512], pi == 0, last),
                                      sdeps, signal=last)
                        if il >= 0:
                            tk = S.op("pe", f_mm(sv[:, c0:c0 + 128], identb, maskb, False, True), [])
                        p_s, pv, pdeps = pt_ring.acquire()
                        t_e = S.op("act", f_act(pv[:, c0:512], sv[:, c0:512], AF.Exp, scale=scale), [tk] + pdeps)
                        s_ring.release(s_s, [t_e])
                        ent.append((p_s, pv, t_e))
                    pend[idx] = (c0, ent)

                def issue_PV(idx):
                    h, j, kb, par = tiles[idx]
                    c0, ent = pend.pop(idx)
                    first = (kb == 0)
                    last = (kb == 4 * j + 3)
                    toks = []
                    for m, (p_s, pv, t_e) in enumerate(ent):
                        ob = bank(o_banks[par][m])
                        S.op("pe", f_mm(ob[:, c0:512], v_of(h, kb), pv[:, c0:512], first, last),
                             [t_e] + (o_free[par][m] if first else []), signal=False)
                    for m, (p_s, pv, t_e) in enumerate(ent):
                        sb_ = bank(sum_banks[par][m])
                        tk = S.op("pe", f_mm(sb_[:, c0:512], onesb, pv[:, c0:512], first, last),
                                  (sum_free[par][m] if first else []))
                        pt_ring.release(p_s, [tk])
                        toks.append(tk)
                    if last:
                        fo, fs, dfn = epilogue(h, j, toks[-1], o_banks[par], sum_banks[par])
                        for m in range(nmaps):
                            o_free[par][m] = fo[m]
                            sum_free[par][m] = fs[m]
                        if dfn is not None:
                            deferred.append([4, dfn])

                n = len(tiles)
                for i in range(min(look, n)):
                    issue_S(i)
                for i in range(n):
                    issue_PV(i)
                    for d in deferred:
                        d[0] -= 1
                    while deferred and deferred[0][0] <= 0:
                        deferred.pop(0)[1]()
                    if i + look < n:
                        issue_S(i + look)
                while deferred:
                    deferred.pop(0)[1]()

            ep_i = [0]
            ep_prev = [[], []]

            def epilogue_A(h, j, t_last, obanks, sbanks):
                r = ep_i[0] % 2
                ep_i[0] += 1
                rec = ep[r * 6 + 0]
                on = ep[r * 6 + 1]
                tsl = slice(j * 512, (j + 1) * 512)
                t_r = S.op("dve", f_recip(rec, bank(sbanks[0])), [t_last] + ep_prev[r])
                t_o = S.op("dve", f_tt(on, bank(obanks[0]), rec, ALU.mult), [t_r])
                t_g = S.op("pool", f_tt(mixA_t[:, h, tsl], on, gaT[:, h, tsl], ALU.mult), [t_o])
                ep_prev[r] = [t_o, t_g]
                return [[t_o]], [[t_r]], None

            def qparts_A(h, m):
                kr = KrLo if h % 2 == 0 else KrHi
                return [(KnT[:, h, :], QnT[:, h, :]), (kr, QrT[:, h // 2, :])]

            attention(1, qparts_A, lambda h, kb: Va[:, kb, h * 128:(h + 1) * 128], MLA_SCALE,
                      s_banks=[0, 1, 2, 7], o_banks=[[5], [3]], sum_banks=[[6], [4]], epilogue=epilogue_A, look=3)

            stop_at(4)
            if debug:
                S.barrier()
                dve_tail = None
            else:
                toks_ = [S.last_token(e_) for e_ in ("pe", "act", "dve", "pool")]
                for e_ in ("act", "dve", "pool", "sp"):
                    S.wait(e_, toks_)
                S.wait("pe", [toks_[1]])
                dve_tail = toks_[2]
            dump("mixA", mixA_t[:], [128, 4, S_LEN], BF16)

            NR[0] = 6
            del xbf[:], t1[:], t2[:], rope_free[:]
            for i_ in range(6):
                xbf.append(tv(i_ * 5120, 1024))
                t1.append(tv(i_ * 5120 + 1024, 2048, F32))
                t2.append(tv(i_ * 5120 + 3072, 2048, F32))
                rope_free.append((None, None))
            rope_i[0] = 0
            RG['acc'] = Ring([bank(2), bank(0), bank(7), bank(3)])
            RG['aux'] = Ring([bank(1), bank(5), bank(6), bank(4)])
            RG['acc'].free[3] = [dve_tail]
            for i_ in (1, 2, 3):
                RG['aux'].free[i_] = [dve_tail]
            for ci in range(4):
                for tb in range(NB):
                    a_s, acc, tk = proj_fm(1, ci, tb)
                    evac_rope(a_s, acc, tk, permB, tb, [((0, 128), dqT[:, ci, tb * 512:(tb + 1) * 512])])
            flush_deferred()
            load_wgroup(5, 1)
            for ci in range(4):
                for tb in range(NB):
                    a_s, acc, tk = proj_fm(0, ci, tb)
                    tsl = slice(tb * 512, (tb + 1) * 512)
                    evac_rope(a_s, acc, tk, permB, tb, [((0, 64), dkLo[0:64, ci, tsl]), ((64, 128), dkHi[64:128, ci, tsl])])
                    S.op("act", f_act(dkLo[64:128, ci, tsl], mask512[0][64:128, :], AF.Copy, scale=0.0), [t_c])
                    S.op("act", f_act(dkHi[0:64, ci, tsl], mask512[0][0:64, :], AF.Copy, scale=0.0), [t_c])
            flush_deferred()
            load_wgroup(6, 0)
            for ci in range(4):
                for tb in range(NB):
                    a_s, acc, tk = proj_fm(1, ci, tb)
                    t_e = S.op("act", f_act(gbT[:, ci, tb * 512:(tb + 1) * 512], acc, AF.Silu), [tk])
                    RG['acc'].release(a_s, [t_e])
            for t in range(NT):
                a_s, acc, adeps = RG['acc'].acquire()
                tk = None
                for k in range(8):
                    tk = S.op("pe", f_mm(acc, hT[:, k, t * 128:(t + 1) * 128], W_t[:, 0, k, :], k == 0, k == 7),
                              [w_ready[0]] + adeps, signal=(k == 7))
                w_free[0].append(tk)
                t_e = evac_copy(Vb[:, t, :], acc, [tk])
                RG['acc'].release(a_s, [t_e])
            wout = W_t[:].rearrange("p s k f -> p (s k f)").rearrange("p (k f) -> p k f", k=8)
            t_wout = S.dma("pool", d_w[0], wout, wo_d, deps=w_free[0] + w_free[1])

            stop_at(5)
            S.barrier()
            dump("AB2", AB_t[:], [128, 40960], BF16)

            ep_prev[0] = []
            ep_prev[1] = []

            QB = 256
            NQ = S_LEN // QB
            s_ring = Ring([bank(0), bank(1), bank(2), bank(3)])
            pt_ring = Ring(PT)
            o_bk = [4, 5]
            sum_bk = [6, 7]
            o_free = [[], []]
            sum_free = [[], []]
            ssq_free = [None]
            tilesB = []
            blk = 0
            for h in range(4):
                for jq in (0, 7, 1, 6, 2, 5, 3, 4):
                    for kb in range(2 * jq + 2):
                        tilesB.append((h, jq, kb, blk % 2))
                    blk += 1
            pendB = {}
            deferredB = []

            EB0 = 6144
            p1_prev = [[], []]
            dq_prev = [[], [], [], []]
            p2_prev = [[], []]
            blkB = [0]
            p2_i = [0]

            def epilogue_B(h, jq, par, t_last):
                n = blkB[0]
                blkB[0] += 1
                Erec = tv(EB0 + par * 4096, 2048, F32)
                Eo = tv(EB0 + par * 4096 + 2048, 2048, F32)
                k4 = n % 4
                Ed = tv(EB0 + 8192 + k4 * 1536, 1024, F32)
                Eq = tv(EB0 + 8192 + k4 * 1536 + 1024, 512)
                qsl = slice(jq * QB, (jq + 1) * QB)
                if jq in (7, 6, 4):
                    t_ln = S.op("act", f_act(Erec, bank(sum_bk[par]), AF.Ln), [t_last] + p1_prev[par])
                    t_rc = S.op("act", f_act(Erec, Erec, AF.Exp, scale=-1.0), [t_ln])
                else:
                    t_ln = S.op("dve", f_recip(Erec, bank(sum_bk[par])), [t_last] + p1_prev[par])
                    t_rc = t_ln
                t_o = S.op("dve", f_tt(Eo, bank(o_bk[par]), Erec, ALU.mult), [t_rc, t_last] + p1_prev[par])
                t_d = S.op("dve", f_stt(Ed, Eo[:, QB:2 * QB], neglam, Eo[:, 0:QB], ALU.mult, ALU.add),
                           [t_o, t_nl] + dq_prev[k4])
                p1_prev[par] = [t_d]
                t_q = S.op("pool", f_tt(Eq, Ed, Ed, ALU.mult), [t_d] + dq_prev[k4])

                def part2():
                    r2 = p2_i[0] % 2
                    p2_i[0] += 1
                    Er = tv(EB0 + 14336 + r2 * 2048, 1024, F32)
                    Eob = tv(EB0 + 14336 + r2 * 2048 + 1024, 1024, F32)
                    q_s, qbk, qdeps = s_ring.acquire()
                    sq = qbk[:, 0:QB]
                    t_s = S.op("pe", f_mm(sq, onesb, Eq), [t_q] + qdeps)
                    t_m = S.op("act", f_act(Er, sq, AF.Ln, scale=1.0 / 128, bias=epsB), [t_s, t_eps] + p2_prev[r2])
                    s_ring.release(q_s, [t_m])
                    t_p = S.op("act", f_act(Er, Er, AF.Exp, scale=-0.5), [t_m])
                    t_b = S.op("dve", f_stt(Eob, Ed, sgcol, Er, ALU.mult, ALU.mult), [t_p, t_d, t_sg] + p2_prev[r2])
                    t_g = S.op("pool", f_tt(mixB[:, h, qsl], Eob, gbT[:, h, qsl], ALU.mult), [t_b])
                    p2_prev[r2] = [t_b, t_g]
                    dq_prev[k4] = [t_b, t_s]

                return [t_o], [t_ln], part2

            def issue_SB(idx):
                h, jq, kb, par = tilesB[idx]
                il = kb - 2 * jq
                s_s, sv, sdeps = s_ring.acquire()
                qv = dqT[:, h, jq * QB:(jq + 1) * QB]
                ksl = slice(kb * 128, (kb + 1) * 128)
                live = None
                if il == 1:
                    live = lambda ap: ap.rearrange("p (m c) -> p m c", m=2)[:, :, 128:256]
                    qv1 = dqT[:, h, jq * QB + 128:(jq + 1) * QB]
                    S.op("pe", f_mm(sv[:, 128:2 * QB], identb, tri3, True, False), sdeps, signal=False)
                    S.op("pe", f_mm(sv[:, 128:QB], dkLo[:, h, ksl], qv1, False, False), [], signal=False)
                    tk = S.op("pe", f_mm(sv[:, QB + 128:2 * QB], dkHi[:, h, ksl], qv1, False, True), [])
                elif il == 0:
                    S.op("pe", f_mm(sv, identb, mask512[il], True, False), sdeps, signal=False)
                    S.op("pe", f_mm(sv[:, 0:QB], dkLo[:, h, ksl], qv, False, False), [], signal=False)
                    tk = S.op("pe", f_mm(sv[:, QB:2 * QB], dkHi[:, h, ksl], qv, False, True), [])
                else:
                    S.op("pe", f_mm(sv[:, 0:QB], dkLo[:, h, ksl], qv, True, True), sdeps, signal=False)
                    tk = S.op("pe", f_mm(sv[:, QB:2 * QB], dkHi[:, h, ksl], qv, True, True), [])
                p_s, pv, pdeps = pt_ring.acquire()
                if live is not None:
                    t_e = S.op("act", f_act(live(pv), live(sv), AF.Exp, scale=DIFF_SCALE), [tk] + pdeps)
                else:
                    t_e = S.op("act", f_act(pv, sv, AF.Exp, scale=DIFF_SCALE), [tk] + pdeps)
                s_ring.release(s_s, [t_e])
                pendB[idx] = (p_s, pv, t_e, live)

            def issue_PVB(idx):
                h, jq, kb, par = tilesB[idx]
                p_s, pv, t_e, live = pendB.pop(idx)
                first = (kb == 0)
                last = (kb == 2 * jq + 1)
                ob_, sb_ = bank(o_bk[par]), bank(sum_bk[par])
                vh = Vb[:, kb, h * 128:(h + 1) * 128]
                if live is not None:
                    assert last and not first
                    l0, l1 = slice(128, QB), slice(QB + 128, 2 * QB)
                    S.op("pe", f_mm(ob_[:, l0], vh, pv[:, l0], False, False), [t_e], signal=False)
                    S.op("pe", f_mm(ob_[:, l1], vh, pv[:, l1], False, True), [], signal=False)
                    S.op("pe", f_mm(sb_[:, l0], onesb, pv[:, l0], False, False), [], signal=False)
                    tk = S.op("pe", f_mm(sb_[:, l1], onesb, pv[:, l1], False, True), [])
                else:
                    S.op("pe", f_mm(ob_, vh, pv, first, last),
                         [t_e] + (o_free[par] if first else []), signal=False)
                    tk = S.op("pe", f_mm(sb_, onesb, pv, first, last), (sum_free[par] if first else []))
                pt_ring.release(p_s, [tk])
                if last:
                    fo, fs, dfn = epilogue_B(h, jq, par, tk)
                    o_free[par] = fo
                    sum_free[par] = fs
                    deferredB.append([12, dfn])

            nB = len(tilesB)
            LOOK = 3
            for i in range(min(LOOK, nB)):
                issue_SB(i)
            for i in range(nB):
                issue_PVB(i)
                for d_ in deferredB:
                    d_[0] -= 1
                while deferredB and deferredB[0][0] <= 0:
                    deferredB.pop(0)[1]()
                if i + LOOK < nB:
                    issue_SB(i + LOOK)
            while deferredB:
                deferredB.pop(0)[1]()

            stop_at(6)
            S.barrier()
            dump("mixB", mixB, [128, 4, S_LEN], BF16)

            t_gpost = S.dma("sp", d_c2, gpost, gpost_d.partition_broadcast(128))
            pair_ring = Ring([psp[0], psp[1], psp[2], psp[3]])
            xs_free5 = [None, None, None]
            y_free = [None, None, None]
            out_toks = []
            gv = gpost.rearrange("p (a f) -> p a f", a=2)
            st5 = {}

            def p5_A(t):
                s = t % 3
                tx = S.dma("sp", d_xs5[s], xs5[s], x_d[t * 128:(t + 1) * 128, :], deps=[xs_free5[s]])
                p_s, pp, pdeps = pair_ring.acquire()
                tk = None
                for half in range(2):
                    for c in range(8):
                        src_ = mixA_t[:, c, t * 128:(t + 1) * 128] if c < 4 else mixB[:, c - 4, t * 128:(t + 1) * 128]
                        tk = S.op("pe", f_mm(pp[:, half, :], src_, wout[:, c, half * 512:(half + 1) * 512], c == 0, c == 7),
                                  [t_wout] + pdeps, signal=(half == 1 and c == 7))
                t_q0 = S.op("act", f_act(junk5[:, 0:512], pp[:, 0, :], AF.Square, accum_out=ssq5[:, 2 * t:2 * t + 1]), [tk])
                t_q1 = S.op("act", f_act(junk5[:, 512:1024], pp[:, 1, :], AF.Square,
                                         accum_out=ssq5[:, 2 * t + 1:2 * t + 2]), [tk, t_q0])
                t_m = S.op("dve", f_stt(ms5[:, t:t + 1], ssq5[:, 2 * t:2 * t + 1], 1.0, ssq5[:, 2 * t + 1:2 * t + 2],
                                        ALU.mult, ALU.add), [t_q0, t_q1])
                t_m2 = S.op("dve", f_ts(rstd5[:, t:t + 1], ms5[:, t:t + 1], 1.0 / D, NORM_EPS, ALU.mult, ALU.add), [t_m])
                t_r = S.op("pool", f_tt(ms5[:, t:t + 1], rstd5[:, t:t + 1], mhcol, ALU.pow), [t_m2, t_mh2])
                st5[t] = (tx, p_s, pp, t_r)

            def p5_B(t):
                s = t % 3
                tx, p_s, pp, t_r = st5.pop(t)
                yv = ybuf[s].rearrange("p (a f) -> p a f", a=2)
                t_y = None
                for half in range(2):
                    t_y = S.op("act", f_act(yv[:, half, :], pp[:, half, :], AF.Identity, scale=ms5[:, t:t + 1]),
                               [t_r, y_free[s]])
                pair_ring.release(p_s, [t_y])
                t_w = S.op("dve", f_tt(ybuf[s], ybuf[s], gpost, ALU.mult), [t_y, t_gpost])
                t_a = S.op("dve", f_tt(ybuf[s], ybuf[s], xs5[s], ALU.add), [t_w, tx])
                xs_free5[s] = t_a
                t_o = S.dma("sp", d_out[s], out_d[t * 128:(t + 1) * 128, :], ybuf[s], deps=[t_a])
                y_free[s] = t_o
                out_toks.append(t_o)

            p5_A(0)
            for t in range(NT):
                if t + 1 < NT:
                    p5_A(t + 1)
                p5_B(t)
            S.barrier(out_toks[-3:])
            S.emit()
        except _Stop:
            pass
    if debug:
        return nc, dbg_outs
    return nc


_CACHE = {}


def _prep_inputs(inputs):
    x = np.asarray(inputs["x"], dtype=np.float32)
    wl, wo, wkv = _layout_weights(np.asarray(inputs["w_in"]), np.asarray(inputs["w_uk"]),
                                  np.asarray(inputs["w_uv"]), np.asarray(inputs["w_out"]))
    cmat, tabs = _host_consts()
    gpre = np.ascontiguousarray(np.asarray(inputs["ln_pre_g"], dtype=np.float32).reshape(1, D))
    gpost = np.ascontiguousarray(np.asarray(inputs["ln_post_g"], dtype=np.float32).reshape(1, D))
    cols = np.ascontiguousarray(np.stack([np.asarray(inputs["kv_norm_g"], dtype=np.float32).reshape(128),
                                          np.asarray(inputs["subln_g"], dtype=np.float32).reshape(128)], axis=1))
    lamv = np.ascontiguousarray(np.concatenate([
        np.asarray(inputs["lambda_q1"], dtype=np.float32).reshape(64),
        np.asarray(inputs["lambda_k1"], dtype=np.float32).reshape(64),
        np.asarray(inputs["lambda_q2"], dtype=np.float32).reshape(64),
        np.asarray(inputs["lambda_k2"], dtype=np.float32).reshape(64)]).reshape(1, 256))
    shared = {"wl": wl, "wo": wo, "wkv": wkv, "cmat": cmat, "tabs": tabs, "gpre": gpre, "gpost": gpost,
              "cols": cols, "lamv": lamv}
    return x, shared


def kernel(**inputs):
    x, shared = _prep_inputs(inputs)
    B = x.shape[0]
    if "nc" not in _CACHE:
        _CACHE["nc"] = build_program()
    nc = _CACHE["nc"]
    in_maps = [dict(shared, x=np.ascontiguousarray(x[b])) for b in range(B)]
    res = run_bass_kernel_spmd(nc, in_maps, core_ids=list(range(B)))
    out = np.stack([np.asarray(r["out"], dtype=np.float32) for r in res.results], axis=0)
    return out
```

```python
import math
from contextlib import ExitStack

import numpy as np

import concourse.bass as bass
import concourse.mybir as mybir
from concourse.bass_utils import run_bass_kernel_spmd

F32 = mybir.dt.float32
BF16 = mybir.dt.bfloat16
AF = mybir.ActivationFunctionType
ALU = mybir.AluOpType

S_LEN = 2048
D = 1024
NT = S_LEN // 128
NB = S_LEN // 512
ROPE_THETA = 500000.0
NORM_EPS = 1e-6
SUBLN_EPS = 1e-5
MLA_SCALE = 1.0 / math.sqrt(128 + 64)
DIFF_SCALE = 1.0 / math.sqrt(64)
LAMBDA_INIT = 0.8 - 0.6 * math.exp(-0.3 * 0)
MASK_NEG = -30000.0

ENGS = ["pe", "act", "dve", "pool", "sp"]


class Sched:
    def __init__(self, nc, ctx):
        self.nc = nc
        self.ctx = ctx
        self.streams = {e: [] for e in ENGS}
        self.sem = {e: ctx.enter_context(nc.semaphore("s_" + e)) for e in ENGS if e != "sp"}
        self.cnt = {e: 0 for e in ENGS}
        self.waited = {e: {} for e in ENGS}
        self.nd = 0

    def dma_sem(self):
        self.nd += 1
        return [self.ctx.enter_context(self.nc.semaphore(f"d{self.nd}")), 0]

    def _waits(self, eng, deps):
        ws = []
        for d in deps:
            if d is None:
                continue
            if isinstance(d, list):
                ws += self._waits(eng, d)
                continue
            sem, val = d
            k = id(sem)
            if self.waited[eng].get(k, 0) >= val:
                continue
            self.waited[eng][k] = val
            ws.append((sem, val))
        return ws

    def op(self, eng, fn, deps=(), signal=True):
        ws = self._waits(eng, deps)
        ent = [ws, fn, None]
        self.streams[eng].append(ent)
        if signal:
            self.cnt[eng] += 1
            ent[2] = (self.sem[eng], 1)
            return (self.sem[eng], self.cnt[eng])
        return None

    def last_token(self, eng):
        st = self.streams[eng]
        for ent in reversed(st):
            if ent[1] is None:
                continue
            if ent[2] is None:
                self.cnt[eng] += 1
                ent[2] = (self.sem[eng], 1)
            elif ent[2][0] is not self.sem[eng]:
                continue
            break
        if self.cnt[eng] == 0:
            return None
        return (self.sem[eng], self.cnt[eng])

    def dma(self, queue, dsem, out, in_, deps=()):
        ws = self._waits(queue, deps)
        dsem[1] += 16
        self.streams[queue].append([ws, lambda e: e.dma_start(out=out, in_=in_), (dsem[0], 16)])
        return (dsem[0], dsem[1])

    def wait(self, eng, deps):
        ws = self._waits(eng, deps)
        if ws:
            self.streams[eng].append([ws, None, None])

    def barrier(self, extra=()):
        toks = [self.last_token(e) for e in ("pe", "act", "dve", "pool")]
        toks = [t for t in toks if t is not None] + list(extra)
        for e in ENGS:
            self.wait(e, toks)
        return toks

    def emit(self):
        with self.nc.Block() as block:
            def run(name):
                def f(e):
                    embed_ok = name in ("act", "dve", "pool")
                    for ws, fn, inc in self.streams[name]:
                        ws = list(ws)
                        emb = None
                        if fn is not None and ws and embed_ok and not getattr(fn, "no_embed", False) \
                                and (inc is None or inc[0] is self.sem.get(name)):
                            emb = ws.pop()
                        for sem, val in ws:
                            e.wait_ge(sem, val)
                        if fn is not None:
                            ins = fn(e)
                            if emb is not None:
                                ins._wait_ge(emb[0], emb[1])
                            if inc is not None:
                                ins.then_inc(inc[0], inc[1])
                return f
            block.sync(run("sp"))
            block.tensor(run("pe"))
            block.scalar(run("act"))
            block.vector(run("dve"))
            block.gpsimd(run("pool"))


def f_mm(out, lhsT, rhs, start=True, stop=True):
    return lambda e: e.matmul(out, lhsT=lhsT, rhs=rhs, start=start, stop=stop)


def f_tr(out, in_, ident):
    return lambda e: e.transpose(out=out, in_=in_, identity=ident)


def f_act(out, in_, func, **kw):
    fn = lambda e: e.activation(out=out, in_=in_, func=func, **kw)
    if "accum_out" in kw:
        fn.no_embed = True
    return fn


def f_acopy(out, in_):
    return lambda e: e.copy(out=out, in_=in_)


def f_copy(out, in_):
    return lambda e: e.tensor_copy(out=out, in_=in_)


def f_tt(out, in0, in1, op):
    return lambda e: e.tensor_tensor(out=out, in0=in0, in1=in1, op=op)


def f_ts(out, in0, s1, s2, op0, op1=None):
    if op1 is None:
        return lambda e: e.tensor_scalar(out=out, in0=in0, scalar1=s1, scalar2=None, op0=op0)
    return lambda e: e.tensor_scalar(out=out, in0=in0, scalar1=s1, scalar2=s2, op0=op0, op1=op1)


def f_stt(out, in0, scalar, in1, op0, op1):
    return lambda e: e.scalar_tensor_tensor(out=out, in0=in0, scalar=scalar, in1=in1, op0=op0, op1=op1)


def f_recip(out, in_):
    return lambda e: e.reciprocal(out=out, in_=in_)


def f_memset(ap, v):
    return lambda e: e.memset(ap, v)


class Ring:
    def __init__(self, views):
        self.views = views
        self.free = [[] for _ in views]
        self.i = 0

    def acquire(self):
        s = self.i % len(self.views)
        self.i += 1
        deps = self.free[s]
        self.free[s] = []
        return s, self.views[s], deps

    def release(self, s, toks):
        self.free[s] = [t for t in toks if t is not None]


def _rope_tables():
    t = np.arange(S_LEN, dtype=np.float32)

    def tab(dim):
        inv = (np.float32(ROPE_THETA) ** (-(np.arange(0, dim, 2, dtype=np.float32)) / np.float32(dim))).astype(np.float32)
        ang = (t[:, None] * inv[None, :]).astype(np.float32)
        return np.cos(ang.astype(np.float64)).astype(np.float32), np.sin(ang.astype(np.float64)).astype(np.float32)

    ca, sa = tab(64)
    cb, sb = tab(16)
    CA = np.zeros((128, S_LEN), np.float32)
    SA = np.zeros((128, S_LEN), np.float32)
    CB = np.ones((128, S_LEN), np.float32)
    SB = np.zeros((128, S_LEN), np.float32)
    PA = np.zeros((128, 128), np.float32)
    PB = np.zeros((128, 128), np.float32)
    for r in range(128):
        d = r % 64
        i = d % 32
        CA[r] = ca[:, i]
        if d < 32:
            SA[r] = -sa[:, i]
            PA[r + 32, r] = 1.0
        else:
            SA[r] = sa[:, i]
            PA[r - 32, r] = 1.0
        if d < 8:
            CB[r] = cb[:, d]
            SB[r] = -sb[:, d]
            PB[r + 8, r] = 1.0
        elif d < 16:
            CB[r] = cb[:, d - 8]
            SB[r] = sb[:, d - 8]
            PB[r - 8, r] = 1.0
    return CA, SA, CB, SB, PA, PB


def _host_consts():
    CA, SA, CB, SB, PA, PB = _rope_tables()
    ident = np.eye(128, dtype=np.float32)
    kk = np.arange(128)[:, None]
    qq = np.arange(128)[None, :]
    mask = np.where(qq >= kk, 0.0, MASK_NEG).astype(np.float32)
    zero = np.zeros((128, 128), np.float32)
    neg = np.full((128, 128), MASK_NEG, np.float32)
    cmat = np.stack([ident, PA, np.ones((128, 128), np.float32), PB, mask,
                     mask, zero, mask, zero,
                     neg, mask, neg, mask],
                    axis=1)
    tabs = np.stack([CA, SA, CB, SB], axis=0)
    return np.ascontiguousarray(cmat), np.ascontiguousarray(tabs)


def _layout_weights(w_in, w_uk, w_uv, w_out):
    w = w_in[0]
    q_nope = w[:, 0:512]
    q_rope = w[:, 512:768]
    c_kv = w[:, 768:896]
    k_rope = w[:, 896:960]
    g_a = w[:, 960:1472]
    dq = w[:, 1472:1984]
    dk = w[:, 1984:2496]
    dv = w[:, 2496:3008]
    g_b = w[:, 3008:3520]
    groups = [
        np.concatenate([q_rope, c_kv, k_rope, k_rope], axis=1),
        q_nope,
        g_a,
        dq,
        dk,
        g_b,
        dv,
    ]
    wl = np.stack([g.reshape(8, 128, 512).transpose(1, 0, 2) for g in groups], axis=0)
    wo = w_out[0].reshape(8, 128, 1024).transpose(1, 0, 2)
    wkv = np.stack([w_uk[0], w_uv[0]], axis=1)
    return (np.ascontiguousarray(wl, dtype=np.float32), np.ascontiguousarray(wo, dtype=np.float32),
            np.ascontiguousarray(wkv, dtype=np.float32))


def build_program(debug=False, stop=None):
    nc = bass.Bass("TRN2", target_bir_lowering=False)
    x_d = nc.dram_tensor("x", [S_LEN, D], F32, kind="ExternalInput").ap()
    wl_d = nc.dram_tensor("wl", [7, 128, 8, 512], F32, kind="ExternalInput").ap()
    wo_d = nc.dram_tensor("wo", [128, 8, 1024], F32, kind="ExternalInput").ap()
    wkv_d = nc.dram_tensor("wkv", [128, 2, 512], F32, kind="ExternalInput").ap()
    cmat_d = nc.dram_tensor("cmat", [128, 13, 128], F32, kind="ExternalInput").ap()
    tabs_d = nc.dram_tensor("tabs", [4, 128, S_LEN], F32, kind="ExternalInput").ap()
    gpre_d = nc.dram_tensor("gpre", [1, D], F32, kind="ExternalInput").ap()
    gpost_d = nc.dram_tensor("gpost", [1, D], F32, kind="ExternalInput").ap()
    cols_d = nc.dram_tensor("cols", [128, 2], F32, kind="ExternalInput").ap()
    lamv_d = nc.dram_tensor("lamv", [1, 256], F32, kind="ExternalInput").ap()
    out_d = nc.dram_tensor("out", [S_LEN, D], F32, kind="ExternalOutput").ap()
    dbg_outs = {}

    ctx = ExitStack()
    with ctx:
        S = Sched(nc, ctx)

        def sb(name, shape, dt):
            return nc.alloc_sbuf_tensor(name, shape, dt)

        class _Stop(Exception):
            pass

        def stop_at(k):
            if stop is not None and stop == k:
                S.barrier()
                S.emit()
                raise _Stop()

        hT_t = sb("hT", [128, 8, S_LEN], BF16)
        AB_t = sb("AB", [128, 40960], BF16)
        mixA_t = sb("mixA", [128, 4, S_LEN], BF16)
        W_t = sb("W", [128, 2, 8, 512], BF16)
        tab_t = sb("tab", [128, 2, S_LEN], F32)
        T_t = sb("T", [128, 17408], BF16)
        cm_t = sb("cm", [128, 13, 128], BF16)
        small = sb("small", [128, 128], F32)
        lam_t = sb("lamt", [128, 320], F32)
        xs2_t = sb("xs2", [128, 2, 1024], F32)

        identb = cm_t[:, 0, :]
        permA = cm_t[:, 1, :]
        onesb = cm_t[:, 2, :]
        permB = cm_t[:, 3, :]
        maskb = cm_t[:, 4, :]
        tri3 = cm_t[:, 10:13, :].rearrange("p a b -> p (a b)")
        mask512 = [cm_t[:, 5:9, :].rearrange("p a b -> p (a b)"), cm_t[:, 9:13, :].rearrange("p a b -> p (a b)")]

        hT = hT_t
        mixB = hT_t[:, 0:4, :]

        def ab(off_k, n_k):
            return AB_t[:, off_k * 512:(off_k + n_k) * 512]

        QnT = ab(0, 16).rearrange("p (h t) -> p h t", h=4)
        QrT = ab(16, 8).rearrange("p (c t) -> p c t", c=2)
        KnT = ab(24, 16).rearrange("p (h t) -> p h t", h=4)
        KrLo = ab(40, 4)
        KrHi = ab(44, 4)
        Va = ab(48, 16).rearrange("p (k f) -> p k f", k=16)
        gaT = ab(64, 16).rearrange("p (h t) -> p h t", h=4)
        dqT = ab(0, 16).rearrange("p (h t) -> p h t", h=4)
        dkLo = ab(16, 16).rearrange("p (h t) -> p h t", h=4)
        dkHi = ab(32, 16).rearrange("p (h t) -> p h t", h=4)
        Vb = ab(48, 16).rearrange("p (k f) -> p k f", k=16)
        gbT = ab(64, 16).rearrange("p (h t) -> p h t", h=4)

        def tv(off_b, n_b, dt=BF16):
            v = T_t[:, off_b // 2:(off_b + n_b) // 2]
            if dt == F32:
                v = v.bitcast(F32)
            return v

        xs = [tv(0, 4096, F32), tv(4096, 4096, F32), xs2_t[:, 0, :], xs2_t[:, 1, :]]
        xn = [tv(8192, 2048), tv(10240, 2048)]
        junk = tv(12288, 2048)
        xbf = [tv(14336, 1024), tv(15360, 1024)]
        t1 = [tv(16384, 2048, F32), tv(18432, 2048, F32)]
        t2 = [tv(20480, 2048, F32), tv(22528, 2048, F32)]
        ckvnT = tv(24576, 4096)
        wkv = tv(28672, 2048).rearrange("p (a f) -> p a f", a=2)
        gpre = tv(30720, 4096, F32)
        PT = [tv(i * 1024, 1024) for i in range(6)]
        ep = [tv(6144 + i * 2048, 2048, F32) for i in range(12)]
        dsq = [tv(30720, 1024), tv(31744, 1024)]
        xs5 = [tv(i * 4096, 4096, F32) for i in range(3)]
        ybuf = [tv(12288 + i * 4096, 4096, F32) for i in range(3)]
        gpost = tv(24576, 4096, F32)
        junk5 = tv(28672, 2048)

        ssq = small[:, 0:16]
        ms = small[:, 16:32]
        rstd = small[:, 32:48]
        ssq5 = small[:, 48:80]
        ms5 = small[:, 80:96]
        rstd5 = small[:, 96:112]
        cols = small[:, 112:114]
        sgcol = small[:, 114:115]
        lsum = small[:, 115:117]
        lexp = small[:, 117:119]
        neglam = small[:, 119:120]
        mhcol = small[:, 120:121]
        epsA = small[:, 121:122]
        epsB = small[:, 122:123]

        psp = [nc.alloc_psum_tensor(f"psp{i}", [128, 2, 512], F32) for i in range(4)]

        def bank(b):
            return psp[b // 2][:, b % 2, :]

        d_xs = [S.dma_sem() for _ in range(4)]
        d_w = [S.dma_sem(), S.dma_sem()]
        d_tab = S.dma_sem()
        d_c = S.dma_sem()
        d_c2 = S.dma_sem()
        d_out = [S.dma_sem(), S.dma_sem(), S.dma_sem()]
        d_xs5 = [S.dma_sem(), S.dma_sem(), S.dma_sem()]
        d_dbg = S.dma_sem()

        try:
            x_tok = {}
            x_tok[0] = S.dma("sp", d_xs[0], xs[0], x_d[0:128, :])
            t_c = S.dma("pool", d_c, cm_t[:, 0:3, :], cmat_d[:, 0:3, :])
            S.dma("sp", d_c2, gpre, gpre_d.partition_broadcast(128))
            S.dma("sp", d_c2, cols, cols_d)
            t_c2 = S.dma("sp", d_c2, lam_t[:, 0:256], lamv_d.partition_broadcast(128))
            t_mh2 = S.op("pool", f_memset(mhcol, -0.5))
            S.op("pool", f_memset(epsA, NORM_EPS))
            t_eps = S.op("pool", f_memset(epsB, SUBLN_EPS))
            stop_at(0)

            pT_ring = Ring([bank(0), bank(1)])
            RG = {'acc': Ring([bank(2), bank(3), bank(4), bank(5)]), 'aux': Ring([bank(6), bank(7)])}
            xs_free = [None] * 4
            xn_free = [None, None]
            hT_ready = [None] * NT
            w_free = [[], []]
            w_ready = [None, None]
            flip = [0]
            junk_free = [None]

            def load_wgroup(g, slot, extra=()):
                w_ready[slot] = S.dma("pool", d_w[slot], W_t[:, slot], wl_d[g], deps=list(w_free[slot]) + list(extra))
                w_free[slot] = []

            xn_tok = {}

            def p0_A(t):
                s = t % 2
                xq = t % 4
                tx = x_tok.pop(t) if t in x_tok else S.dma("sp", d_xs[xq], xs[xq], x_d[t * 128:(t + 1) * 128, :],
                                                           deps=[xs_free[xq]])
                t_ssq = S.op("act", f_act(junk, xs[xq], AF.Square, accum_out=ssq[:, t:t + 1]), [tx, junk_free[0]])
                junk_free[0] = t_ssq
                t_ms = S.op("dve", f_ts(ms[:, t:t + 1], ssq[:, t:t + 1], 1.0 / D, NORM_EPS, ALU.mult, ALU.add), [t_ssq])
                t_rs = S.op("pool", f_tt(rstd[:, t:t + 1], ms[:, t:t + 1], mhcol, ALU.pow), [t_ms, t_mh2])
                t_xn = S.op("dve", f_stt(xn[s], xs[xq], rstd[:, t:t + 1], gpre, ALU.mult, ALU.mult),
                            [t_rs, t_c2, tx, xn_free[s]])
                xs_free[xq] = t_xn
                xn_tok[t] = t_xn

            def p0_B(t):
                s = t % 2
                t_xn = xn_tok.pop(t)
                ps_, pv, pdeps = pT_ring.acquire()
                pTv = pv.bitcast(BF16).rearrange("p (c t) -> p c t", c=8)
                t_tr = None
                for c in range(8):
                    t_tr = S.op("pe", f_tr(pTv[:, c, :], xn[s][:, c * 128:(c + 1) * 128], identb),
                                [t_xn, t_c] + pdeps, signal=(c == 7))
                xn_free[s] = t_tr
                dst = hT[:, :, t * 128:(t + 1) * 128]
                if t % 4 != 3:
                    t_h = S.op("act", f_acopy(dst, pTv), [t_tr])
                else:
                    t_h = S.op("dve", f_copy(dst, pTv), [t_tr])
                pT_ring.release(ps_, [t_h])
                hT_ready[t] = t_h

            p0_steps = [("A", 0)]
            for t in range(NT):
                if t + 1 < NT:
                    p0_steps.append(("A", t + 1))
                p0_steps.append(("B", t))

            def p0_run(until_tile=None, nsteps=None):
                n = 0
                while p0_steps:
                    if nsteps is not None and n >= nsteps:
                        break
                    if until_tile is not None and hT_ready[until_tile] is not None:
                        break
                    kind, t = p0_steps.pop(0)
                    (p0_A if kind == "A" else p0_B)(t)
                    n += 1

            deferred_pe = []

            def run_deferred():
                todo = list(deferred_pe)
                del deferred_pe[:]
                for ent_ in todo:
                    ent_[0] -= 1
                    if ent_[0] <= 0:
                        ent_[1]()
                    else:
                        deferred_pe.append(ent_)

            def flush_deferred():
                while deferred_pe:
                    run_deferred()

            def proj_fm(slot, ci, tb):
                a_s, acc, adeps = RG['acc'].acquire()
                tk = None
                for k in range(8):
                    tk = S.op("pe", f_mm(acc, W_t[:, slot, k, ci * 128:(ci + 1) * 128],
                                         hT[:, k, tb * 512:(tb + 1) * 512], k == 0, k == 7),
                              [w_ready[slot]] + hT_ready[4 * tb:4 * tb + 4] + adeps, signal=(k == 7))
                w_free[slot].append(tk)
                run_deferred()
                return a_s, acc, tk

            def evac_copy(dst, src, deps):
                flip[0] ^= 1
                if flip[0]:
                    return S.op("act", f_acopy(dst, src), deps)
                return S.op("dve", f_copy(dst, src), deps)

            rope_i = [0]
            rope_free = [(None, None), (None, None)]
            NR = [2]
            ROPE_DEFER = [1]
            tab_ready = [None]
            tab_users = []

            def evac_rope(acc_s, acc, t_mm, perm, tb, dsts):
                r = rope_i[0] % NR[0]
                rope_i[0] += 1
                tsl = slice(tb * 512, (tb + 1) * 512)
                t_xb = S.op("act", f_acopy(xbf[r], acc), [t_mm, rope_free[r][0]])
                t_a = S.op("dve", f_tt(t1[r], acc, tab_t[:, 0, tsl], ALU.mult), [t_mm, t_xb, tab_ready[0], rope_free[r][1]])
                RG['acc'].release(acc_s, [t_xb, t_a])
                old_free = rope_free[r][1]
                tab_r = tab_ready[0]

                def part2():
                    p_s, pr, pdeps = RG['aux'].acquire()
                    t_pm = S.op("pe", f_mm(pr, perm, xbf[r]), [t_xb, t_c] + pdeps)
                    t_b = S.op("dve", f_tt(t2[r], pr, tab_t[:, 1, tsl], ALU.mult), [t_pm, tab_r, old_free])
                    RG['aux'].release(p_s, [t_b])
                    t_c_ = []
                    for (p0, p1), dst in dsts:
                        eng_ = "dve" if p0 == 64 else "pool"
                        t_c_.append(S.op(eng_, f_tt(dst, t1[r][p0:p1, :], t2[r][p0:p1, :], ALU.add), [t_a, t_b]))
                    rope_free[r] = (t_pm, t_c_)
                    tab_users.append(t_b)

                deferred_pe.append([ROPE_DEFER[0], part2])

            def load_tabs(which, extra=()):
                deps = list(tab_users) + list(extra)
                del tab_users[:]
                S.dma("sp", d_tab, tab_t[:, 0, :], tabs_d[2 * which], deps=deps)
                tab_ready[0] = S.dma("sp", d_tab, tab_t[:, 1, :], tabs_d[2 * which + 1], deps=deps)

            load_wgroup(0, 0)
            d_cB = S.dma_sem()
            t_cB = S.dma("pool", d_cB, cm_t[:, 3:13, :], cmat_d[:, 3:13, :])
            for t in range(1, 4):
                x_tok[t] = S.dma("sp", d_xs[t], xs[t], x_d[t * 128:(t + 1) * 128, :])
            load_tabs(0, extra=[x_tok[3], w_ready[0]])
            S.op("pool", f_memset(KrLo[64:128, :], 0.0))
            S.op("pool", f_memset(KrHi[0:64, :], 0.0))

            def ckv_chunk(tb):
                tsl = slice(tb * 512, (tb + 1) * 512)
                a_s, acc, tk = proj_fm(0, 2, tb)
                r = rope_i[0] % NR[0]
                rope_i[0] += 1
                sqb = xbf[r]
                msb = t1[r]
                rsb = t2[r]
                t_sq = S.op("act", f_act(sqb, acc, AF.Square), [tk, rope_free[r][0]])
                old_free = rope_free[r][1]

                def part2():
                    p_s, pr, pdeps = RG['aux'].acquire()
                    t_ss = S.op("pe", f_mm(pr, onesb, sqb), [t_sq, t_c] + pdeps)
                    t_m = S.op("act", f_act(msb, pr, AF.Ln, scale=1.0 / 128, bias=epsA), [t_ss, old_free, t_eps])
                    RG['aux'].release(p_s, [t_m])
                    t_r = S.op("act", f_act(rsb, msb, AF.Exp, scale=-0.5), [t_m, old_free])
                    t_ck = S.op("dve", f_stt(ckvnT[:, tsl], acc, cols[:, 0:1], rsb, ALU.mult, ALU.mult), [t_r, tk, t_c2])
                    RG['acc'].release(a_s, [t_sq, t_ck])
                    rope_free[r] = (t_ss, t_ck)
                    kv_up(tb, t_ck)

                deferred_pe.append([1, part2])

            kv_items = []

            def kv_up(tb, t_ck):
                tsl = slice(tb * 512, (tb + 1) * 512)

                def one(lhsT, rhs, dst):
                    def run():
                        p_s, pr, pdeps = RG['aux'].acquire()
                        t_u = S.op("pe", f_mm(pr, lhsT, rhs), [t_ck, t_wkv] + pdeps)
                        t_e = S.op("dve", f_copy(dst, pr), [t_u])
                        RG['aux'].release(p_s, [t_e])
                    return run

                for h in range(4):
                    kv_items.append(one(wkv[:, 0, h * 128:(h + 1) * 128], ckvnT[:, tsl], KnT[:, h, tsl]))
                for kl in range(4):
                    kb = 4 * tb + kl
                    kv_items.append(one(ckvnT[:, kb * 128:(kb + 1) * 128], wkv[:, 1, :], Va[:, kb, :]))

            def g1_block(tb):
                tsl = slice(tb * 512, (tb + 1) * 512)
                ckv_chunk(tb)
                p0_run(nsteps=2)
                for ci in range(2):
                    a_s, acc, tk = proj_fm(0, ci, tb)
                    evac_rope(a_s, acc, tk, permA, tb, [((0, 128), QrT[:, ci, tsl])])
                    p0_run(nsteps=2)
                a_s, acc, tk = proj_fm(0, 3, tb)
                evac_rope(a_s, acc, tk, permA, tb, [((0, 64), KrLo[0:64, tsl]), ((64, 128), KrHi[64:128, tsl])])
                p0_run(nsteps=2)

            p0_run(until_tile=3)
            load_wgroup(1, 1)
            d_wkv = S.dma_sem()
            t_wkv = S.dma("pool", d_wkv, wkv, wkv_d)
            t_l0 = S.op("dve", f_tt(lam_t[:, 256:320], lam_t[:, 0:64], lam_t[:, 64:128], ALU.mult), [t_c2])
            t_l1 = S.op("dve", lambda e: e.reduce_sum(out=lsum[:, 0:1], in_=lam_t[:, 256:320],
                                                      axis=mybir.AxisListType.X), [t_l0])
            t_l0b = S.op("dve", f_tt(lam_t[:, 256:320], lam_t[:, 128:192], lam_t[:, 192:256], ALU.mult), [t_l1])
            t_l2 = S.op("dve", lambda e: e.reduce_sum(out=lsum[:, 1:2], in_=lam_t[:, 256:320],
                                                      axis=mybir.AxisListType.X), [t_l0b])
            t_le = S.op("act", f_act(lexp, lsum, AF.Exp), [t_l2])
            t_nl = S.op("dve", f_stt(neglam, lexp[:, 1:2], -LAMBDA_INIT, lexp[:, 0:1], ALU.add, ALU.subtract), [t_le])
            t_sg = S.op("dve", f_ts(sgcol, cols[:, 1:2], 1.0 - LAMBDA_INIT, None, ALU.mult), [t_c2, t_nl])

            def g0_block(tb):
                for ci in range(4):
                    a_s, acc, tk = proj_fm(1, ci, tb)
                    t_e = S.op("act", f_acopy(QnT[:, ci, tb * 512:(tb + 1) * 512], acc), [tk])
                    RG['acc'].release(a_s, [t_e])
                    p0_run(nsteps=1)

            for tb in range(2):
                g1_block(tb)
                g0_block(tb)
                p0_run(until_tile=4 * tb + 7)
            g1_block(2)
            p0_run()
            g1_block(3)
            flush_deferred()
            load_wgroup(2, 0)
            load_tabs(1)
            g0_block(2)
            g0_block(3)
            flush_deferred()
            load_wgroup(3, 1)
            for ci in range(4):
                for tb in range(NB):
                    a_s, acc, tk = proj_fm(0, ci, tb)
                    t_e = S.op("act", f_act(gaT[:, ci, tb * 512:(tb + 1) * 512], acc, AF.Silu), [tk])
                    RG['acc'].release(a_s, [t_e])
                    for _ in range(2):
                        if kv_items:
                            kv_items.pop(0)()
            while kv_items:
                kv_items.pop(0)()
            load_wgroup(4, 0)

            def dump(name, view, shape, dt):
                if not debug:
                    return
                o = nc.dram_tensor("dbg_" + name, shape, dt, kind="ExternalOutput").ap()
                dbg_outs[name] = o
                tok = S.dma("sp", d_dbg, o, view, deps=S.barrier())
                S.barrier([tok])

            stop_at(3)
            S.barrier([t_cB])
            dump("hT", hT[:], [128, 8, S_LEN], BF16)
            dump("AB1", AB_t[:], [128, 40960], BF16)

            s_ring_box = [None]

            def attention(nmaps, qT_of, v_of, scale, s_banks, o_banks, sum_banks, epilogue, look):
                npar = len(o_banks)
                s_ring = Ring([bank(b) for b in s_banks])
                s_ring_box[0] = s_ring
                pt_ring = Ring(PT)
                o_free = [[[] for _ in range(nmaps)] for _ in range(npar)]
                sum_free = [[[] for _ in range(nmaps)] for _ in range(npar)]
                tiles = []
                blk = 0
                for h in range(4):
                    for j in (0, 3, 1, 2):
                        for kb in range(4 * j + 4):
                            tiles.append((h, j, kb, blk % npar))
                        blk += 1
                pend = {}
                deferred = []

                def issue_S(idx):
                    h, j, kb, par = tiles[idx]
                    il = kb - 4 * j
                    c0 = 128 * max(0, il)
                    ent = []
                    for m in range(nmaps):
                        s_s, sv, sdeps = s_ring.acquire()
                        parts = qT_of(h, m)
                        tk = None
                        for pi, (kview, qview) in enumerate(parts):
                            last = (pi == len(parts) - 1) and il < 0
                            tk = S.op("pe", f_mm(sv[:, c0:512], kview[:, kb * 128:(kb + 1) * 128],
                                                 qview[:, j * 512 + c0:(j + 1) * 512], pi == 0, last),
                                      sdeps, signal=last)
                        if il >= 0:
                            tk = S.op("pe", f_mm(sv[:, c0:c0 + 128], identb, maskb, False, True), [])
                        p_s, pv, pdeps = pt_ring.acquire()
                        t_e = S.op("act", f_act(pv[:, c0:512], sv[:, c0:512], AF.Exp, scale=scale), [tk] + pdeps)
                        s_ring.release(s_s, [t_e])
                        ent.append((p_s, pv, t_e))
                    pend[idx] = (c0, ent)

                def issue_PV(idx):
                    h, j, kb, par = tiles[idx]
                    c0, ent = pend.pop(idx)
                    first = (kb == 0)
                    last = (kb == 4 * j + 3)
                    toks = []
                    for m, (p_s, pv, t_e) in enumerate(ent):
                        ob = bank(o_banks[par][m])
                        S.op("pe", f_mm(ob[:, c0:512], v_of(h, kb), pv[:, c0:512], first, last),
                             [t_e] + (o_free[par][m] if first else []), signal=False)
                    for m, (p_s, pv, t_e) in enumerate(ent):
                        sb_ = bank(sum_banks[par][m])
                        tk = S.op("pe", f_mm(sb_[:, c0:512], onesb, pv[:, c0:512], first, last),
                                  (sum_free[par][m] if first else []))
                        pt_ring.release(p_s, [tk])
                        toks.append(tk)
                    if last:
                        fo, fs, dfn = epilogue(h, j, toks[-1], o_banks[par], sum_banks[par])
                        for m in range(nmaps):
                            o_free[par][m] = fo[m]
                            sum_free[par][m] = fs[m]
                        if dfn is not None:
                            deferred.append([4, dfn])

                n = len(tiles)
                for i in range(min(look, n)):
                    issue_S(i)
                for i in range(n):
                    issue_PV(i)
                    for d in deferred:
                        d[0] -= 1
                    while deferred and deferred[0][0] <= 0:
                        deferred.pop(0)[1]()
                    if i + look < n:
                        issue_S(i + look)
                while deferred:
                    deferred.pop(0)[1]()

            ep_i = [0]
            ep_prev = [[], []]

            def epilogue_A(h, j, t_last, obanks, sbanks):
                r = ep_i[0] % 2
                ep_i[0] += 1
                rec = ep[r * 6 + 0]
                on = ep[r * 6 + 1]
                tsl = slice(j * 512, (j + 1) * 512)
                t_r = S.op("dve", f_recip(rec, bank(sbanks[0])), [t_last] + ep_prev[r])
                t_o = S.op("dve", f_tt(on, bank(obanks[0]), rec, ALU.mult), [t_r])
                t_g = S.op("pool", f_tt(mixA_t[:, h, tsl], on, gaT[:, h, tsl], ALU.mult), [t_o])
                ep_prev[r] = [t_o, t_g]
                return [[t_o]], [[t_r]], None

            def qparts_A(h, m):
                kr = KrLo if h % 2 == 0 else KrHi
                return [(KnT[:, h, :], QnT[:, h, :]), (kr, QrT[:, h // 2, :])]

            attention(1, qparts_A, lambda h, kb: Va[:, kb, h * 128:(h + 1) * 128], MLA_SCALE,
                      s_banks=[0, 1, 2, 7], o_banks=[[5], [3]], sum_banks=[[6], [4]], epilogue=epilogue_A, look=3)

            stop_at(4)
            if debug:
                S.barrier()
                dve_tail = None
            else:
                toks_ = [S.last_token(e_) for e_ in ("pe", "act", "dve", "pool")]
                for e_ in ("act", "dve", "pool", "sp"):
                    S.wait(e_, toks_)
                S.wait("pe", [toks_[1]])
                dve_tail = toks_[2]
            dump("mixA", mixA_t[:], [128, 4, S_LEN], BF16)

            NR[0] = 6
            ROPE_DEFER[0] = 3
            del xbf[:], t1[:], t2[:], rope_free[:]
            for i_ in range(6):
                xbf.append(tv(i_ * 5120, 1024))
                t1.append(tv(i_ * 5120 + 1024, 2048, F32))
                t2.append(tv(i_ * 5120 + 3072, 2048, F32))
                rope_free.append((None, None))
            rope_i[0] = 0
            RG['acc'] = Ring([bank(2), bank(0), bank(3), bank(4)])
            RG['aux'] = Ring([bank(7), bank(1), bank(5), bank(6)])
            for rg_ in (RG['acc'], RG['aux']):
                rg_.free[2] = [dve_tail]
                rg_.free[3] = [dve_tail]
            for ci in range(4):
                for tb in range(NB):
                    a_s, acc, tk = proj_fm(1, ci, tb)
                    evac_rope(a_s, acc, tk, permB, tb, [((0, 128), dqT[:, ci, tb * 512:(tb + 1) * 512])])
            flush_deferred()
            load_wgroup(5, 1)
            for ci in range(4):
                for tb in range(NB):
                    a_s, acc, tk = proj_fm(0, ci, tb)
                    tsl = slice(tb * 512, (tb + 1) * 512)
                    evac_rope(a_s, acc, tk, permB, tb, [((0, 64), dkLo[0:64, ci, tsl]), ((64, 128), dkHi[64:128, ci, tsl])])
                    S.op("act", f_act(dkLo[64:128, ci, tsl], mask512[0][64:128, :], AF.Copy, scale=0.0), [t_c])
                    S.op("act", f_act(dkHi[0:64, ci, tsl], mask512[0][0:64, :], AF.Copy, scale=0.0), [t_c])
            flush_deferred()
            load_wgroup(6, 0)
            for ci in range(4):
                for tb in range(NB):
                    a_s, acc, tk = proj_fm(1, ci, tb)
                    t_e = S.op("act", f_act(gbT[:, ci, tb * 512:(tb + 1) * 512], acc, AF.Silu), [tk])
                    RG['acc'].release(a_s, [t_e])
            for t in range(NT):
                a_s, acc, adeps = RG['acc'].acquire()
                tk = None
                for k in range(8):
                    tk = S.op("pe", f_mm(acc, hT[:, k, t * 128:(t + 1) * 128], W_t[:, 0, k, :], k == 0, k == 7),
                              [w_ready[0]] + adeps, signal=(k == 7))
                w_free[0].append(tk)
                t_e = evac_copy(Vb[:, t, :], acc, [tk])
                RG['acc'].release(a_s, [t_e])
            wout = W_t[:].rearrange("p s k f -> p (s k f)").rearrange("p (k f) -> p k f", k=8)
            t_wout = S.dma("pool", d_w[0], wout, wo_d, deps=w_free[0] + w_free[1])

            stop_at(5)
            S.barrier()
            dump("AB2", AB_t[:], [128, 40960], BF16)

            ep_prev[0] = []
            ep_prev[1] = []

            QB = 256
            NQ = S_LEN // QB
            s_ring = Ring([bank(0), bank(1), bank(2), bank(3)])
            pt_ring = Ring(PT)
            o_bk = [4, 5]
            sum_bk = [6, 7]
            o_free = [[], []]
            sum_free = [[], []]
            ssq_free = [None]
            tilesB = []
            blk = 0
            for h in range(4):
                for jq in (0, 7, 1, 6, 2, 5, 3, 4):
                    for kb in range(2 * jq + 2):
                        tilesB.append((h, jq, kb, blk % 2))
                    blk += 1
            pendB = {}
            deferredB = []

            EB0 = 6144
            p1_prev = [[], []]
            dq_prev = [[], [], [], []]
            p2_prev = [[], []]
            blkB = [0]
            p2_i = [0]

            def epilogue_B(h, jq, par, t_last):
                n = blkB[0]
                blkB[0] += 1
                Erec = tv(EB0 + par * 4096, 2048, F32)
                Eo = tv(EB0 + par * 4096 + 2048, 2048, F32)
                k4 = n % 4
                Ed = tv(EB0 + 8192 + k4 * 1536, 1024, F32)
                Eq = tv(EB0 + 8192 + k4 * 1536 + 1024, 512)
                qsl = slice(jq * QB, (jq + 1) * QB)
                if jq in (7, 6, 4):
                    t_ln = S.op("act", f_act(Erec, bank(sum_bk[par]), AF.Ln), [t_last] + p1_prev[par])
                    t_rc = S.op("act", f_act(Erec, Erec, AF.Exp, scale=-1.0), [t_ln])
                else:
                    t_ln = S.op("dve", f_recip(Erec, bank(sum_bk[par])), [t_last] + p1_prev[par])
                    t_rc = t_ln
                t_o = S.op("dve", f_tt(Eo, bank(o_bk[par]), Erec, ALU.mult), [t_rc, t_last] + p1_prev[par])
                t_d = S.op("dve", f_stt(Ed, Eo[:, QB:2 * QB], neglam, Eo[:, 0:QB], ALU.mult, ALU.add),
                           [t_o, t_nl] + dq_prev[k4])
                p1_prev[par] = [t_d]
                t_q = S.op("pool", f_tt(Eq, Ed, Ed, ALU.mult), [t_d] + dq_prev[k4])

                def part2():
                    r2 = p2_i[0] % 2
                    p2_i[0] += 1
                    Er = tv(EB0 + 14336 + r2 * 2048, 1024, F32)
                    Eob = tv(EB0 + 14336 + r2 * 2048 + 1024, 1024, F32)
                    q_s, qbk, qdeps = s_ring.acquire()
                    sq = qbk[:, 0:QB]
                    t_s = S.op("pe", f_mm(sq, onesb, Eq), [t_q] + qdeps)
                    t_m = S.op("act", f_act(Er, sq, AF.Ln, scale=1.0 / 128, bias=epsB), [t_s, t_eps] + p2_prev[r2])
                    s_ring.release(q_s, [t_m])
                    t_p = S.op("act", f_act(Er, Er, AF.Exp, scale=-0.5), [t_m])
                    t_b = S.op("dve", f_stt(Eob, Ed, sgcol, Er, ALU.mult, ALU.mult), [t_p, t_d, t_sg] + p2_prev[r2])
                    t_g = S.op("pool", f_tt(mixB[:, h, qsl], Eob, gbT[:, h, qsl], ALU.mult), [t_b])
                    p2_prev[r2] = [t_b, t_g]
                    dq_prev[k4] = [t_b, t_s]

                return [t_o], [t_ln], part2

            def issue_SB(idx):
                h, jq, kb, par = tilesB[idx]
                il = kb - 2 * jq
                s_s, sv, sdeps = s_ring.acquire()
                qv = dqT[:, h, jq * QB:(jq + 1) * QB]
                ksl = slice(kb * 128, (kb + 1) * 128)
                live = None
                if il == 1:
                    live = lambda ap: ap.rearrange("p (m c) -> p m c", m=2)[:, :, 128:256]
                    qv1 = dqT[:, h, jq * QB + 128:(jq + 1) * QB]
                    S.op("pe", f_mm(sv[:, 128:2 * QB], identb, tri3, True, False), sdeps, signal=False)
                    S.op("pe", f_mm(sv[:, 128:QB], dkLo[:, h, ksl], qv1, False, False), [], signal=False)
                    tk = S.op("pe", f_mm(sv[:, QB + 128:2 * QB], dkHi[:, h, ksl], qv1, False, True), [])
                elif il == 0:
                    S.op("pe", f_mm(sv, identb, mask512[il], True, False), sdeps, signal=False)
                    S.op("pe", f_mm(sv[:, 0:QB], dkLo[:, h, ksl], qv, False, False), [], signal=False)
                    tk = S.op("pe", f_mm(sv[:, QB:2 * QB], dkHi[:, h, ksl], qv, False, True), [])
                else:
                    S.op("pe", f_mm(sv[:, 0:QB], dkLo[:, h, ksl], qv, True, True), sdeps, signal=False)
                    tk = S.op("pe", f_mm(sv[:, QB:2 * QB], dkHi[:, h, ksl], qv, True, True), [])
                p_s, pv, pdeps = pt_ring.acquire()
                if live is not None:
                    t_e = S.op("act", f_act(live(pv), live(sv), AF.Exp, scale=DIFF_SCALE), [tk] + pdeps)
                else:
                    t_e = S.op("act", f_act(pv, sv, AF.Exp, scale=DIFF_SCALE), [tk] + pdeps)
                s_ring.release(s_s, [t_e])
                pendB[idx] = (p_s, pv, t_e, live)

            def issue_PVB(idx):
                h, jq, kb, par = tilesB[idx]
                p_s, pv, t_e, live = pendB.pop(idx)
                first = (kb == 0)
                last = (kb == 2 * jq + 1)
                ob_, sb_ = bank(o_bk[par]), bank(sum_bk[par])
                vh = Vb[:, kb, h * 128:(h + 1) * 128]
                if live is not None:
                    assert last and not first
                    l0, l1 = slice(128, QB), slice(QB + 128, 2 * QB)
                    S.op("pe", f_mm(ob_[:, l0], vh, pv[:, l0], False, False), [t_e], signal=False)
                    S.op("pe", f_mm(ob_[:, l1], vh, pv[:, l1], False, True), [], signal=False)
                    S.op("pe", f_mm(sb_[:, l0], onesb, pv[:, l0], False, False), [], signal=False)
                    tk = S.op("pe", f_mm(sb_[:, l1], onesb, pv[:, l1], False, True), [])
                else:
                    S.op("pe", f_mm(ob_, vh, pv, first, last),
                         [t_e] + (o_free[par] if first else []), signal=False)
                    tk = S.op("pe", f_mm(sb_, onesb, pv, first, last), (sum_free[par] if first else []))
                pt_ring.release(p_s, [tk])
                if last:
                    fo, fs, dfn = epilogue_B(h, jq, par, tk)
                    o_free[par] = fo
                    sum_free[par] = fs
                    deferredB.append([12, dfn])

            nB = len(tilesB)
            LOOK = 3
            for i in range(min(LOOK, nB)):
                issue_SB(i)
            for i in range(nB):
                issue_PVB(i)
                for d_ in deferredB:
                    d_[0] -= 1
                while deferredB and deferredB[0][0] <= 0:
                    deferredB.pop(0)[1]()
                if i + LOOK < nB:
                    issue_SB(i + LOOK)
            while deferredB:
                deferredB.pop(0)[1]()

            stop_at(6)
            S.barrier()
            dump("mixB", mixB, [128, 4, S_LEN], BF16)

            t_gpost = S.dma("sp", d_c2, gpost, gpost_d.partition_broadcast(128))
            pair_ring = Ring([psp[0], psp[1], psp[2], psp[3]])
            xs_free5 = [None, None, None]
            y_free = [None, None, None]
            out_toks = []
            gv = gpost.rearrange("p (a f) -> p a f", a=2)
            st5 = {}

            def p5_A(t):
                s = t % 3
                tx = S.dma("sp", d_xs5[s], xs5[s], x_d[t * 128:(t + 1) * 128, :], deps=[xs_free5[s]])
                p_s, pp, pdeps = pair_ring.acquire()
                tk = None
                for half in range(2):
                    for c in range(8):
                        src_ = mixA_t[:, c, t * 128:(t + 1) * 128] if c < 4 else mixB[:, c - 4, t * 128:(t + 1) * 128]
                        tk = S.op("pe", f_mm(pp[:, half, :], src_, wout[:, c, half * 512:(half + 1) * 512], c == 0, c == 7),
                                  [t_wout] + pdeps, signal=(half == 1 and c == 7))
                t_q0 = S.op("act", f_act(junk5[:, 0:512], pp[:, 0, :], AF.Square, accum_out=ssq5[:, 2 * t:2 * t + 1]), [tk])
                t_q1 = S.op("act", f_act(junk5[:, 512:1024], pp[:, 1, :], AF.Square,
                                         accum_out=ssq5[:, 2 * t + 1:2 * t + 2]), [tk, t_q0])
                t_m = S.op("dve", f_stt(ms5[:, t:t + 1], ssq5[:, 2 * t:2 * t + 1], 1.0, ssq5[:, 2 * t + 1:2 * t + 2],
                                        ALU.mult, ALU.add), [t_q0, t_q1])
                t_m2 = S.op("dve", f_ts(rstd5[:, t:t + 1], ms5[:, t:t + 1], 1.0 / D, NORM_EPS, ALU.mult, ALU.add), [t_m])
                t_r = S.op("pool", f_tt(ms5[:, t:t + 1], rstd5[:, t:t + 1], mhcol, ALU.pow), [t_m2, t_mh2])
                st5[t] = (tx, p_s, pp, t_r)

            def p5_B(t):
                s = t % 3
                tx, p_s, pp, t_r = st5.pop(t)
                yv = ybuf[s].rearrange("p (a f) -> p a f", a=2)
                t_y = None
                for half in range(2):
                    t_y = S.op("act", f_act(yv[:, half, :], pp[:, half, :], AF.Identity, scale=ms5[:, t:t + 1]),
                               [t_r, y_free[s]])
                pair_ring.release(p_s, [t_y])
                t_w = S.op("dve", f_tt(ybuf[s], ybuf[s], gpost, ALU.mult), [t_y, t_gpost])
                t_a = S.op("dve", f_tt(ybuf[s], ybuf[s], xs5[s], ALU.add), [t_w, tx])
                xs_free5[s] = t_a
                t_o = S.dma("sp", d_out[s], out_d[t * 128:(t + 1) * 128, :], ybuf[s], deps=[t_a])
                y_free[s] = t_o
                out_toks.append(t_o)

            p5_A(0)
            for t in range(NT):
                if t + 1 < NT:
                    p5_A(t + 1)
                p5_B(t)
            S.barrier(out_toks[-3:])
            S.emit()
        except _Stop:
            pass
    if debug:
        return nc, dbg_outs
    return nc


_CACHE = {}


def _prep_inputs(inputs):
    x = np.asarray(inputs["x"], dtype=np.float32)
    wl, wo, wkv = _layout_weights(np.asarray(inputs["w_in"]), np.asarray(inputs["w_uk"]),
                                  np.asarray(inputs["w_uv"]), np.asarray(inputs["w_out"]))
    cmat, tabs = _host_consts()
    gpre = np.ascontiguousarray(np.asarray(inputs["ln_pre_g"], dtype=np.float32).reshape(1, D))
    gpost = np.ascontiguousarray(np.asarray(inputs["ln_post_g"], dtype=np.float32).reshape(1, D))
    cols = np.ascontiguousarray(np.stack([np.asarray(inputs["kv_norm_g"], dtype=np.float32).reshape(128),
                                          np.asarray(inputs["subln_g"], dtype=np.float32).reshape(128)], axis=1))
    lamv = np.ascontiguousarray(np.concatenate([
        np.asarray(inputs["lambda_q1"], dtype=np.float32).reshape(64),
        np.asarray(inputs["lambda_k1"], dtype=np.float32).reshape(64),
        np.asarray(inputs["lambda_q2"], dtype=np.float32).reshape(64),
        np.asarray(inputs["lambda_k2"], dtype=np.float32).reshape(64)]).reshape(1, 256))
    shared = {"wl": wl, "wo": wo, "wkv": wkv, "cmat": cmat, "tabs": tabs, "gpre": gpre, "gpost": gpost,
              "cols": cols, "lamv": lamv}
    return x, shared


def kernel(**inputs):
    x, shared = _prep_inputs(inputs)
    B = x.shape[0]
    if "nc" not in _CACHE:
        _CACHE["nc"] = build_program()
    nc = _CACHE["nc"]
    in_maps = [dict(shared, x=np.ascontiguousarray(x[b])) for b in range(B)]
    res = run_bass_kernel_spmd(nc, in_maps, core_ids=list(range(B)))
    out = np.stack([np.asarray(r["out"], dtype=np.float32) for r in res.results], axis=0)
    return out
```
